# Optimizing a Trainium2 kernel written in Bass

```python
import jax, jax.numpy as jnp
from jax import lax
import numpy as np

D_MODEL = 1024
BATCH = 8
SEQ = 2048
DEPTH = 4
DEC_BATCH = 128
DEC_SEQ = 8
PAST_LEN = 16384
PAGE_SIZE = 128

CHUNK = 128
A_GROUPS = 4
A_WIDTH = 512
A_GC = A_WIDTH // A_GROUPS
B_WIDTH = 512
CONV_W = 31
C_WIDTH = 512
POOL_WINDOWS = (2, 4, 8, 16)
POOL_GROUPS = len(POOL_WINDOWS)
C_GC = C_WIDTH // POOL_GROUPS
POOL_BUF = max(POOL_WINDOWS) - 1
SSM_HEADS = 12
SSM_HEADDIM = 64
D_WIDTH = SSM_HEADS * SSM_HEADDIM
SSM_GROUPS = 4
SSM_STATE = 128
SSM_CONV = 4
SSD_CHUNK = 128
XBC_WIDTH = D_WIDTH + 2 * SSM_GROUPS * SSM_STATE
N_BRANCH = 4
IN_SPLITS = (A_WIDTH, A_WIDTH, B_WIDTH, B_WIDTH, C_WIDTH, D_WIDTH, XBC_WIDTH, SSM_HEADS, N_BRANCH * D_MODEL)
IN_WIDTH = sum(IN_SPLITS)
IN_SPLIT_IDX = [int(i) for i in np.cumsum(IN_SPLITS)[:-1]]
E_GROUPS = 4
E_PER_GROUP = 8
N_EXPERTS = E_GROUPS * E_PER_GROUP
TOP_K = 2
EXPERT_FF = 512
MOE_BLOCK = 128
DN_ALPHA = (2 * DEPTH) ** 0.25
DN_BETA = (8 * DEPTH) ** -0.25
LN_EPS = 1e-5

kernel_name = 'hybrid_gated_branch_decoder_step'


def _layer_norm(x, g, b):
    xf = x.astype(jnp.float32)
    mu = jnp.mean(xf, axis=-1, keepdims=True)
    var = jnp.mean(jnp.square(xf - mu), axis=-1, keepdims=True)
    y = (xf - mu) * lax.rsqrt(var + LN_EPS) * g.astype(jnp.float32) + b.astype(jnp.float32)
    return y.astype(x.dtype)


def _dw_conv(ext, w, bias):
    ch = ext.shape[-1]
    y = lax.conv_general_dilated(ext, w[:, None, :].astype(ext.dtype), window_strides=(1,), padding='VALID',
                                 dimension_numbers=('NWC', 'WIO', 'NWC'), feature_group_count=ch)
    return y + bias.astype(ext.dtype)


def _pad_seq(t, pad):
    return jnp.pad(t, [(0, 0), (0, pad)] + [(0, 0)] * (t.ndim - 2))


def _sgu_branch(u, v, ln_g, ln_b, w_s, b_s):
    bsz, L, _ = v.shape
    vn = _layer_norm(v, ln_g, ln_b)
    pad = (-L) % CHUNK
    vp = _pad_seq(vn, pad) if pad else vn
    nc = (L + pad) // CHUNK
    vc = vp.reshape(bsz, nc, CHUNK, A_GROUPS, A_GC)
    w = jnp.where(jnp.tril(jnp.ones((CHUNK, CHUNK), bool)), w_s, 0)
    s = jnp.einsum('gts,bnsgc->bntgc', w, vc) + b_s.T[:, :, None]
    s = s.reshape(bsz, nc * CHUNK, A_WIDTH)[:, :L]
    return u * s.astype(u.dtype), vn


def _conv_branch(a, g, conv_st, w, bias, ln_g, ln_b):
    h = a * jax.nn.sigmoid(g)
    ext = jnp.concatenate([conv_st.astype(h.dtype), h], axis=1)
    y = jax.nn.silu(_layer_norm(_dw_conv(ext, w, bias), ln_g, ln_b))
    return y, ext[:, -(CONV_W - 1):]


def _pool_branch(p, pool_st, start_pos, w_pool, scale):
    bsz, L, _ = p.shape
    ext = jnp.concatenate([pool_st.astype(p.dtype), p], axis=1)
    cs = jnp.pad(jnp.cumsum(ext.astype(jnp.float32), axis=1), ((0, 0), (1, 0), (0, 0)))
    pos = (start_pos + jnp.arange(L)).astype(jnp.float32)
    means = []
    for gi, win in enumerate(POOL_WINDOWS):
        sl = slice(gi * C_GC, (gi + 1) * C_GC)
        hi = cs[:, POOL_BUF + 1:POOL_BUF + 1 + L, sl]
        lo = cs[:, POOL_BUF + 1 - win:POOL_BUF + 1 - win + L, sl]
        cnt = jnp.minimum(float(win), pos + 1.0)
        means.append((hi - lo) / cnt[None, :, None])
    mixed = (jnp.concatenate(means, axis=-1) - p.astype(jnp.float32)).reshape(bsz, L, POOL_GROUPS, C_GC)
    out = jnp.einsum('blgc,gcd->blgd', mixed, w_pool.astype(jnp.float32)).reshape(bsz, L, C_WIDTH)
    out = out * scale.astype(jnp.float32)
    return out.astype(p.dtype), ext[:, -POOL_BUF:]


def _ssd_scan(x, dt, a, bm, cm, s0):
    bsz, L = x.shape[:2]
    pad = (-L) % SSD_CHUNK
    if pad:
        x, dt, bm, cm = _pad_seq(x, pad), _pad_seq(dt, pad), _pad_seq(bm, pad), _pad_seq(cm, pad)
    q = SSD_CHUNK
    nc = (L + pad) // q
    rep = SSM_HEADS // SSM_GROUPS
    bh = jnp.repeat(bm, rep, axis=2).reshape(bsz, nc, q, SSM_HEADS, SSM_STATE)
    ch = jnp.repeat(cm, rep, axis=2).reshape(bsz, nc, q, SSM_HEADS, SSM_STATE)
    xc = x.reshape(bsz, nc, q, SSM_HEADS, SSM_HEADDIM)
    dtc = dt.reshape(bsz, nc, q, SSM_HEADS)
    xd = xc * dtc[..., None]
    acs = jnp.cumsum(jnp.moveaxis(dtc * a, 3, 1), axis=-1)
    causal = jnp.tril(jnp.ones((q, q), bool))
    decay = jnp.exp(jnp.where(causal, acs[..., :, None] - acs[..., None, :], -jnp.inf))
    y_diag = jnp.einsum('bclhn,bcshn,bhcls,bcshp->bclhp', ch, bh, decay, xd)
    dstate = jnp.exp(acs[..., -1:] - acs)
    chunk_states = jnp.einsum('bclhn,bhcl,bclhp->bchpn', bh, dstate, xd)
    chunk_decay = jnp.exp(acs[..., -1])

    def step(s, inp):
        dec, st = inp
        return s * dec[:, :, None, None] + st, s

    s_fin, s_in = lax.scan(step, s0, (jnp.moveaxis(chunk_decay, 2, 0), jnp.moveaxis(chunk_states, 1, 0)))
    s_in = jnp.moveaxis(s_in, 0, 1)
    y_off = jnp.einsum('bclhn,bchpn,bhcl->bclhp', ch, s_in, jnp.exp(acs))
    y = (y_diag + y_off).reshape(bsz, nc * q, SSM_HEADS, SSM_HEADDIM)[:, :L]
    return y, s_fin


def _mamba_branch(z, xbc, dt_raw, conv_st, ssm_st, conv_w, conv_b, dt_bias, a_log, d_skip, norm_g):
    bsz, L, _ = z.shape
    f32 = jnp.float32
    ext = jnp.concatenate([conv_st.astype(xbc.dtype), xbc], axis=1)
    xbc_c = jax.nn.silu(_dw_conv(ext, conv_w, conv_b))
    xs, bm, cm = jnp.split(xbc_c, [D_WIDTH, D_WIDTH + SSM_GROUPS * SSM_STATE], axis=-1)
    xs = xs.reshape(bsz, L, SSM_HEADS, SSM_HEADDIM).astype(f32)
    bm = bm.reshape(bsz, L, SSM_GROUPS, SSM_STATE).astype(f32)
    cm = cm.reshape(bsz, L, SSM_GROUPS, SSM_STATE).astype(f32)
    dt = jax.nn.softplus(dt_raw.astype(f32) + dt_bias.astype(f32))
    a = -jnp.exp(a_log.astype(f32))
    y, s_new = _ssd_scan(xs, dt, a, bm, cm, ssm_st.astype(f32))
    y = y + d_skip.astype(f32)[:, None] * xs
    y = y.reshape(bsz, L, D_WIDTH) * jax.nn.silu(z.astype(f32))
    y = y * lax.rsqrt(jnp.mean(y * y, axis=-1, keepdims=True) + LN_EPS) * norm_g.astype(f32)
    return y.astype(z.dtype), ext[:, -(SSM_CONV - 1):], s_new.astype(ssm_st.dtype)


def _hier_moe(h, router_g, router_e, w_gate, w_up, w_down):
    bsz, L, dm = h.shape
    t = h.reshape(bsz * L, dm)
    n_tok = t.shape[0]
    tf = t.astype(jnp.float32)
    logit_g = tf @ router_g.astype(jnp.float32)
    grp = jnp.argmax(logit_g, axis=-1)
    p_grp = jnp.take_along_axis(jax.nn.softmax(logit_g, axis=-1), grp[:, None], axis=1)
    logit_e = (tf @ router_e.astype(jnp.float32)).reshape(n_tok, E_GROUPS, E_PER_GROUP)
    logit_e = jnp.take_along_axis(logit_e, grp[:, None, None], axis=1)[:, 0]
    top_v, top_i = lax.top_k(logit_e, TOP_K)
    w_slot = (jax.nn.softmax(top_v, axis=-1) * p_grp).reshape(-1)
    e_slot = (grp[:, None] * E_PER_GROUP + top_i).reshape(-1)
    n_slot = n_tok * TOP_K
    order = jnp.argsort(e_slot)
    e_sorted = e_slot[order]
    tok_sorted = order // TOP_K
    counts = jnp.bincount(e_slot, length=N_EXPERTS)
    padded = (counts + MOE_BLOCK - 1) // MOE_BLOCK * MOE_BLOCK
    end_pad = jnp.cumsum(padded)
    start_pad = end_pad - padded
    start_raw = jnp.cumsum(counts) - counts
    dest = start_pad[e_sorted] + jnp.arange(n_slot) - start_raw[e_sorted]
    n_blocks = -(-(n_slot + N_EXPERTS * (MOE_BLOCK - 1)) // MOE_BLOCK)
    buf = jnp.zeros((n_blocks * MOE_BLOCK, dm), h.dtype).at[dest].set(t[tok_sorted])
    block_e = jnp.minimum(jnp.searchsorted(end_pad, jnp.arange(n_blocks) * MOE_BLOCK, side='right'), N_EXPERTS - 1)

    def expert_block(args):
        xb, e = args
        hid = jax.nn.silu(xb @ w_gate[e]) * (xb @ w_up[e])
        return hid @ w_down[e]

    ybuf = lax.map(expert_block, (buf.reshape(n_blocks, MOE_BLOCK, dm), block_e)).reshape(-1, dm)
    y_slot = ybuf[dest].astype(jnp.float32) * w_slot[order][:, None]
    out = jnp.zeros((n_tok, dm), jnp.float32).at[tok_sorted].add(y_slot)
    return out.astype(h.dtype).reshape(bsz, L, dm)


def _trunk(x, c, conv_st, pool_st, sconv_st, ssm_st, start_pos, prm):
    new_conv, new_pool, new_sconv, new_ssm, new_v = [], [], [], [], []
    for l in range(DEPTH):
        ada = jax.nn.silu(c) @ prm['w_ada'][l] + prm['b_ada'][l]
        sh1, sc1, g1, sh2, sc2, g2 = jnp.split(ada, 6, axis=-1)
        xm = x * (1.0 + sc1[:, None, :]) + sh1[:, None, :]
        proj = xm @ prm['w_in'][l]
        u, v, ga, gg, pin, zz, xbc, dtr, gl = jnp.split(proj, IN_SPLIT_IDX, axis=-1)
        ya, vn = _sgu_branch(jax.nn.gelu(u), jax.nn.gelu(v), prm['sgu_ln_g'][l], prm['sgu_ln_b'][l],
                             prm['sgu_w'][l], prm['sgu_b'][l])
        yb, cst = _conv_branch(ga, gg, conv_st[l], prm['conv_w'][l], prm['conv_bias'][l],
                               prm['conv_ln_g'][l], prm['conv_ln_b'][l])
        yc, pst = _pool_branch(pin, pool_st[l], start_pos, prm['pool_w'][l], prm['pool_scale'][l])
        yd, scst, sst = _mamba_branch(zz, xbc, dtr, sconv_st[l], ssm_st[l], prm['ssm_conv_w'][l],
                                      prm['ssm_conv_b'][l], prm['ssm_dt_bias'][l], prm['ssm_a_log'][l],
                                      prm['ssm_d'][l], prm['ssm_norm_g'][l])
        gates = jax.nn.sigmoid(gl).reshape(gl.shape[:-1] + (N_BRANCH, D_MODEL))
        merged = (gates[..., 0, :] * (ya @ prm['w_br_a'][l]) + gates[..., 1, :] * (yb @ prm['w_br_b'][l])
                  + gates[..., 2, :] * (yc @ prm['w_br_c'][l]) + gates[..., 3, :] * (yd @ prm['w_br_d'][l]))
        mix = merged @ prm['w_o'][l]
        x = _layer_norm(DN_ALPHA * x + g1[:, None, :] * mix, prm['ln1_g'][l], prm['ln1_b'][l])
        xm2 = x * (1.0 + sc2[:, None, :]) + sh2[:, None, :]
        f = _hier_moe(xm2, prm['router_g'][l], prm['router_e'][l], prm['w_e_gate'][l], prm['w_e_up'][l],
                      prm['w_e_down'][l])
        x = _layer_norm(DN_ALPHA * x + g2[:, None, :] * f, prm['ln2_g'][l], prm['ln2_b'][l])
        new_conv.append(cst)
        new_pool.append(pst)
        new_sconv.append(scst)
        new_ssm.append(sst)
        new_v.append(vn)
    return (x, jnp.stack(new_conv), jnp.stack(new_pool), jnp.stack(new_sconv), jnp.stack(new_ssm),
            jnp.stack(new_v))


def setup_inputs(seed: int = 0) -> dict:
    key = jax.random.key(seed)
    ks = iter(jax.random.split(key, 64))

    def nrm(shape, scale=1.0):
        return jax.random.normal(next(ks), shape, jnp.float32) * scale

    def gain(shape):
        return 1.0 + nrm(shape, 0.02)

    d = D_MODEL
    dt = jnp.exp(jax.random.uniform(next(ks), (DEPTH, SSM_HEADS), jnp.float32, np.log(1e-3), np.log(1e-1)))
    inp = {}
    inp['x_prompt'] = nrm((BATCH, SEQ, d))
    inp['x_sample'] = nrm((DEC_BATCH, DEC_SEQ, d))
    inp['state_conv'] = nrm((DEPTH, DEC_BATCH, CONV_W - 1, B_WIDTH), 0.5)
    inp['state_pool'] = nrm((DEPTH, DEC_BATCH, POOL_BUF, C_WIDTH))
    inp['state_ssm_conv'] = nrm((DEPTH, DEC_BATCH, SSM_CONV - 1, XBC_WIDTH))
    inp['state_ssm'] = nrm((DEPTH, DEC_BATCH, SSM_HEADS, SSM_HEADDIM, SSM_STATE), 0.1)
    inp['c_prompt'] = nrm((BATCH, d))
    inp['c_sample'] = nrm((DEC_BATCH, d))
    inp['w_ada'] = nrm((DEPTH, d, 6 * d), 0.2 * d ** -0.5)
    inp['b_ada'] = nrm((DEPTH, 6 * d), 0.01)
    inp['w_in'] = nrm((DEPTH, d, IN_WIDTH), d ** -0.5)
    inp['sgu_ln_g'] = gain((DEPTH, A_WIDTH))
    inp['sgu_ln_b'] = nrm((DEPTH, A_WIDTH), 0.02)
    inp['sgu_w'] = nrm((DEPTH, A_GROUPS, CHUNK, CHUNK), CHUNK ** -0.5)
    inp['sgu_b'] = gain((DEPTH, A_GROUPS, CHUNK))
    inp['conv_w'] = nrm((DEPTH, CONV_W, B_WIDTH), CONV_W ** -0.5)
    inp['conv_bias'] = nrm((DEPTH, B_WIDTH), 0.02)
    inp['conv_ln_g'] = gain((DEPTH, B_WIDTH))
    inp['conv_ln_b'] = nrm((DEPTH, B_WIDTH), 0.02)
    inp['pool_w'] = nrm((DEPTH, POOL_GROUPS, C_GC, C_GC), C_GC ** -0.5)
    inp['pool_scale'] = gain((DEPTH, C_WIDTH))
    inp['ssm_conv_w'] = nrm((DEPTH, SSM_CONV, XBC_WIDTH), SSM_CONV ** -0.5)
    inp['ssm_conv_b'] = nrm((DEPTH, XBC_WIDTH), 0.02)
    inp['ssm_dt_bias'] = dt + jnp.log(-jnp.expm1(-dt))
    inp['ssm_a_log'] = jnp.log(jax.random.uniform(next(ks), (DEPTH, SSM_HEADS), jnp.float32, 1.0, 16.0))
    inp['ssm_d'] = gain((DEPTH, SSM_HEADS))
    inp['ssm_norm_g'] = gain((DEPTH, D_WIDTH))
    inp['w_br_a'] = nrm((DEPTH, A_WIDTH, d), A_WIDTH ** -0.5 * DN_BETA)
    inp['w_br_b'] = nrm((DEPTH, B_WIDTH, d), B_WIDTH ** -0.5 * DN_BETA)
    inp['w_br_c'] = nrm((DEPTH, C_WIDTH, d), C_WIDTH ** -0.5 * DN_BETA)
    inp['w_br_d'] = nrm((DEPTH, D_WIDTH, d), D_WIDTH ** -0.5 * DN_BETA)
    inp['w_o'] = nrm((DEPTH, d, d), d ** -0.5 * DN_BETA)
    inp['ln1_g'] = gain((DEPTH, d))
    inp['ln1_b'] = nrm((DEPTH, d), 0.02)
    inp['router_g'] = nrm((DEPTH, d, E_GROUPS), d ** -0.5)
    inp['router_e'] = nrm((DEPTH, d, N_EXPERTS), d ** -0.5)
    inp['w_e_gate'] = nrm((DEPTH, N_EXPERTS, d, EXPERT_FF), d ** -0.5)
    inp['w_e_up'] = nrm((DEPTH, N_EXPERTS, d, EXPERT_FF), d ** -0.5 * DN_BETA)
    inp['w_e_down'] = nrm((DEPTH, N_EXPERTS, EXPERT_FF, d), EXPERT_FF ** -0.5 * DN_BETA)
    inp['ln2_g'] = gain((DEPTH, d))
    inp['ln2_b'] = nrm((DEPTH, d), 0.02)
    return inp


def reference(x_prompt, x_sample, state_conv, state_pool, state_ssm_conv, state_ssm, c_prompt, c_sample,
              w_ada, b_ada, w_in, sgu_ln_g, sgu_ln_b, sgu_w, sgu_b, conv_w, conv_bias, conv_ln_g, conv_ln_b,
              pool_w, pool_scale, ssm_conv_w, ssm_conv_b, ssm_dt_bias, ssm_a_log, ssm_d, ssm_norm_g,
              w_br_a, w_br_b, w_br_c, w_br_d, w_o, ln1_g, ln1_b, router_g, router_e, w_e_gate, w_e_up,
              w_e_down, ln2_g, ln2_b):
    prm = dict(w_ada=w_ada, b_ada=b_ada, w_in=w_in, sgu_ln_g=sgu_ln_g, sgu_ln_b=sgu_ln_b, sgu_w=sgu_w,
               sgu_b=sgu_b, conv_w=conv_w, conv_bias=conv_bias, conv_ln_g=conv_ln_g, conv_ln_b=conv_ln_b,
               pool_w=pool_w, pool_scale=pool_scale, ssm_conv_w=ssm_conv_w, ssm_conv_b=ssm_conv_b,
               ssm_dt_bias=ssm_dt_bias, ssm_a_log=ssm_a_log, ssm_d=ssm_d, ssm_norm_g=ssm_norm_g,
               w_br_a=w_br_a, w_br_b=w_br_b, w_br_c=w_br_c, w_br_d=w_br_d, w_o=w_o, ln1_g=ln1_g,
               ln1_b=ln1_b, router_g=router_g, router_e=router_e, w_e_gate=w_e_gate, w_e_up=w_e_up,
               w_e_down=w_e_down, ln2_g=ln2_g, ln2_b=ln2_b)
    bp = x_prompt.shape[0]
    dtp = x_prompt.dtype
    z_conv = jnp.zeros((DEPTH, bp, CONV_W - 1, B_WIDTH), dtp)
    z_pool = jnp.zeros((DEPTH, bp, POOL_BUF, C_WIDTH), dtp)
    z_sconv = jnp.zeros((DEPTH, bp, SSM_CONV - 1, XBC_WIDTH), dtp)
    z_ssm = jnp.zeros((DEPTH, bp, SSM_HEADS, SSM_HEADDIM, SSM_STATE), jnp.float32)
    y_prompt, p_conv, p_pool, p_sconv, p_ssm, _ = _trunk(x_prompt, c_prompt, z_conv, z_pool, z_sconv, z_ssm, 0, prm)
    y_sample, s_conv, s_pool, s_sconv, s_ssm, s_v = _trunk(x_sample, c_sample, state_conv, state_pool,
                                                           state_ssm_conv, state_ssm, PAST_LEN, prm)
    return (y_prompt, y_sample, p_conv, p_pool, p_sconv, p_ssm, s_conv, s_pool, s_sconv, s_ssm, s_v)
```

```python
import numpy as np
import concourse.bass as bass
import concourse.mybir as mybir
from concourse.bass_utils import run_bass_kernel_spmd

F32 = mybir.dt.float32
BF16 = mybir.dt.bfloat16
AF = mybir.ActivationFunctionType
ALU = mybir.AluOpType
AX = mybir.AxisListType

ENGINES = ("pe", "act", "dve", "pool", "sp")
NDMA = 8
NWS = 5
DN_ALPHA = 8.0 ** 0.25
LN_EPS = 1e-5
NEG = -30000.0

C_U, C_V, C_GA, C_GG, C_PIN, C_ZZ, C_XBC, C_DT, C_GL = 0, 512, 1024, 1536, 2048, 2560, 3328, 5120, 5132
FV_CONVB, FV_CLNG, FV_CLNB, FV_PSC, FV_SCB, FV_L1G, FV_L1B, FV_L2G, FV_L2B, FV_CW, FV_SCW = 0, 4, 8, 12, 16, 30, 38, 46, 54, 62, 186
FV_NG = 242
NFV = 248
RV_SLG, RV_SLB, RV_DTB, RV_ALOG, RV_D = 0, 512, 1024, 1036, 1048
NRV = 1060


class Sched:
    def __init__(self, nc):
        self.nc = nc
        self.ops = {e: [] for e in ENGINES}
        self.count = {e: 0 for e in ENGINES}
        self.dmaj = {e: 0 for e in ENGINES}
        self.last_writer = {}
        self.readers = {}
        self.known = {e: {} for e in ENGINES}
        self.out_tokens = []

    def add(self, engine, fn, reads=(), writes=(), dma=False, out=False):
        deps = []
        for r in reads:
            t = self.last_writer.get(r)
            if t is not None:
                deps.append(t)
        for w in writes:
            t = self.last_writer.get(w)
            if t is not None:
                deps.append(t)
            deps.extend(self.readers.get(w, ()))
        if dma:
            j = self.dmaj[engine]
            s = j % NDMA
            key = ("dma", engine, s)
            tok = (key, 16 * (j // NDMA + 1))
            if j >= NDMA:
                deps.append((key, 16 * (j // NDMA)))
            self.dmaj[engine] = j + 1
        else:
            self.count[engine] += 1
            tok = (("eng", engine), self.count[engine])
        best = {}
        for k, v in deps:
            if best.get(k, 0) < v:
                best[k] = v
        waits = []
        kn = self.known[engine]
        for k, v in best.items():
            if kn.get(k, 0) < v:
                waits.append((k, v))
                kn[k] = v
        self.ops[engine].append((waits, fn, tok, dma))
        for r in reads:
            self.readers.setdefault(r, []).append(tok)
        for w in writes:
            self.last_writer[w] = tok
            self.readers[w] = []
        if out:
            self.out_tokens.append(tok)
        return tok

    def emit(self):
        nc = self.nc
        best = {}
        for k, v in self.out_tokens:
            if best.get(k, 0) < v:
                best[k] = v
        for e in ENGINES:
            if self.count[e]:
                best[("eng", e)] = self.count[e]
            for j in range(max(0, self.dmaj[e] - NDMA), self.dmaj[e]):
                k = ("dma", e, j % NDMA)
                best[k] = max(best.get(k, 0), 16 * (j // NDMA + 1))
        final_waits = list(best.items())
        keys = set()
        for e in ENGINES:
            for waits, fn, tok, dma in self.ops[e]:
                keys.add(tok[0])
        sems = {}
        for k in sorted(keys, key=str):
            sems[k] = nc.alloc_semaphore(name="s_" + "_".join(str(x) for x in k))
        attr = {"pe": "tensor", "act": "scalar", "dve": "vector", "pool": "gpsimd", "sp": "sync"}
        ops = self.ops
        with nc.Block() as block:
            def mk(e):
                def body(eng):
                    for waits, fn, tok, dma in ops[e]:
                        for k, v in waits:
                            eng.wait_ge(sems[k], v)
                        ins = fn(eng)
                        ins.then_inc(sems[tok[0]], 16 if dma else 1)
                    if e == "sp":
                        for k, v in final_waits:
                            eng.wait_ge(sems[k], v)
                return body
            for e in ENGINES:
                if ops[e] or e == "sp":
                    getattr(block, attr[e])(mk(e))


class Builder:
    def __init__(self, depth, npt, stop_after=None):
        self.depth = depth
        self.npt = npt
        self.T = npt * 128 + 128
        self.ntile = npt + 1
        self.stop_after = stop_after
        nc = bass.Bass("TRN2", target_bir_lowering=False)
        self.nc = nc
        self.S = Sched(nc)
        self._wsi = 0
        self._gch = 0
        self.NBLK = -(-(2 * self.T + 32 * 127) // 128)
        self.NJ = self.ntile
        self.declare()
        self.program()
        self.S.emit()

    def TT(self, out, in0, in1, op, r, w, eng="dve"):
        self.S.add(eng, lambda e: e.tensor_tensor(out=out, in0=in0, in1=in1, op=op), r, w)

    def TS(self, out, in0, s1, s2, op0, op1, r, w, eng="dve"):
        if s2 is None:
            self.S.add(eng, lambda e: e.tensor_scalar(out=out, in0=in0, scalar1=s1, scalar2=None, op0=op0), r, w)
        else:
            self.S.add(eng, lambda e: e.tensor_scalar(out=out, in0=in0, scalar1=s1, scalar2=s2, op0=op0, op1=op1), r, w)

    def STT(self, out, in0, scalar, in1, op0, op1, r, w, eng="dve"):
        self.S.add(eng, lambda e: e.scalar_tensor_tensor(out=out, in0=in0, scalar=scalar, in1=in1, op0=op0, op1=op1), r, w)

    def CP(self, out, in_, r, w, eng="dve"):
        if eng == "act":
            self.S.add("act", lambda e: e.copy(out=out, in_=in_), r, w)
        else:
            self.S.add(eng, lambda e: e.tensor_copy(out=out, in_=in_), r, w)

    def ACT(self, out, in_, func, r, w, scale=None, bias=None):
        kw = {}
        if scale is not None:
            kw["scale"] = scale
        if bias is not None:
            kw["bias"] = bias
        self.S.add("act", lambda e: e.activation(out=out, in_=in_, func=func, **kw), r, w)

    def SQRT(self, ap, res):
        self.S.add("act", lambda e: e.sqrt(out=ap, in_=ap), [res], [res])

    def MM(self, mms, r, w):
        def fn(e):
            ins = None
            for (o, l, rh, st, sp) in mms:
                ins = e.matmul(o, lhsT=l, rhs=rh, start=st, stop=sp)
            return ins
        self.S.add("pe", fn, r, w)

    def TR(self, trs, r, w):
        def fn(e):
            ins = None
            for (o, i, idn) in trs:
                ins = e.transpose(o, i, idn)
            return ins
        self.S.add("pe", fn, r, w)

    def DMA(self, out, in_, r, w, q="sp", final=False):
        self.S.add(q, lambda e: e.dma_start(out=out, in_=in_), r, w, dma=True, out=final)

    def DMA4(self, out, in_, r, w, final=False):
        for c in range(out.shape[1]):
            self.DMA(out[:, c], in_[:, c], r, w, final=final)

    def MEMSET(self, ap, val, w, eng="dve"):
        self.S.add(eng, lambda e: e.memset(ap, val), (), w)

    def RSUM(self, out, in_, r, w):
        self.S.add("dve", lambda e: e.reduce_sum(out=out, in_=in_, axis=AX.X), r, w)

    def RMAX(self, out, in_, r, w):
        self.S.add("dve", lambda e: e.reduce_max(out=out, in_=in_, axis=AX.X), r, w)

    def RECIP(self, out, in_, r, w):
        self.S.add("dve", lambda e: e.reciprocal(out=out, in_=in_), r, w)

    def wload(self, src, kc, ncols):
        i = self._wsi % len(self.ws)
        self._wsi += 1
        t = self.ws[i]
        view = t[:, 0:kc * ncols].rearrange("p (k n) -> p k n", k=kc)
        self.DMA(view, src.rearrange("(k p) n -> p k n", p=128), (), [("ws", i)], q="pool")
        return view, ("ws", i)

    def declare(self):
        nc, D, T = self.nc, self.depth, self.T
        di = lambda n, s: nc.dram_tensor(n, s, F32, kind="ExternalInput").ap()
        do = lambda n, s: nc.dram_tensor(n, s, F32, kind="ExternalOutput").ap()
        self.d_xT = di("xT_in", [128, 8, T])
        self.d_cT = di("cT_in", [128, 8, 17])
        self.d_wada = di("w_ada", [D, 1024, 6144])
        self.d_badaT = di("b_adaT", [128, D, 48])
        self.d_win = di("w_in", [D, 1024, 9228])
        self.d_fvec = di("fvec", [128, D, NFV])
        self.d_rvec = di("rvec", [D, NRV])
        self.d_sgub = di("sgub", [D, 2, 512])
        self.d_sguw = di("sgu_wT", [D, 2, 128, 4, 128])
        self.d_poolw = di("pool_w", [D, 4, 128, 128])
        self.d_wbr = [di("w_br_a", [D, 512, 1024]), di("w_br_b", [D, 512, 1024]), di("w_br_c", [D, 512, 1024]),
                      di("w_br_d", [D, 768, 1024])]
        self.d_wo = di("w_o", [D, 1024, 1024])
        self.d_router = di("router", [D, 1024, 36])
        self.d_wegp = di("w_e_gate_p", [D * 4096, 4096])
        self.d_weup = di("w_e_up_p", [D * 4096, 4096])
        self.d_wedp = di("w_e_down_p", [D * 4096, 4096])
        self.d_crow = di("crow", [1, 32 + self.NBLK])
        self.d_pidx = di("pidx", [128, 1])
        self.d_buf = nc.dram_tensor("moe_buf", [self.NBLK * 128, 1024], BF16, kind="Internal").ap()
        self.d_ybuf = nc.dram_tensor("moe_ybuf", [self.NBLK * 128, 1024], F32, kind="Internal").ap()
        self.d_convS = di("conv_sT", [D, 128, 4, 16, 30])
        self.d_poolS = di("pool_sT", [D, 128, 4, 16, 15])
        self.d_sconvS = di("sconv_sT", [D, 128, 14, 16, 3])
        self.d_ssmS = di("ssm_sT", [D, 128, 16, 768])
        self.d_consts = di("consts", [128, 8, 128])
        self.d_m3 = di("m3", [128, 16])
        self.d_corr = di("corr", [128, 4, 15])
        self.o_yT = do("yT", [128, 8, T])
        self.o_convP = do("o_conv_p", [D, 128, 4, 30])
        self.o_poolP = do("o_pool_p", [D, 128, 4, 15])
        self.o_sconvP = do("o_sconv_p", [D, 128, 14, 3])
        self.o_ssmP = do("o_ssm_p", [D, 128, 768])
        self.o_convS = do("o_conv_s", [D, 128, 4, 16, 30])
        self.o_poolS = do("o_pool_s", [D, 128, 4, 16, 15])
        self.o_sconvS = do("o_sconv_s", [D, 128, 14, 16, 3])
        self.o_ssmS = do("o_ssm_s", [D, 128, 16, 768])
        self.o_vS = do("o_v_s", [D, 128, 512])
        self.o_dbg1 = do("dbg_desti", [128, self.ntile * 2])
        self.o_dbg2 = do("dbg_widx", [128, self.NBLK])
        self.o_dbg3 = do("dbg_gm", [128, 320])

        sb = lambda n, s, dt=F32: nc.alloc_sbuf_tensor(n, s, dt)
        self.xT = sb("xT", [128, 8, T])
        self.ws = [sb(f"ws{i}", [128, 4096], BF16) for i in range(NWS)]
        self.ada = sb("ada", [128, 48, 17])
        self.cTf = sb("cTf", [128, 8, 17])
        self.scT = sb("scT", [128, 8, 17], BF16)
        self.badaT = sb("badaT", [128, D, 48])
        self.fvec = sb("fvecs", [128, D, NFV])
        self.rvec = sb("rvecs", [128, NRV])
        self.sgub = sb("sgubs", [128, 512])
        self.consts = sb("constss", [128, 8, 128])
        self.identB = sb("identB", [128, 128], BF16)
        self.m3 = sb("m3s", [128, 16])
        self.corr = sb("corrs", [128, 4, 15])
        self.sguwB = sb("sguwB", [128, 2, 4, 128], BF16)
        self.poolw = sb("poolws", [128, 4, 128], BF16)
        self.routerw = sb("routerws", [128, 8, 36])
        self.arow = sb("arow", [128, 12])
        self.convc = sb("convc", [128, 4, 30])
        self.poolc = sb("poolc", [128, 4, 15])
        self.sconvc = sb("sconvc", [128, 14, 3])
        self.state = sb("state", [128, 768])
        self.stateB = sb("stateB", [128, 768], BF16)
        self.X2 = sb("X2", [128, 8, 128])
        self.sm = sb("sm", [128, 128])
        self.st1 = sb("st1", [128, 128])
        self.st2 = sb("st2", [128, 128])
        self.st3 = sb("st3", [128, 128])
        self.xm = sb("xm", [128, 8, 128], BF16)
        self.merged = sb("merged", [128, 8, 128])
        self.mergedB = sb("mergedB", [128, 8, 128], BF16)
        self.g4a = sb("g4a", [128, 4, 128])
        self.g4b = sb("g4b", [128, 4, 128])
        self.fa = sb("fa", [128, 1024])
        self.fb = sb("fb", [128, 1024])
        self.fc = sb("fc", [128, 1024])
        self.yk = sb("yk", [128, 6, 128], BF16)
        self.hb1 = sb("hb1", [128, 768], BF16)
        self.hb2 = sb("hb2", [128, 768], BF16)
        self.ext = sb("ext", [128, 4, 158])
        self.pext = sb("pext", [128, 4, 143])
        self.ptA = sb("ptA", [128, 143])
        self.ptB = sb("ptB", [128, 143])
        self.xext = [sb(f"xext{i}", [128, 4, 131]) for i in range(2)]
        self.xc = sb("xc", [128, 6, 128])
        self.bc = sb("bc", [128, 8, 128], BF16)
        self.btok = sb("btok", [128, 4, 128], BF16)
        self.dec = sb("dec", [128, 12, 128], BF16)
        self.cmT = sb("cmT", [128, 4, 128], BF16)
        NT, NBk = self.ntile, self.NBLK
        self.rt = sb("rt", [128, 256])
        self.OHall = sb("OHall", [128, NT, 64])
        self.Wv = sb("Wv", [128, NT, 2])
        self.rank = sb("rank", [128, NT, 2])
        self.desti = sb("desti", [128, NT, 2], mybir.dt.int32)
        self.carry = sb("carry", [128, 64])
        self.gm = sb("gm", [128, 320])
        self.blke = sb("blke", [128, NBk])
        self.widx = sb("widx", [128, NBk], mybir.dt.int32)
        self.crow = sb("crows", [128, 32 + NBk])
        self.pidx = sb("pidxs", [128, 1])
        self.xblk = [sb(f"xblk{i}", [128, 1024], BF16) for i in range(2)]
        self.xtok = self.xblk
        self.xbT = sb("xbT", [128, 8, 128], BF16)
        self.hidtok = sb("hidtok", [128, 512], BF16)
        self.hidT = sb("hidT", [128, 4, 128], BF16)
        self.yblk = [self.fa, self.fc]
        self.psf = [nc.alloc_psum_tensor(f"psf{i}", [128, 512], F32) for i in range(7)]
        self.psb = nc.alloc_psum_tensor("psb", [128, 1024], BF16)

    def program(self):
        D = self.depth
        cs = self.consts
        self.identF, self.maskP, self.maskS = cs[:, 0, :], cs[:, 1, :], cs[:, 2, :]
        self.negP, self.negS, self.bdS, self.onesF = cs[:, 3, :], cs[:, 4, :], cs[:, 5, :], cs[:, 6, :]
        self.triS = cs[:, 7, :]
        self.DMA(self.crow[:], self.d_crow.partition_broadcast(128), (), ["crow"])
        self.DMA(self.pidx[:], self.d_pidx, (), ["pidx"])
        self.MEMSET(self.xblk[0][:], 0.0, [("xblk", 0)])
        for b in range(self.NBLK):
            self.DMA(self.d_buf[b * 128:(b + 1) * 128, :], self.xblk[0][:], [("xblk", 0)], ["bufz"])
        self.xres = [("xT", t) for t in range(self.ntile)]
        self.DMA(self.xT[:], self.d_xT, (), self.xres)
        self.DMA(self.cTf[:], self.d_cT, (), ["cTf"])
        self.DMA(self.consts[:], self.d_consts, (), ["consts"])
        self.DMA(self.m3[:], self.d_m3, (), ["m3"])
        self.DMA(self.corr[:], self.d_corr, (), ["corr"])
        self.DMA(self.badaT[:], self.d_badaT, (), ["badaT"])
        self.DMA(self.fvec[:], self.d_fvec, (), ["fvec"])
        self.CP(self.identB[:], self.identF, ["consts"], ["identB"])
        self.ACT(self.scT[:], self.cTf[:], AF.Silu, ["cTf"], ["scT"])
        for l in range(D):
            self.layer(l)
        self.DMA(self.o_yT, self.xT[:], self.xres, (), final=True)

    def fv(self, l, off, n=1):
        return self.fvec[:, l, off:off + n]

    def layer(self, l):
        self.DMA(self.rvec[:], self.d_rvec[l:l + 1, :].partition_broadcast(128), (), ["rvec"])
        self.DMA(self.sgub[:], self.d_sgub[l, 0:1, :].partition_broadcast(128), (), ["sgub"])
        self.DMA(self.poolw[:], self.d_poolw[l].rearrange("g c d -> c g d"), (), ["poolw"], q="pool")
        self.DMA(self.routerw[:], self.d_router[l].rearrange("(k p) n -> p k n", p=128), (), ["routerw"])
        for v in range(2):
            stg = self.X2[:, 0:4, :]
            self.DMA(stg, self.d_sguw[l, v], (), ["X2"])
            m = self.maskP if v == 0 else self.maskS
            self.TT(self.sguwB[:, v], stg, m.unsqueeze(1).to_broadcast([128, 4, 128]), ALU.mult,
                    ["X2", "consts"], ["sguwB"])
        self.ACT(self.arow[:], self.rvec[:, RV_ALOG:RV_ALOG + 12], AF.Exp, ["rvec"], ["arow"])
        self.TS(self.arow[:], self.arow[:], -1.0, None, ALU.mult, None, ["arow"], ["arow"])
        for fb in range(12):
            wv, wr = self.wload(self.d_wada[l][:, fb * 512:(fb + 1) * 512], 8, 512)
            ps = self.psf[fb % 2]
            mms = []
            for j in range(4):
                for dc in range(8):
                    mms.append((ps[:, j * 32:j * 32 + 17], wv[:, dc, j * 128:(j + 1) * 128], self.scT[:, dc, :],
                                dc == 0, dc == 7))
            self.MM(mms, [wr, "scT"], [("ps", fb % 2)])
            for j in range(4):
                fcx = fb * 4 + j
                self.ACT(self.ada[:, fcx, :], ps[:, j * 32:j * 32 + 17], AF.Identity, [("ps", fb % 2), "badaT"],
                         ["ada"], bias=self.badaT[:, l, fcx:fcx + 1])
        self.TS(self.ada[:, 8:16, :], self.ada[:, 8:16, :], 1.0, None, ALU.add, None, ["ada"], ["ada"])
        self.TS(self.ada[:, 32:40, :], self.ada[:, 32:40, :], 1.0, None, ALU.add, None, ["ada"], ["ada"])
        self.MEMSET(self.convc[:], 0.0, ["convc"])
        self.MEMSET(self.poolc[:], 0.0, ["poolc"])
        self.MEMSET(self.sconvc[:], 0.0, ["sconvc"])
        self.MEMSET(self.state[:], 0.0, ["state"])
        self.MEMSET(self.stateB[:], 0.0, ["stateB"])
        for ti in range(self.ntile):
            if ti == self.npt:
                self.DMA(self.sgub[:], self.d_sgub[l, 1:2, :].partition_broadcast(128), (), ["sgub"])
            self.mix_tile(l, ti)
        if self.stop_after == "mix":
            return
        self.moe(l)

    def modulate(self, ti, sh0, sc0, out, wr):
        t0 = ti * 128
        kindS = ti == self.npt
        xr = ("xT", ti)
        for c in range(8):
            xin = self.xT[:, c, t0:t0 + 128]
            if not kindS:
                self.ACT(out[:, c, :], xin, AF.Identity, [xr, "ada"], wr, scale=self.ada[:, sc0 + c, 0:1],
                         bias=self.ada[:, sh0 + c, 0:1])
            else:
                v3 = lambda a: a.rearrange("p (s j) -> p s j", s=16)
                scb = self.ada[:, sc0 + c, 1:17].unsqueeze(2).to_broadcast([128, 16, 8])
                shb = self.ada[:, sh0 + c, 1:17].unsqueeze(2).to_broadcast([128, 16, 8])
                self.TT(v3(self.st1[:]), v3(xin), scb, ALU.mult, [xr, "ada"], ["st1"])
                self.TT(v3(out[:, c, :]), v3(self.st1[:]), shb, ALU.add, ["st1", "ada"], wr)

    def ln_feat(self, src, nch, rd, g_off, b_off, l, dst_fn, dst_res, func=AF.Identity):
        n = nch * 128
        sq = self.X2[:, 0:nch, :]
        self.ACT(sq, src, AF.Square, rd, ["X2"])
        p1, p2 = self.psf[5], self.psf[6]
        self.MM([(p1[:, 0:128], self.onesF, src[:, c, :], c == 0, c == nch - 1) for c in range(nch)], rd + ["consts"],
                [("ps", 5)])
        self.MM([(p2[:, 0:128], self.onesF, sq[:, c, :], c == 0, c == nch - 1) for c in range(nch)], ["X2", "consts"],
                [("ps", 6)])
        mean, rstd, tmp = self.st1[:], self.st2[:], self.st3[:]
        self.TS(mean, p1[:, 0:128], 1.0 / n, None, ALU.mult, None, [("ps", 5)], ["st1"])
        self.TT(tmp, mean, mean, ALU.mult, ["st1"], ["st3"])
        self.STT(rstd, p2[:, 0:128], 1.0 / n, tmp, ALU.mult, ALU.subtract, [("ps", 6), "st3"], ["st2"])
        self.TS(rstd, rstd, LN_EPS, None, ALU.add, None, ["st2"], ["st2"])
        self.SQRT(rstd, "st2")
        self.RECIP(rstd, rstd, ["st2"], ["st2"])
        self.TT(src, src, mean.unsqueeze(1).to_broadcast([128, nch, 128]), ALU.subtract, rd + ["st1"], rd)
        self.TT(src, src, rstd.unsqueeze(1).to_broadcast([128, nch, 128]), ALU.mult, rd + ["st2"], rd)
        for c in range(nch):
            self.ACT(dst_fn(c), src[:, c, :], func, rd + ["fvec"], dst_res, scale=self.fv(l, g_off + c),
                     bias=self.fv(l, b_off + c))

    def merge_branch(self, l, k, ykT, kc, rd):
        for h in range(2):
            wg, wgr = self.wload(self.d_win[l][:, C_GL + k * 1024 + h * 512: C_GL + k * 1024 + (h + 1) * 512], 8, 512)
            pg = self.psf[2]
            mms = []
            for mc in range(4):
                for dc in range(8):
                    mms.append((pg[:, mc * 128:(mc + 1) * 128], wg[:, dc, mc * 128:(mc + 1) * 128], self.xm[:, dc, :],
                                dc == 0, dc == 7))
            self.MM(mms, [wgr, "xm"], [("ps", 2)])
            sig = self.fa[:, 0:512]
            self.ACT(sig, pg[:], AF.Sigmoid, [("ps", 2)], ["fa"])
            wb, wbr = self.wload(self.d_wbr[k][l][:, h * 512:(h + 1) * 512], kc, 512)
            pb = self.psf[3]
            mms = []
            for mc in range(4):
                for c in range(kc):
                    mms.append((pb[:, mc * 128:(mc + 1) * 128], wb[:, c, mc * 128:(mc + 1) * 128], ykT[:, c, :],
                                c == 0, c == kc - 1))
            self.MM(mms, [wbr] + rd, [("ps", 3)])
            mg = self.merged[:, h * 4:(h + 1) * 4, :].rearrange("p c t -> p (c t)")
            if k == 0:
                self.TT(mg, sig, pb[:], ALU.mult, ["fa", ("ps", 3)], [("merged", h)])
            else:
                self.TT(sig, sig, pb[:], ALU.mult, ["fa", ("ps", 3)], ["fa"])
                self.TT(mg, mg, sig, ALU.add, ["fa", ("merged", h)], [("merged", h)])

    def mix_tile(self, l, ti):
        kindS = ti == self.npt
        lastP = ti == self.npt - 1
        t0 = ti * 128
        L = 8 if kindS else 128
        xr = ("xT", ti)
        ps = self.psf
        xm = self.xm
        sm = self.sm
        yk = self.yk
        h3 = lambda a: a.rearrange("p (h d) -> p h d", h=12)
        c4 = lambda a, c=4: a.rearrange("p (c t) -> p c t", c=c)
        self.modulate(ti, 0, 8, xm, ["xm"])

        def proj_feat(col0, ncols, pst, psr):
            wv, wr = self.wload(self.d_win[l][:, col0:col0 + ncols], 8, ncols)
            mms = []
            for fc in range(ncols // 128):
                for dc in range(8):
                    mms.append((pst[:, fc * 128:(fc + 1) * 128], wv[:, dc, fc * 128:(fc + 1) * 128], xm[:, dc, :], dc == 0,
                                dc == 7))
            self.MM(mms, [wr, "xm"], [psr])

        def proj_tok(col0, ncols, pst, psr):
            wv, wr = self.wload(self.d_win[l][:, col0:col0 + ncols], 8, ncols)
            self.MM([(pst[:, 0:ncols], xm[:, dc, :], wv[:, dc, :], dc == 0, dc == 7) for dc in range(8)], [wr, "xm"], [psr])

        def seq_groups(per):
            return [(0, 1)] if not kindS else [(s, per) for s in range(0, 16, per)]

        proj_feat(C_U, 512, ps[0], ("ps", 0))
        gu = self.g4a[:].rearrange("p c t -> p (c t)")
        self.ACT(gu, ps[0][:], AF.Gelu_apprx_tanh, [("ps", 0)], ["g4a"])
        proj_tok(C_V, 512, ps[1], ("ps", 1))
        vg = self.fa[:, 0:512]
        self.ACT(vg, ps[1][:], AF.Gelu_apprx_tanh, [("ps", 1)], ["fa"])
        self.RSUM(sm[:, 0:1], vg, ["fa"], ["sm"])
        self.TT(self.fb[:, 0:512], vg, vg, ALU.mult, ["fa"], ["fb"])
        self.RSUM(sm[:, 1:2], self.fb[:, 0:512], ["fb"], ["sm"])
        self.TS(sm[:, 0:1], sm[:, 0:1], 1.0 / 512, None, ALU.mult, None, ["sm"], ["sm"])
        self.TT(sm[:, 2:3], sm[:, 0:1], sm[:, 0:1], ALU.mult, ["sm"], ["sm"])
        self.STT(sm[:, 3:4], sm[:, 1:2], 1.0 / 512, sm[:, 2:3], ALU.mult, ALU.subtract, ["sm"], ["sm"])
        self.TS(sm[:, 3:4], sm[:, 3:4], LN_EPS, None, ALU.add, None, ["sm"], ["sm"])
        self.SQRT(sm[:, 3:4], "sm")
        self.RECIP(sm[:, 3:4], sm[:, 3:4], ["sm"], ["sm"])
        self.TS(vg, vg, sm[:, 0:1], sm[:, 3:4], ALU.subtract, ALU.mult, ["fa", "sm"], ["fa"])
        self.TT(vg, vg, self.rvec[:, RV_SLG:RV_SLG + 512], ALU.mult, ["fa", "rvec"], ["fa"])
        self.TT(vg, vg, self.rvec[:, RV_SLB:RV_SLB + 512], ALU.add, ["fa", "rvec"], ["fa"])
        if kindS:
            self.DMA(self.o_vS[l], vg, ["fa"], (), final=True)
        vnb = self.hb1[:, 0:512]
        self.CP(vnb, vg, ["fa"], ["hb1"], eng="act")
        sw = self.sguwB[:, 1 if kindS else 0]
        self.MM([(ps[0][:, g * 128:(g + 1) * 128], vnb[:, g * 128:(g + 1) * 128], sw[:, g, :], True, True) for g in range(4)],
                ["hb1", "sguwB"], [("ps", 0)])
        self.TT(self.fb[:, 0:512], ps[0][:], self.sgub[:], ALU.add, [("ps", 0), "sgub"], ["fb"])
        self.TT(yk[:, 0:4, :].rearrange("p c t -> p (c t)"), self.fb[:, 0:512], gu, ALU.mult, ["fb", "g4a"], ["yk"])
        self.merge_branch(l, 0, yk, 4, ["yk"])

        proj_feat(C_GA, 512, ps[0], ("ps", 0))
        proj_feat(C_GG, 512, ps[1], ("ps", 1))
        self.ACT(self.g4a[:].rearrange("p c t -> p (c t)"), ps[1][:], AF.Sigmoid, [("ps", 1)], ["g4a"])
        E = 30 + L
        acc = self.g4b
        accr = [("g4b", c) for c in range(4)]
        for (s0, ns) in seq_groups(4):
            a0, n = s0 * L, ns * L
            ext = self.ext[:, :, 0:ns * E].rearrange("p c (s e) -> p c s e", s=ns)
            sv = lambda a: a[:, :, a0:a0 + n].rearrange("p c (s j) -> p c s j", s=ns)
            if kindS:
                self.DMA4(ext[:, :, :, 0:30], self.d_convS[l][:, :, s0:s0 + ns, :], (), ["ext"])
            else:
                self.CP(ext[:, :, 0, 0:30], self.convc[:], ["convc"], ["ext"])
            self.TT(ext[:, :, :, 30:E], sv(c4(ps[0][:])), sv(self.g4a[:]), ALU.mult, [("ps", 0), "g4a"], ["ext"])
            if kindS:
                self.DMA4(self.o_convS[l][:, :, s0:s0 + ns, :], ext[:, :, :, L:E], ["ext"], (), final=True)
            else:
                self.CP(self.convc[:], ext[:, :, 0, L:E], ["ext"], ["convc"])
                if lastP:
                    self.DMA(self.o_convP[l], self.convc[:], ["convc"], (), final=True)
            for k in range(31):
                for c in range(4):
                    a3 = acc[:, c, a0:a0 + n].rearrange("p (s j) -> p s j", s=ns)
                    wk = self.fv(l, FV_CW + c * 31 + k)
                    if k == 0:
                        self.TS(a3, ext[:, c, :, 0:L], wk, self.fv(l, FV_CONVB + c), ALU.mult, ALU.add, ["ext", "fvec"],
                                [("g4b", c)])
                    else:
                        self.STT(a3, ext[:, c, :, k:k + L], wk, a3, ALU.mult, ALU.add, ["ext", "fvec", ("g4b", c)],
                                 [("g4b", c)])
        self.ln_feat(acc[:], 4, accr, FV_CLNG, FV_CLNB, l, lambda c: yk[:, c, :], ["yk"], func=AF.Silu)
        self.merge_branch(l, 1, yk, 4, ["yk"])

        proj_feat(C_PIN, 512, ps[0], ("ps", 0))
        E = 15 + L
        mixed = self.hb2[:, 0:512].rearrange("p (c t) -> p c t", c=4)
        for (s0, ns) in seq_groups(4):
            a0, n = s0 * L, ns * L
            pext = self.pext[:, :, 0:ns * E].rearrange("p c (s e) -> p c s e", s=ns)
            sv = lambda a: a[:, :, a0:a0 + n].rearrange("p c (s j) -> p c s j", s=ns)
            if kindS:
                self.DMA4(pext[:, :, :, 0:15], self.d_poolS[l][:, :, s0:s0 + ns, :], (), ["pext"])
            else:
                self.CP(pext[:, :, 0, 0:15], self.poolc[:], ["poolc"], ["pext"])
            self.CP(pext[:, :, :, 15:E], sv(c4(ps[0][:])), [("ps", 0)], ["pext"], eng="act")
            if kindS:
                self.DMA4(self.o_poolS[l][:, :, s0:s0 + ns, :], pext[:, :, :, L:E], ["pext"], (), final=True)
            else:
                self.CP(self.poolc[:], pext[:, :, 0, L:E], ["pext"], ["poolc"])
                if lastP:
                    self.DMA(self.o_poolP[l], self.poolc[:], ["poolc"], (), final=True)
            tA = self.ptA[:, 0:ns * E].rearrange("p (s e) -> p s e", s=ns)
            tB = self.ptB[:, 0:ns * E].rearrange("p (s e) -> p s e", s=ns)
            for gi in range(4):
                src, srcr = pext[:, gi], "pext"
                bufs = [(tA, "ptA"), (tB, "ptB")]
                sh = 1
                for step in range(gi + 1):
                    dst, dstr = bufs[step % 2]
                    lo = 2 * sh - 1
                    self.TT(dst[:, :, lo:E], src[:, :, lo:E], src[:, :, lo - sh:E - sh], ALU.add, [srcr], [dstr])
                    src, srcr = dst, dstr
                    sh *= 2
                win = 2 ** (gi + 1)
                if (not kindS) and ti == 0:
                    self.TT(src[:, 0, 15:30], src[:, 0, 15:30], self.corr[:, gi, :], ALU.mult, [srcr, "corr"], [srcr])
                mo = mixed[:, gi, a0:a0 + n].rearrange("p (s j) -> p s j", s=ns)
                self.STT(mo, src[:, :, 15:E], 1.0 / win, pext[:, gi, :, 15:E], ALU.mult, ALU.subtract, [srcr, "pext"], ["hb2"])
        self.MM([(ps[1][:, g * 128:(g + 1) * 128], self.poolw[:, g, :], mixed[:, g, :], True, True) for g in range(4)],
                ["hb2", "poolw"], [("ps", 1)])
        self.TT(yk[:, 0:4, :], c4(ps[1][:]), self.fv(l, FV_PSC, 4).unsqueeze(2).to_broadcast([128, 4, 128]), ALU.mult,
                [("ps", 1), "fvec"], ["yk"])
        self.merge_branch(l, 2, yk, 4, ["yk"])

        proj_tok(C_ZZ, 512, ps[4], ("ps", 4))
        proj_tok(C_ZZ + 512, 256, ps[5], ("ps", 5))
        zs = self.fa[:, 0:768]
        self.ACT(zs[:, 0:512], ps[4][:], AF.Silu, [("ps", 4)], ["fa"])
        self.ACT(zs[:, 512:768], ps[5][:, 0:256], AF.Silu, [("ps", 5)], ["fa"])
        E = 3 + L
        xc = self.xc
        for j, (c0, nco) in enumerate([(0, 512), (512, 512), (1024, 512), (1536, 256)]):
            pst, psr = ps[j % 2], ("ps", j % 2)
            proj_feat(C_XBC + c0, nco, pst, psr)
            nch = nco // 128
            ch0 = c0 // 128
            xe, xer = self.xext[j % 2], ("xext", j % 2)
            for (s0, ns) in seq_groups(8):
                a0, n = s0 * L, ns * L
                xext = xe[:, 0:nch, 0:ns * E].rearrange("p c (s e) -> p c s e", s=ns)
                if kindS:
                    self.DMA4(xext[:, :, :, 0:3], self.d_sconvS[l][:, ch0:ch0 + nch, s0:s0 + ns, :], (), [xer])
                else:
                    self.CP(xext[:, :, 0, 0:3], self.sconvc[:, ch0:ch0 + nch, :], ["sconvc"], [xer])
                self.CP(xext[:, :, :, 3:E], c4(pst[:, 0:nco], nch)[:, :, a0:a0 + n].rearrange("p c (s j) -> p c s j", s=ns),
                        [psr], [xer], eng="act")
                if kindS:
                    self.DMA4(self.o_sconvS[l][:, ch0:ch0 + nch, s0:s0 + ns, :], xext[:, :, :, L:E], [xer], (), final=True)
                else:
                    self.CP(self.sconvc[:, ch0:ch0 + nch, :], xext[:, :, 0, L:E], [xer], ["sconvc"])
                for k in range(4):
                    for cc in range(nch):
                        c = ch0 + cc
                        a3 = self.g4b[:, cc, a0:a0 + n].rearrange("p (s j) -> p s j", s=ns)
                        wk = self.fv(l, FV_SCW + c * 4 + k)
                        if k == 0:
                            self.TS(a3, xext[:, cc, :, 0:L], wk, self.fv(l, FV_SCB + c), ALU.mult, ALU.add, [xer, "fvec"],
                                    [("g4b", cc)])
                        else:
                            self.STT(a3, xext[:, cc, :, k:k + L], wk, a3, ALU.mult, ALU.add, [xer, "fvec", ("g4b", cc)],
                                     [("g4b", cc)])
            gr = [("g4b", cc) for cc in range(nch)]
            if j == 0:
                self.ACT(xc[:, 0:4, :], self.g4b[:, 0:4, :], AF.Silu, gr, ["xc"])
            elif j == 1:
                self.ACT(xc[:, 4:6, :], self.g4b[:, 0:2, :], AF.Silu, gr, ["xc"])
                self.ACT(self.bc[:, 0:2, :], self.g4b[:, 2:4, :], AF.Silu, gr, ["bc"])
            elif j == 2:
                self.ACT(self.bc[:, 2:6, :], self.g4b[:, 0:4, :], AF.Silu, gr, ["bc"])
            else:
                self.ACT(self.bc[:, 6:8, :], self.g4b[:, 0:2, :], AF.Silu, gr, ["bc"])
        if lastP:
            self.DMA(self.o_sconvP[l], self.sconvc[:], ["sconvc"], (), final=True)
        wv, wr = self.wload(self.d_win[l][:, C_DT:C_DT + 12], 8, 12)
        self.MM([(ps[2][:, 0:12], xm[:, dc, :], wv[:, dc, :], dc == 0, dc == 7) for dc in range(8)], [wr, "xm"], [("ps", 2)])
        dt, dta = sm[:, 8:20], sm[:, 20:32]
        self.TT(dt, ps[2][:, 0:12], self.rvec[:, RV_DTB:RV_DTB + 12], ALU.add, [("ps", 2), "rvec"], ["sm"])
        self.ACT(dt, dt, AF.Exp, ["sm"], ["sm"])
        self.TS(dt, dt, 1.0, None, ALU.add, None, ["sm"], ["sm"])
        self.ACT(dt, dt, AF.Ln, ["sm"], ["sm"])
        self.TT(dta, dt, self.arow[:], ALU.mult, ["sm", "arow"], ["sm"])
        self.TR([(ps[4][:, c * 128:(c + 1) * 128], xc[:, c, :], self.identF) for c in range(4)], ["xc", "consts"], [("ps", 4)])
        self.TR([(ps[5][:, (c - 4) * 128:(c - 3) * 128], xc[:, c, :], self.identF) for c in range(4, 6)], ["xc", "consts"],
                [("ps", 5)])
        xs = self.fb[:, 0:768]
        self.CP(xs[:, 0:512], ps[4][:], [("ps", 4)], ["fb"], eng="act")
        self.CP(xs[:, 512:768], ps[5][:, 0:256], [("ps", 5)], ["fb"], eng="act")
        self.TR([(self.psb[:, g * 128:(g + 1) * 128], self.bc[:, g, :], self.identB[:]) for g in range(4)], ["bc", "identB"],
                [("psb", 0)])
        self.CP(self.btok[:].rearrange("p g t -> p (g t)"), self.psb[:, 0:512], [("psb", 0)], ["btok"])
        xd = self.hb1
        self.TT(h3(xd[:]), h3(xs[:]), dt.unsqueeze(2).to_broadcast([128, 12, 64]), ALU.mult, ["fb", "sm"], ["hb1"])
        tri = self.maskS if kindS else self.maskP
        bd = self.bdS if kindS else self.onesF
        neg = self.negS if kindS else self.negP
        self.MM([(ps[2][:, 16:28], tri, dta, True, True), (ps[2][:, 28:40], bd, dta, True, True)], ["sm", "consts"], [("ps", 2)])
        acs, tot, nacs, eacs, dst, et = sm[:, 32:44], sm[:, 44:56], sm[:, 56:68], sm[:, 68:80], sm[:, 80:92], sm[:, 92:104]
        self.CP(sm[:, 32:56], ps[2][:, 16:40], [("ps", 2)], ["sm"])
        self.TS(nacs, acs, -1.0, None, ALU.mult, None, ["sm"], ["sm"])
        self.ACT(eacs, acs, AF.Exp, ["sm"], ["sm"])
        self.TT(dst, tot, acs, ALU.subtract, ["sm"], ["sm"])
        self.ACT(dst, dst, AF.Exp, ["sm"], ["sm"])
        pxi = [0, 1, 3]
        for b3 in range(3):
            pxt, pxr = ps[pxi[b3]], ("ps", pxi[b3])
            mms = []
            for hh in range(4):
                h = b3 * 4 + hh
                o = pxt[:, hh * 128:(hh + 1) * 128]
                mms.append((o, dta[:, h:h + 1].to_broadcast([128, 128]), tri, True, False))
                mms.append((o, self.identF, neg, False, True))
            self.MM(mms, ["sm", "consts"], [pxr])
            for hh in range(4):
                h = b3 * 4 + hh
                self.ACT(self.dec[:, h, :], pxt[:, hh * 128:(hh + 1) * 128], AF.Exp, [pxr, "sm"], [("dec", h // 3)],
                         bias=nacs[:, h:h + 1])
        self.MM([(ps[2][:, g * 128:(g + 1) * 128], self.bc[:, g, :], self.bc[:, 4 + g, :], True, True) for g in range(4)], ["bc"],
                [("ps", 2)])
        for g in range(4):
            self.TT(self.dec[:, 3 * g:3 * g + 3, :], ps[2][:, g * 128:(g + 1) * 128].unsqueeze(1).to_broadcast([128, 3, 128]),
                    self.dec[:, 3 * g:3 * g + 3, :], ALU.mult, [("ps", 2), ("dec", g)], [("dec", g)])
        decr = [("dec", g) for g in range(4)]
        pyo = lambda pa, pb, h: (pa if h < 8 else pb)[:, (h % 8) * 64:(h % 8) * 64 + 64]
        self.MM([(pyo(ps[4], ps[5], h), self.dec[:, h, :], xd[:, h * 64:(h + 1) * 64], True, True) for h in range(12)],
                decr + ["hb1"], [("ps", 4), ("ps", 5)])
        xdd = self.hb2
        self.TT(h3(xdd[:]), h3(xd[:]), dst.unsqueeze(2).to_broadcast([128, 12, 64]), ALU.mult, ["hb1", "sm"], ["hb2"])
        if not kindS:
            self.MM([(pyo(ps[6], ps[2], h), self.bc[:, 4 + h // 3, :], self.stateB[:, h * 64:(h + 1) * 64], True, True)
                     for h in range(12)], ["bc", "stateB"], [("ps", 6), ("ps", 2)])
        else:
            sgi, sgB, sout, xddm = self.state, self.stateB, self.fc[:, 0:768], self.hb1
            for i in range(16):
                self.DMA(sgi[:], self.d_ssmS[l][:, i, :], (), ["state"])
                self.CP(sgB[:], sgi[:], ["state"], ["stateB"], eng="act")
                self.MEMSET(self.cmT[:], 0.0, ["cmT"])
                self.CP(self.cmT[:, :, i * 8:(i + 1) * 8], self.bc[:, 4:8, i * 8:(i + 1) * 8], ["bc"], ["cmT"])
                self.MM([(pyo(ps[6], ps[2], h), self.cmT[:, h // 3, :], sgB[:, h * 64:(h + 1) * 64], i == 0, i == 15)
                         for h in range(12)], ["cmT", "stateB"], [("ps", 6), ("ps", 2)])
                self.TS(xddm[:], xdd[:], self.m3[:, i:i + 1], None, ALU.mult, None, ["hb2", "m3"], ["hb1"])
                self.MM([(pyo(ps[0], ps[1], h), self.btok[:, h // 3, :], xddm[:, h * 64:(h + 1) * 64], True, True)
                         for h in range(12)], ["btok", "hb1"], [("ps", 0), ("ps", 1)])
                self.MM([(ps[3][:, 0:12], self.m3[:, i:i + 1].to_broadcast([128, 128]), dta, True, True)], ["m3", "sm"],
                        [("ps", 3)])
                self.ACT(et, ps[3][:, 0:12], AF.Exp, [("ps", 3)], ["sm"])
                self.TT(h3(sout[:]), h3(sgi[:]), et.unsqueeze(2).to_broadcast([128, 12, 64]), ALU.mult, ["state", "sm"], ["fc"])
                self.TT(sout[:, 0:512], sout[:, 0:512], ps[0][:], ALU.add, ["fc", ("ps", 0)], ["fc"])
                self.TT(sout[:, 512:768], sout[:, 512:768], ps[1][:, 0:256], ALU.add, ["fc", ("ps", 1)], ["fc"])
                self.DMA(self.o_ssmS[l][:, i, :], sout[:], ["fc"], (), final=True)
        y = self.fc[:, 0:768]
        self.TT(h3(y[:])[:, 0:8, :], ps[6][:].rearrange("p (h d) -> p h d", h=8),
                eacs[:, 0:8].unsqueeze(2).to_broadcast([128, 8, 64]), ALU.mult, [("ps", 6), "sm"], ["fc"])
        self.TT(h3(y[:])[:, 8:12, :], ps[2][:, 0:256].rearrange("p (h d) -> p h d", h=4),
                eacs[:, 8:12].unsqueeze(2).to_broadcast([128, 4, 64]), ALU.mult, [("ps", 2), "sm"], ["fc"])
        self.TT(y[:, 0:512], y[:, 0:512], ps[4][:], ALU.add, ["fc", ("ps", 4)], ["fc"])
        self.TT(y[:, 512:768], y[:, 512:768], ps[5][:, 0:256], ALU.add, ["fc", ("ps", 5)], ["fc"])
        self.TT(h3(xs[:]), h3(xs[:]), self.rvec[:, RV_D:RV_D + 12].unsqueeze(2).to_broadcast([128, 12, 64]), ALU.mult,
                ["fb", "rvec"], ["fb"])
        self.TT(y[:], y[:], xs[:], ALU.add, ["fc", "fb"], ["fc"])
        self.TT(y[:], y[:], zs[:], ALU.mult, ["fc", "fa"], ["fc"])
        self.TT(xs[:], y[:], y[:], ALU.mult, ["fc"], ["fb"])
        self.RSUM(sm[:, 4:5], xs[:], ["fb"], ["sm"])
        self.TS(sm[:, 4:5], sm[:, 4:5], 1.0 / 768, LN_EPS, ALU.mult, ALU.add, ["sm"], ["sm"])
        self.SQRT(sm[:, 4:5], "sm")
        self.RECIP(sm[:, 4:5], sm[:, 4:5], ["sm"], ["sm"])
        yd = self.hb1
        self.TS(yd[:], y[:], sm[:, 4:5], None, ALU.mult, None, ["fc", "sm"], ["hb1"])
        self.TR([(self.psb[:, c * 128:(c + 1) * 128], yd[:, c * 128:(c + 1) * 128], self.identB[:]) for c in range(6)],
                ["hb1", "identB"], [("psb", 0), ("psb", 1)])
        for c in range(6):
            self.ACT(yk[:, c, :], self.psb[:, c * 128:(c + 1) * 128], AF.Identity, [("psb", 0), ("psb", 1), "fvec"], ["yk"],
                     scale=self.fv(l, FV_NG + c))
        if not kindS:
            self.MM([(pyo(ps[0], ps[1], h), self.btok[:, h // 3, :], xdd[:, h * 64:(h + 1) * 64], True, True) for h in range(12)],
                    ["btok", "hb2"], [("ps", 0), ("ps", 1)])
            self.ACT(et, tot, AF.Exp, ["sm"], ["sm"])
            self.TT(h3(self.state[:]), h3(self.state[:]), et.unsqueeze(2).to_broadcast([128, 12, 64]), ALU.mult,
                    ["state", "sm"], ["state"])
            self.TT(self.state[:, 0:512], self.state[:, 0:512], ps[0][:], ALU.add, ["state", ("ps", 0)], ["state"])
            self.TT(self.state[:, 512:768], self.state[:, 512:768], ps[1][:, 0:256], ALU.add, ["state", ("ps", 1)], ["state"])
            self.CP(self.stateB[:], self.state[:], ["state"], ["stateB"], eng="act")
            if lastP:
                self.DMA(self.o_ssmP[l], self.state[:], ["state"], (), final=True)
        self.merge_branch(l, 3, yk, 6, ["yk"])

        self.CP(self.mergedB[:], self.merged[:], [("merged", 0), ("merged", 1)], ["mergedB"], eng="act")
        xt = self.xT[:, :, t0:t0 + 128]
        self.TS(xt, xt, DN_ALPHA, None, ALU.mult, None, [xr, "xm"], [xr])
        for h in range(2):
            wo, wor = self.wload(self.d_wo[l][:, h * 512:(h + 1) * 512], 8, 512)
            pm, pmr = ps[h], ("ps", h)
            mms = []
            for mc in range(4):
                for c in range(8):
                    mms.append((pm[:, mc * 128:(mc + 1) * 128], wo[:, c, mc * 128:(mc + 1) * 128], self.mergedB[:, c, :], c == 0,
                                c == 7))
            self.MM(mms, [wor, "mergedB"], [pmr])
            if not kindS:
                for mc in range(4):
                    c = h * 4 + mc
                    self.STT(xt[:, c, :], pm[:, mc * 128:(mc + 1) * 128], self.ada[:, 16 + c, 0:1], xt[:, c, :], ALU.mult, ALU.add,
                             [pmr, "ada", xr], [xr])
            else:
                g1b = self.ada[:, 16 + h * 4:16 + h * 4 + 4, 1:17].unsqueeze(3).to_broadcast([128, 4, 16, 8])
                tmp = self.g4a[:].rearrange("p c (s j) -> p c s j", s=16)
                self.TT(tmp, pm[:].rearrange("p (c s j) -> p c s j", c=4, s=16), g1b, ALU.mult, [pmr, "ada"], ["g4a"])
                self.TT(xt[:, h * 4:h * 4 + 4, :], xt[:, h * 4:h * 4 + 4, :], self.g4a[:], ALU.add, ["g4a", xr], [xr])
        self.ln_feat(xt, 8, [xr], FV_L1G, FV_L1B, l, lambda c: xt[:, c, :], [xr])

    def route_tile(self, l, ti):
        ps = self.psf
        rt = self.rt
        X2 = self.X2
        self.modulate(ti, 24, 32, X2, ["X2"])
        self.MM([(ps[0][:, 0:36], X2[:, dc, :], self.routerw[:, dc, :], dc == 0, dc == 7) for dc in range(8)],
                ["X2", "routerw"], [("ps", 0)])
        lg = rt[:, 0:36]
        self.CP(lg, ps[0][:, 0:36], [("ps", 0)], ["rt"])
        R = lambda a, b: rt[:, a:b]
        mg, nmg, ohg, eg, sg_, les, v1 = R(36, 37), R(37, 38), R(40, 44), R(44, 48), R(48, 49), R(52, 60), R(60, 61)
        oh1, le2, v2, oh2 = R(64, 72), R(72, 80), R(80, 81), R(84, 92)
        e2, w1, w2, tmp32 = R(92, 93), R(93, 94), R(94, 95), R(112, 144)
        rr = ["rt"]
        self.RMAX(mg, lg[:, 0:4], rr, rr)
        self.TS(ohg, lg[:, 0:4], mg, None, ALU.is_equal, None, rr, rr)
        self.TS(nmg, mg, -1.0, None, ALU.mult, None, rr, rr)
        self.ACT(eg, lg[:, 0:4], AF.Exp, rr, rr, bias=nmg)
        self.RSUM(sg_, eg, rr, rr)
        self.RECIP(sg_, sg_, rr, rr)
        self.TT(tmp32.rearrange("p (g j) -> p g j", g=4), lg[:, 4:36].rearrange("p (g j) -> p g j", g=4),
                ohg.unsqueeze(2).to_broadcast([128, 4, 8]), ALU.mult, rr, rr)
        self.RSUM(les, tmp32.rearrange("p (g j) -> p j g", g=4), rr, rr)
        self.RMAX(v1, les, rr, rr)
        self.TS(oh1, les, v1, None, ALU.is_equal, None, rr, rr)
        self.STT(le2, oh1, -1e30, les, ALU.mult, ALU.add, rr, rr)
        self.RMAX(v2, le2, rr, rr)
        self.TS(oh2, le2, v2, None, ALU.is_equal, None, rr, rr)
        self.TT(e2, v2, v1, ALU.subtract, rr, rr)
        self.ACT(e2, e2, AF.Exp, rr, rr)
        self.TS(w1, e2, 1.0, None, ALU.add, None, rr, rr)
        self.RECIP(w1, w1, rr, rr)
        self.TT(w2, e2, w1, ALU.mult, rr, rr)
        self.TT(self.Wv[:, ti, 0:1], w1, sg_, ALU.mult, rr, ["Wv"])
        self.TT(self.Wv[:, ti, 1:2], w2, sg_, ALU.mult, rr, ["Wv"])
        oh = self.OHall[:, ti, :]
        ohr = ("OH", ti)
        g3 = lambda a: a.rearrange("p (g j) -> p g j", g=4)
        self.TT(g3(oh[:, 0:32]), ohg.unsqueeze(2).to_broadcast([128, 4, 8]), oh1.unsqueeze(1).to_broadcast([128, 4, 8]),
                ALU.mult, rr, [ohr])
        self.TT(g3(oh[:, 32:64]), ohg.unsqueeze(2).to_broadcast([128, 4, 8]), oh2.unsqueeze(1).to_broadcast([128, 4, 8]),
                ALU.mult, rr, [ohr])
        self.MM([(ps[1][:, 0:64], self.triS, oh, True, True), (ps[1][:, 64:128], self.onesF, oh, True, True)],
                [ohr, "consts"], [("ps", 1)])
        t64 = rt[:, 160:224]
        self.TT(t64, ps[1][:, 0:64], self.carry[:], ALU.add, [("ps", 1), "carry"], rr)
        self.TT(t64, t64, oh, ALU.mult, rr + [ohr], rr)
        self.RSUM(self.rank[:, ti, :], t64.rearrange("p (k e) -> p k e", k=2), rr, ["rank"])
        self.TT(self.carry[:], self.carry[:], ps[1][:, 64:128], ALU.add, [("ps", 1), "carry"], ["carry"])

    def moe(self, l):
        ps = self.psf
        NB = self.NBLK
        I32 = mybir.dt.int32
        self.MEMSET(self.carry[:], 0.0, ["carry"])
        for ti in range(self.ntile):
            self.route_tile(l, ti)
        gm = self.gm
        gr = ["gm"]
        G = lambda a, b: gm[:, a:b]
        cnt1, cnt2, tot, nblk, pad, e0, e1, startp = G(0, 32), G(32, 64), G(64, 96), G(96, 128), G(128, 160), G(160, 192), G(192, 224), G(224, 256)
        basev = G(256, 320)
        self.CP(gm[:, 0:64], self.carry[:], ["carry"], gr)
        self.TT(tot, cnt1, cnt2, ALU.add, gr, gr)
        NJ = self.NJ
        cmp = self.fa[:, 0:32 * NJ].rearrange("p (e j) -> p e j", e=32)
        self.TT(cmp, tot.unsqueeze(2).to_broadcast([128, 32, NJ]), self.crow[:, 0:NJ].unsqueeze(1).to_broadcast([128, 32, NJ]),
                ALU.is_gt, gr + ["crow"], ["fa"])
        self.RSUM(nblk, cmp, ["fa"], gr)
        self.TS(pad, nblk, 128.0, None, ALU.mult, None, gr, gr)
        src, dst = pad, e0
        sh = 1
        while sh < 32:
            self.CP(dst[:, 0:sh], src[:, 0:sh], gr, gr)
            self.TT(dst[:, sh:32], src[:, sh:32], src[:, 0:32 - sh], ALU.add, gr, gr)
            src, dst = dst, (e1 if dst is e0 else e0)
            sh *= 2
        endp = src
        self.TT(startp, endp, pad, ALU.subtract, gr, gr)
        self.CP(basev[:, 0:32], startp, gr, gr)
        self.TT(basev[:, 32:64], startp, cnt1, ALU.add, gr, gr)
        CH = 22
        for b0 in range(0, NB, CH):
            nb = min(CH, NB - b0)
            cm2 = self.fa[:, 0:nb * 32].rearrange("p (b e) -> p b e", b=nb)
            self.TT(cm2, endp.unsqueeze(1).to_broadcast([128, nb, 32]),
                    self.crow[:, 32 + b0:32 + b0 + nb].unsqueeze(2).to_broadcast([128, nb, 32]), ALU.is_le, gr + ["crow"], ["fa"])
            self.RSUM(self.blke[:, b0:b0 + nb], cm2, ["fa"], ["blke"])
        self.TS(self.blke[:], self.blke[:], 31.0, 128.0, ALU.min, ALU.mult, ["blke"], ["blke"])
        self.TS(self.blke[:], self.blke[:], self.pidx[:, 0:1], float(l * 4096), ALU.add, ALU.add, ["blke", "pidx"], ["blke"])
        self.CP(self.widx[:], self.blke[:], ["blke"], ["widx"])
        if l == 0:
            self.CP(self.blke[:], self.widx[:], ["widx"], ["blke"])
            self.DMA(self.o_dbg2, self.blke[:], ["blke"], (), final=True)
            self.DMA(self.o_dbg3, gm[:], gr, (), final=True)
        for ti in range(self.ntile):
            t0 = ti * 128
            xr = ("xT", ti)
            oh = self.OHall[:, ti, :]
            t64 = self.rt[:, 160:224]
            self.TT(t64, oh, basev, ALU.mult, [("OH", ti)] + gr, ["rt"])
            df = self.rt[:, 224:226]
            self.RSUM(df, t64.rearrange("p (k e) -> p k e", k=2), ["rt"], ["rt"])
            self.TT(df, df, self.rank[:, ti, :], ALU.add, ["rt", "rank"], ["rt"])
            self.CP(self.desti[:, ti, :], df, ["rt"], [("desti", ti)])
            X2 = self.X2
            self.modulate(ti, 24, 32, X2, ["X2"])
            self.TR([(ps[2][:, c * 128:(c + 1) * 128], X2[:, c, :], self.identF) for c in range(4)], ["X2", "consts"], [("ps", 2)])
            self.TR([(ps[3][:, (c - 4) * 128:(c - 3) * 128], X2[:, c, :], self.identF) for c in range(4, 8)], ["X2", "consts"],
                    [("ps", 3)])
            xtok = self.xtok[ti % 2]
            xtr = ("xtok", ti % 2)
            self.CP(xtok[:, 0:512], ps[2][:], [("ps", 2)], [xtr], eng="act")
            self.CP(xtok[:, 512:1024], ps[3][:], [("ps", 3)], [xtr])
            for k in range(2):
                if self.stop_after == "route":
                    break
                off = self.desti[:, ti, k:k + 1]
                self.S.add("pool", lambda e, off=off, xtok=xtok: e.indirect_dma_start(
                    out=self.d_buf, out_offset=bass.IndirectOffsetOnAxis(ap=off, axis=0), in_=xtok[:, :], in_offset=None), [xtr, ("desti", ti), "bufz"], [("bufw", ti, k)], dma=True)
            xt = self.xT[:, :, t0:t0 + 128]
            self.TS(xt, xt, DN_ALPHA, None, ALU.mult, None, [xr, "X2"], [xr])
        if l == 0:
            dd = self.fb[:, 0:self.ntile * 2]
            self.CP(dd, self.desti[:].rearrange("p t k -> p (t k)"), [("desti", ti) for ti in range(self.ntile)], ["fb"])
            self.DMA(self.o_dbg1, dd, ["fb"], (), final=True)
        if self.stop_after in ("route", "scatter"):
            return
        bufw = [("bufw", ti, k) for ti in range(self.ntile) for k in range(2)]
        for b in range(NB):
            xb = self.xblk[b % 2]
            xbr = ("xblk", b % 2)
            self.DMA(xb[:], self.d_buf[b * 128:(b + 1) * 128, :], bufw, [xbr], q="pool")
            if self.stop_after == "blocks0":
                continue
            for h in range(2):
                self.TR([(self.psb[:, (h * 4 + c) * 128:(h * 4 + c + 1) * 128], xb[:, (h * 4 + c) * 128:(h * 4 + c + 1) * 128],
                          self.identB[:]) for c in range(4)], [xbr, "identB"], [("psb", 0), ("psb", 1)])
                if self.stop_after == "blocks1a":
                    continue
                if h == 0:
                    self.ACT(self.xbT[:, 0:4, :].rearrange("p c t -> p (c t)"), self.psb[:, 0:512], AF.Identity, [("psb", 0), ("psb", 1)], [("xbT", 0)])
                else:
                    self.CP(self.xbT[:, 4:8, :].rearrange("p c t -> p (c t)"), self.psb[:, 512:1024], [("psb", 0), ("psb", 1)], [("xbT", 1)])
            if self.stop_after in ("blocks1", "blocks1a"):
                continue
            off = self.widx[:, b:b + 1]
            wts = []
            for (dsrc, nrow) in ((self.d_wegp, 4096), (self.d_weup, 4096), (self.d_wedp, 4096)):
                i = self._wsi % len(self.ws)
                self._wsi += 1
                t = self.ws[i]
                self.S.add("pool", lambda e, off=off, t=t, dsrc=dsrc: e.indirect_dma_start(
                    out=t[:, :], out_offset=None, in_=dsrc, in_offset=bass.IndirectOffsetOnAxis(ap=off, axis=0)), ["widx"], [("ws", i)], dma=True)
                self._gch += 1
                wts.append((t, ("ws", i)))
            if self.stop_after == "blocks2":
                continue
            wg = wts[0][0][:, :].rearrange("p (k n) -> p k n", k=8)
            wu = wts[1][0][:, :].rearrange("p (k n) -> p k n", k=8)
            wd = wts[2][0][:, :].rearrange("p (k n) -> p k n", k=4)
            xbTr = [("xbT", 0), ("xbT", 1)]
            self.MM([(ps[0][:], self.xbT[:, k, :], wg[:, k, :], k == 0, k == 7) for k in range(8)], xbTr + [wts[0][1]], [("ps", 0)])
            self.MM([(ps[1][:], self.xbT[:, k, :], wu[:, k, :], k == 0, k == 7) for k in range(8)], xbTr + [wts[1][1]], [("ps", 1)])
            sgt = self.fb[:, 0:512]
            self.ACT(sgt, ps[0][:], AF.Silu, [("ps", 0)], ["fb"])
            self.TT(self.hidtok[:], sgt, ps[1][:], ALU.mult, ["fb", ("ps", 1)], ["hidtok"])
            self.TR([(self.psb[:, c * 128:(c + 1) * 128], self.hidtok[:, c * 128:(c + 1) * 128], self.identB[:]) for c in range(4)],
                    ["hidtok", "identB"], [("psb", 0), ("psb", 1)])
            self.ACT(self.hidT[:].rearrange("p c t -> p (c t)"), self.psb[:, 0:512], AF.Identity, [("psb", 0), ("psb", 1)], ["hidT"])
            self.MM([(ps[4][:], self.hidT[:, fc, :], wd[:, fc, 0:512], fc == 0, fc == 3) for fc in range(4)], ["hidT", wts[2][1]],
                    [("ps", 4)])
            self.MM([(ps[5][:], self.hidT[:, fc, :], wd[:, fc, 512:1024], fc == 0, fc == 3) for fc in range(4)], ["hidT", wts[2][1]],
                    [("ps", 5)])
            yb = self.yblk[b % 2]
            ybr = ("yblk", b % 2)
            self.CP(yb[:, 0:512], ps[4][:], [("ps", 4)], [ybr], eng="act")
            self.CP(yb[:, 512:1024], ps[5][:], [("ps", 5)], [ybr])
            self.DMA(self.d_ybuf[b * 128:(b + 1) * 128, :], yb[:], [ybr], [("ybuf", b)], q="pool")
        ybr_all = [("ybuf", b) for b in range(NB)]
        if self.stop_after in ("blocks", "blocks0", "blocks1", "blocks1a", "blocks2"):
            return
        for ti in range(self.ntile):
            t0 = ti * 128
            kindS = ti == self.npt
            xr = ("xT", ti)
            G1, G2 = self.fa, self.fc
            for k, (Gt, Gr) in enumerate(((G1, "fa"), (G2, "fc"))):
                off = self.desti[:, ti, k:k + 1]
                self.S.add("pool", lambda e, off=off, Gt=Gt: e.indirect_dma_start(
                    out=Gt[:, :], out_offset=None, in_=self.d_ybuf, in_offset=bass.IndirectOffsetOnAxis(ap=off, axis=0)), ybr_all + [("desti", ti)], [Gr], dma=True)
            self.TS(G1[:], G1[:], self.Wv[:, ti, 0:1], None, ALU.mult, None, ["fa", "Wv"], ["fa"])
            self.STT(G1[:], G2[:], self.Wv[:, ti, 1:2], G1[:], ALU.mult, ALU.add, ["fa", "fc", "Wv"], ["fa"])
            self.TR([(ps[2][:, c * 128:(c + 1) * 128], G1[:, c * 128:(c + 1) * 128], self.identF) for c in range(4)],
                    ["fa", "consts"], [("ps", 2)])
            self.TR([(ps[3][:, (c - 4) * 128:(c - 3) * 128], G1[:, c * 128:(c + 1) * 128], self.identF) for c in range(4, 8)],
                    ["fa", "consts"], [("ps", 3)])
            xt = self.xT[:, :, t0:t0 + 128]
            for h in range(2):
                pm, pmr = ps[2 + h], ("ps", 2 + h)
                if not kindS:
                    for mc in range(4):
                        c = h * 4 + mc
                        self.STT(xt[:, c, :], pm[:, mc * 128:(mc + 1) * 128], self.ada[:, 40 + c, 0:1], xt[:, c, :], ALU.mult, ALU.add,
                                 [pmr, "ada", xr], [xr])
                else:
                    g2b = self.ada[:, 40 + h * 4:40 + h * 4 + 4, 1:17].unsqueeze(3).to_broadcast([128, 4, 16, 8])
                    tmp = self.g4a[:].rearrange("p c (s j) -> p c s j", s=16)
                    self.TT(tmp, pm[:].rearrange("p (c s j) -> p c s j", c=4, s=16), g2b, ALU.mult, [pmr, "ada"], ["g4a"])
                    self.TT(xt[:, h * 4:h * 4 + 4, :], xt[:, h * 4:h * 4 + 4, :], self.g4a[:], ALU.add, ["g4a", xr], [xr])
            self.ln_feat(xt, 8, [xr], FV_L2G, FV_L2B, l, lambda c: xt[:, c, :], [xr])


def _feat_major(a2d):
    r, f = a2d.shape
    return np.ascontiguousarray(a2d.T.reshape(f // 128, 128, r).transpose(1, 0, 2))


def _fm_vec(v):
    f = v.shape[-1]
    return np.moveaxis(v.reshape(v.shape[:-1] + (f // 128, 128)), -1, 0)


_CACHE = {}


def kernel(x_prompt, x_sample, state_conv, state_pool, state_ssm_conv, state_ssm, c_prompt, c_sample,
           w_ada, b_ada, w_in, sgu_ln_g, sgu_ln_b, sgu_w, sgu_b, conv_w, conv_bias, conv_ln_g, conv_ln_b,
           pool_w, pool_scale, ssm_conv_w, ssm_conv_b, ssm_dt_bias, ssm_a_log, ssm_d, ssm_norm_g,
           w_br_a, w_br_b, w_br_c, w_br_d, w_o, ln1_g, ln1_b, router_g, router_e, w_e_gate, w_e_up,
           w_e_down, ln2_g, ln2_b, _stop_after=None):
    f = lambda a: np.asarray(a, dtype=np.float32)
    x_prompt, x_sample = f(x_prompt), f(x_sample)
    ncore = x_prompt.shape[0]
    D = w_in.shape[0]
    seq = x_prompt.shape[1]
    npt = seq // 128
    T = seq + 128
    nsb = x_sample.shape[0] // ncore
    assert nsb == 16 and x_sample.shape[1] == 8
    key = (D, npt, _stop_after)
    if key not in _CACHE:
        _CACHE[key] = Builder(D, npt, stop_after=_stop_after)
    bld = _CACHE[key]

    shared = {}
    shared["w_ada"] = f(w_ada)
    shared["b_adaT"] = np.ascontiguousarray(_fm_vec(f(b_ada)))
    shared["w_in"] = f(w_in)
    fv = np.zeros((128, D, NFV), np.float32)
    fv[:, :, FV_CONVB:FV_CONVB + 4] = _fm_vec(f(conv_bias))
    fv[:, :, FV_CLNG:FV_CLNG + 4] = _fm_vec(f(conv_ln_g))
    fv[:, :, FV_CLNB:FV_CLNB + 4] = _fm_vec(f(conv_ln_b))
    fv[:, :, FV_PSC:FV_PSC + 4] = _fm_vec(f(pool_scale))
    fv[:, :, FV_SCB:FV_SCB + 14] = _fm_vec(f(ssm_conv_b))
    fv[:, :, FV_L1G:FV_L1G + 8] = _fm_vec(f(ln1_g))
    fv[:, :, FV_L1B:FV_L1B + 8] = _fm_vec(f(ln1_b))
    fv[:, :, FV_L2G:FV_L2G + 8] = _fm_vec(f(ln2_g))
    fv[:, :, FV_L2B:FV_L2B + 8] = _fm_vec(f(ln2_b))
    cw = _fm_vec(f(conv_w))
    fv[:, :, FV_CW:FV_CW + 124] = cw.transpose(0, 1, 3, 2).reshape(128, D, 124)
    scw = _fm_vec(f(ssm_conv_w))
    fv[:, :, FV_SCW:FV_SCW + 56] = scw.transpose(0, 1, 3, 2).reshape(128, D, 56)
    fv[:, :, FV_NG:FV_NG + 6] = _fm_vec(f(ssm_norm_g))
    shared["fvec"] = fv
    rv = np.zeros((D, NRV), np.float32)
    rv[:, RV_SLG:RV_SLG + 512] = f(sgu_ln_g)
    rv[:, RV_SLB:RV_SLB + 512] = f(sgu_ln_b)
    rv[:, RV_DTB:RV_DTB + 12] = f(ssm_dt_bias)
    rv[:, RV_ALOG:RV_ALOG + 12] = f(ssm_a_log)
    rv[:, RV_D:RV_D + 12] = f(ssm_d)
    shared["rvec"] = rv
    sgb = np.zeros((D, 2, 512), np.float32)
    sgb[:, 0] = f(sgu_b).reshape(D, 512)
    sgb[:, 1] = np.tile(f(sgu_b)[:, :, :8], (1, 1, 16)).reshape(D, 512)
    shared["sgub"] = sgb
    sw = f(sgu_w)
    sws = np.zeros((D, 2, 128, 4, 128), np.float32)
    sws[:, 0] = sw.transpose(0, 3, 1, 2)
    blk = sw[:, :, :8, :8].transpose(0, 3, 1, 2)
    for i in range(16):
        sws[:, 1, i * 8:(i + 1) * 8, :, i * 8:(i + 1) * 8] = blk
    shared["sgu_wT"] = sws
    shared["pool_w"] = f(pool_w)
    for nm, a in (("w_br_a", w_br_a), ("w_br_b", w_br_b), ("w_br_c", w_br_c), ("w_br_d", w_br_d), ("w_o", w_o),
                  ):
        shared[nm] = f(a)
    shared["w_e_gate_p"] = np.ascontiguousarray(f(w_e_gate).reshape(D, 32, 8, 128, 512).transpose(0, 1, 3, 2, 4)).reshape(D * 4096, 4096)
    shared["w_e_up_p"] = np.ascontiguousarray(f(w_e_up).reshape(D, 32, 8, 128, 512).transpose(0, 1, 3, 2, 4)).reshape(D * 4096, 4096)
    shared["w_e_down_p"] = np.ascontiguousarray(f(w_e_down).reshape(D, 32, 4, 128, 1024).transpose(0, 1, 3, 2, 4)).reshape(D * 4096, 4096)
    crow = np.zeros((1, 32 + bld.NBLK), np.float32)
    crow[0, 0:32] = 128.0 * np.arange(32)
    crow[0, 32:] = 128.0 * np.arange(bld.NBLK)
    shared["crow"] = crow
    shared["pidx"] = np.arange(128, dtype=np.float32).reshape(128, 1)
    shared["router"] = np.ascontiguousarray(np.concatenate([f(router_g), f(router_e)], axis=2))
    idx = np.arange(128)
    same = (idx[:, None] // 8) == (idx[None, :] // 8)
    caus = idx[:, None] <= idx[None, :]
    cst = np.zeros((128, 8, 128), np.float32)
    cst[:, 0] = np.eye(128)
    cst[:, 1] = caus
    cst[:, 2] = caus & same
    cst[:, 3] = np.where(caus, 0.0, NEG)
    cst[:, 4] = np.where(caus & same, 0.0, NEG)
    cst[:, 5] = same
    cst[:, 6] = 1.0
    cst[:, 7] = idx[:, None] < idx[None, :]
    shared["consts"] = cst
    shared["m3"] = ((idx[:, None] // 8) == np.arange(16)[None, :]).astype(np.float32)
    corr = np.zeros((128, 4, 15), np.float32)
    for gi in range(4):
        w = 2 ** (gi + 1)
        corr[:, gi, :] = w / np.minimum(w, np.arange(15) + 1.0)
    shared["corr"] = corr

    in_maps = []
    for c in range(ncore):
        m = dict(shared)
        xs = x_sample[c * 16:(c + 1) * 16].reshape(128, 1024)
        m["xT_in"] = _feat_major(np.concatenate([x_prompt[c], xs], axis=0))
        m["cT_in"] = _feat_major(np.concatenate([f(c_prompt)[c:c + 1], f(c_sample)[c * 16:(c + 1) * 16]], axis=0))
        sl = slice(c * 16, (c + 1) * 16)
        def st(a):
            a = f(a)[:, sl]
            d, s, r, cc = a.shape
            return np.ascontiguousarray(a.reshape(d, s, r, cc // 128, 128).transpose(0, 4, 3, 1, 2))
        m["conv_sT"] = st(state_conv)
        m["pool_sT"] = st(state_pool)
        m["sconv_sT"] = st(state_ssm_conv)
        ss = f(state_ssm)[:, sl]
        m["ssm_sT"] = np.ascontiguousarray(ss.transpose(0, 4, 1, 2, 3).reshape(D, 128, 16, 768))
        in_maps.append(m)

    res = run_bass_kernel_spmd(bld.nc, in_maps, core_ids=list(range(ncore)))
    R = res.results
    kernel._last = R

    def unfm(a):
        a = np.asarray(a)
        nch = a.shape[1]
        return np.moveaxis(a, (0, 1), (-1, -2)).reshape(a.shape[2:] + (nch * 128,))

    y_prompt = np.stack([unfm(R[c]["yT"])[:seq] for c in range(ncore)])
    y_sample = np.concatenate([unfm(R[c]["yT"])[seq:].reshape(16, 8, 1024) for c in range(ncore)])
    def pst(name):
        return np.stack([np.stack([unfm(R[c][name][l]) for l in range(D)]) for c in range(ncore)], axis=1)
    p_conv = pst("o_conv_p")
    p_pool = pst("o_pool_p")
    p_sconv = pst("o_sconv_p")
    p_ssm = np.stack([np.asarray(R[c]["o_ssm_p"]).reshape(D, 128, 12, 64).transpose(0, 2, 3, 1) for c in range(ncore)], axis=1)
    def sst(name):
        return np.concatenate([np.stack([unfm(R[c][name][l]) for l in range(D)]) for c in range(ncore)], axis=1)
    s_conv = sst("o_conv_s")
    s_pool = sst("o_pool_s")
    s_sconv = sst("o_sconv_s")
    s_ssm = np.concatenate([np.asarray(R[c]["o_ssm_s"]).reshape(D, 128, 16, 12, 64).transpose(0, 2, 3, 4, 1)
                            for c in range(ncore)], axis=1)
    s_v = np.concatenate([np.asarray(R[c]["o_v_s"]).reshape(D, 16, 8, 512) for c in range(ncore)], axis=1)
    outs = (y_prompt, y_sample, p_conv, p_pool, p_sconv, p_ssm, s_conv, s_pool, s_sconv, s_ssm, s_v)
    return tuple(np.ascontiguousarray(o, dtype=np.float32) for o in outs)
```

```python
import numpy as np
import concourse.bass as bass
import concourse.mybir as mybir
from concourse.bass_utils import run_bass_kernel_spmd

F32 = mybir.dt.float32
BF16 = mybir.dt.bfloat16
AF = mybir.ActivationFunctionType
ALU = mybir.AluOpType
AX = mybir.AxisListType

ENGINES = ("pe", "act", "dve", "pool", "sp")
NDMA = 8
NWS = 5
DN_ALPHA = 8.0 ** 0.25
LN_EPS = 1e-5
NEG = -30000.0

C_U, C_V, C_GA, C_GG, C_PIN, C_ZZ, C_XBC, C_DT, C_GL = 0, 512, 1024, 1536, 2048, 2560, 3328, 5120, 5132
FV_CONVB, FV_CLNG, FV_CLNB, FV_PSC, FV_SCB, FV_L1G, FV_L1B, FV_L2G, FV_L2B, FV_CW, FV_SCW = 0, 4, 8, 12, 16, 30, 38, 46, 54, 62, 186
FV_NG = 242
NFV = 248
RV_SLG, RV_SLB, RV_DTB, RV_ALOG, RV_D = 0, 512, 1024, 1036, 1048
NRV = 1060


class Sched:
    def __init__(self, nc):
        self.nc = nc
        self.ops = {e: [] for e in ENGINES}
        self.count = {e: 0 for e in ENGINES}
        self.dmaj = {e: 0 for e in ENGINES}
        self.last_writer = {}
        self.readers = {}
        self.known = {e: {} for e in ENGINES}
        self.out_tokens = []

    def add(self, engine, fn, reads=(), writes=(), dma=False, out=False):
        deps = []
        for r in reads:
            t = self.last_writer.get(r)
            if t is not None:
                deps.append(t)
        for w in writes:
            t = self.last_writer.get(w)
            if t is not None:
                deps.append(t)
            deps.extend(self.readers.get(w, ()))
        if dma:
            j = self.dmaj[engine]
            s = j % NDMA
            key = ("dma", engine, s)
            tok = (key, 16 * (j // NDMA + 1))
            if j >= NDMA:
                deps.append((key, 16 * (j // NDMA)))
            self.dmaj[engine] = j + 1
        else:
            self.count[engine] += 1
            tok = (("eng", engine), self.count[engine])
        best = {}
        for k, v in deps:
            if best.get(k, 0) < v:
                best[k] = v
        waits = []
        kn = self.known[engine]
        for k, v in best.items():
            if kn.get(k, 0) < v:
                waits.append((k, v))
                kn[k] = v
        self.ops[engine].append((waits, fn, tok, dma))
        for r in reads:
            self.readers.setdefault(r, []).append(tok)
        for w in writes:
            self.last_writer[w] = tok
            self.readers[w] = []
        if out:
            self.out_tokens.append(tok)
        return tok

    def emit(self):
        nc = self.nc
        best = {}
        for k, v in self.out_tokens:
            if best.get(k, 0) < v:
                best[k] = v
        for e in ENGINES:
            if self.count[e]:
                best[("eng", e)] = self.count[e]
            for j in range(max(0, self.dmaj[e] - NDMA), self.dmaj[e]):
                k = ("dma", e, j % NDMA)
                best[k] = max(best.get(k, 0), 16 * (j // NDMA + 1))
        final_waits = list(best.items())
        keys = set()
        for e in ENGINES:
            for waits, fn, tok, dma in self.ops[e]:
                keys.add(tok[0])
        sems = {}
        for k in sorted(keys, key=str):
            sems[k] = nc.alloc_semaphore(name="s_" + "_".join(str(x) for x in k))
        attr = {"pe": "tensor", "act": "scalar", "dve": "vector", "pool": "gpsimd", "sp": "sync"}
        ops = self.ops
        with nc.Block() as block:
            def mk(e):
                def body(eng):
                    for waits, fn, tok, dma in ops[e]:
                        for k, v in waits:
                            eng.wait_ge(sems[k], v)
                        ins = fn(eng)
                        ins.then_inc(sems[tok[0]], 16 if dma else 1)
                    if e == "sp":
                        for k, v in final_waits:
                            eng.wait_ge(sems[k], v)
                return body
            for e in ENGINES:
                if ops[e] or e == "sp":
                    getattr(block, attr[e])(mk(e))


class Builder:
    def __init__(self, depth, npt, stop_after=None):
        self.depth = depth
        self.npt = npt
        self.T = npt * 128 + 128
        self.ntile = npt + 1
        self.stop_after = stop_after
        nc = bass.Bass("TRN2", target_bir_lowering=False)
        self.nc = nc
        self.S = Sched(nc)
        self._wsi = 0
        self._gch = 0
        self.NBLK = -(-(2 * self.T + 32 * 127) // 128)
        self.NJ = self.ntile
        self.declare()
        self.program()
        self.S.emit()

    def TT(self, out, in0, in1, op, r, w, eng="dve"):
        self.S.add(eng, lambda e: e.tensor_tensor(out=out, in0=in0, in1=in1, op=op), r, w)

    def TS(self, out, in0, s1, s2, op0, op1, r, w, eng="dve"):
        if s2 is None:
            self.S.add(eng, lambda e: e.tensor_scalar(out=out, in0=in0, scalar1=s1, scalar2=None, op0=op0), r, w)
        else:
            self.S.add(eng, lambda e: e.tensor_scalar(out=out, in0=in0, scalar1=s1, scalar2=s2, op0=op0, op1=op1), r, w)

    def STT(self, out, in0, scalar, in1, op0, op1, r, w, eng="dve"):
        self.S.add(eng, lambda e: e.scalar_tensor_tensor(out=out, in0=in0, scalar=scalar, in1=in1, op0=op0, op1=op1), r, w)

    def CP(self, out, in_, r, w, eng="dve"):
        if eng == "act":
            self.S.add("act", lambda e: e.copy(out=out, in_=in_), r, w)
        else:
            self.S.add(eng, lambda e: e.tensor_copy(out=out, in_=in_), r, w)

    def ACT(self, out, in_, func, r, w, scale=None, bias=None):
        kw = {}
        if scale is not None:
            kw["scale"] = scale
        if bias is not None:
            kw["bias"] = bias
        self.S.add("act", lambda e: e.activation(out=out, in_=in_, func=func, **kw), r, w)

    def SQRT(self, ap, res):
        self.S.add("act", lambda e: e.sqrt(out=ap, in_=ap), [res], [res])

    def MM(self, mms, r, w):
        def fn(e):
            ins = None
            for (o, l, rh, st, sp) in mms:
                ins = e.matmul(o, lhsT=l, rhs=rh, start=st, stop=sp)
            return ins
        self.S.add("pe", fn, r, w)

    def TR(self, trs, r, w):
        def fn(e):
            ins = None
            for (o, i, idn) in trs:
                ins = e.transpose(o, i, idn)
            return ins
        self.S.add("pe", fn, r, w)

    def DMA(self, out, in_, r, w, q="sp", final=False):
        self.S.add(q, lambda e: e.dma_start(out=out, in_=in_), r, w, dma=True, out=final)

    def DMA4(self, out, in_, r, w, final=False):
        for c in range(out.shape[1]):
            self.DMA(out[:, c], in_[:, c], r, w, final=final)

    def MEMSET(self, ap, val, w, eng="dve"):
        self.S.add(eng, lambda e: e.memset(ap, val), (), w)

    def RSUM(self, out, in_, r, w):
        self.S.add("dve", lambda e: e.reduce_sum(out=out, in_=in_, axis=AX.X), r, w)

    def RMAX(self, out, in_, r, w):
        self.S.add("dve", lambda e: e.reduce_max(out=out, in_=in_, axis=AX.X), r, w)

    def RECIP(self, out, in_, r, w):
        self.S.add("dve", lambda e: e.reciprocal(out=out, in_=in_), r, w)

    def wload(self, src, kc, ncols):
        i = self._wsi % len(self.ws)
        self._wsi += 1
        t = self.ws[i]
        view = t[:, 0:kc * ncols].rearrange("p (k n) -> p k n", k=kc)
        self.DMA(view, src.rearrange("(k p) n -> p k n", p=128), (), [("ws", i)], q="pool")
        return view, ("ws", i)

    def declare(self):
        nc, D, T = self.nc, self.depth, self.T
        di = lambda n, s: nc.dram_tensor(n, s, F32, kind="ExternalInput").ap()
        do = lambda n, s: nc.dram_tensor(n, s, F32, kind="ExternalOutput").ap()
        self.d_xT = di("xT_in", [128, 8, T])
        self.d_cT = di("cT_in", [128, 8, 17])
        self.d_wada = di("w_ada", [D, 1024, 6144])
        self.d_badaT = di("b_adaT", [128, D, 48])
        self.d_win = di("w_in", [D, 1024, 9228])
        self.d_fvec = di("fvec", [128, D, NFV])
        self.d_rvec = di("rvec", [D, NRV])
        self.d_sgub = di("sgub", [D, 2, 512])
        self.d_sguw = di("sgu_wT", [D, 2, 128, 4, 128])
        self.d_poolw = di("pool_w", [D, 4, 128, 128])
        self.d_wbr = [di("w_br_a", [D, 512, 1024]), di("w_br_b", [D, 512, 1024]), di("w_br_c", [D, 512, 1024]),
                      di("w_br_d", [D, 768, 1024])]
        self.d_wo = di("w_o", [D, 1024, 1024])
        self.d_router = di("router", [D, 1024, 36])
        self.d_wegp = di("w_e_gate_p", [D * 4096, 4096])
        self.d_weup = di("w_e_up_p", [D * 4096, 4096])
        self.d_wedp = di("w_e_down_p", [D * 4096, 4096])
        self.d_crow = di("crow", [1, 32 + self.NBLK])
        self.d_pidx = di("pidx", [128, 1])
        self.d_buf = nc.dram_tensor("moe_buf", [self.NBLK * 128, 1024], BF16, kind="Internal").ap()
        self.d_ybuf = nc.dram_tensor("moe_ybuf", [self.NBLK * 128, 1024], F32, kind="Internal").ap()
        self.d_convS = di("conv_sT", [D, 128, 4, 16, 30])
        self.d_poolS = di("pool_sT", [D, 128, 4, 16, 15])
        self.d_sconvS = di("sconv_sT", [D, 128, 14, 16, 3])
        self.d_ssmS = di("ssm_sT", [D, 128, 16, 768])
        self.d_consts = di("consts", [128, 8, 128])
        self.d_m3 = di("m3", [128, 16])
        self.d_corr = di("corr", [128, 4, 15])
        self.o_yT = do("yT", [128, 8, T])
        self.o_convP = do("o_conv_p", [D, 128, 4, 30])
        self.o_poolP = do("o_pool_p", [D, 128, 4, 15])
        self.o_sconvP = do("o_sconv_p", [D, 128, 14, 3])
        self.o_ssmP = do("o_ssm_p", [D, 128, 768])
        self.o_convS = do("o_conv_s", [D, 128, 4, 16, 30])
        self.o_poolS = do("o_pool_s", [D, 128, 4, 16, 15])
        self.o_sconvS = do("o_sconv_s", [D, 128, 14, 16, 3])
        self.o_ssmS = do("o_ssm_s", [D, 128, 16, 768])
        self.o_vS = do("o_v_s", [D, 128, 512])
        self.o_dbg1 = do("dbg_desti", [128, self.ntile * 2])
        self.o_dbg2 = do("dbg_widx", [128, self.NBLK])
        self.o_dbg3 = do("dbg_gm", [128, 320])

        sb = lambda n, s, dt=F32: nc.alloc_sbuf_tensor(n, s, dt)
        self.xT = sb("xT", [128, 8, T])
        self.ws = [sb(f"ws{i}", [128, 4096], BF16) for i in range(NWS)]
        self.ada = sb("ada", [128, 48, 17])
        self.cTf = sb("cTf", [128, 8, 17])
        self.scT = sb("scT", [128, 8, 17], BF16)
        self.badaT = sb("badaT", [128, D, 48])
        self.fvec = sb("fvecs", [128, D, NFV])
        self.rvec = sb("rvecs", [128, NRV])
        self.sgub = sb("sgubs", [128, 512])
        self.consts = sb("constss", [128, 8, 128])
        self.identB = sb("identB", [128, 128], BF16)
        self.m3 = sb("m3s", [128, 16])
        self.corr = sb("corrs", [128, 4, 15])
        self.sguwB = sb("sguwB", [128, 2, 4, 128], BF16)
        self.poolw = sb("poolws", [128, 4, 128], BF16)
        self.routerw = sb("routerws", [128, 8, 36])
        self.arow = sb("arow", [128, 12])
        self.convc = sb("convc", [128, 4, 30])
        self.poolc = sb("poolc", [128, 4, 15])
        self.sconvc = sb("sconvc", [128, 14, 3])
        self.state = sb("state", [128, 768])
        self.stateB = sb("stateB", [128, 768], BF16)
        self.X2 = sb("X2", [128, 8, 128])
        self.sm = sb("sm", [128, 128])
        self.st1 = sb("st1", [128, 128])
        self.st2 = sb("st2", [128, 128])
        self.st3 = sb("st3", [128, 128])
        self.xm = sb("xm", [128, 8, 128], BF16)
        self.merged = sb("merged", [128, 8, 128])
        self.mergedB = sb("mergedB", [128, 8, 128], BF16)
        self.g4a = sb("g4a", [128, 4, 128])
        self.g4b = sb("g4b", [128, 4, 128])
        self.fa = sb("fa", [128, 1024])
        self.fb = sb("fb", [128, 1024])
        self.fc = sb("fc", [128, 1024])
        self.yk = sb("yk", [128, 6, 128], BF16)
        self.hb1 = sb("hb1", [128, 768], BF16)
        self.hb2 = sb("hb2", [128, 768], BF16)
        self.ext = sb("ext", [128, 4, 158])
        self.pext = sb("pext", [128, 4, 143])
        self.ptA = sb("ptA", [128, 143])
        self.ptB = sb("ptB", [128, 143])
        self.xext = [sb(f"xext{i}", [128, 4, 131]) for i in range(2)]
        self.xc = sb("xc", [128, 6, 128])
        self.bc = sb("bc", [128, 8, 128], BF16)
        self.btok = sb("btok", [128, 4, 128], BF16)
        self.dec = sb("dec", [128, 12, 128], BF16)
        self.cmT = sb("cmT", [128, 4, 128], BF16)
        NT, NBk = self.ntile, self.NBLK
        self.rt = sb("rt", [128, 256])
        self.OHall = sb("OHall", [128, NT, 64])
        self.Wv = sb("Wv", [128, NT, 2])
        self.rank = sb("rank", [128, NT, 2])
        self.desti = sb("desti", [128, NT, 2], mybir.dt.int32)
        self.carry = sb("carry", [128, 64])
        self.gm = sb("gm", [128, 320])
        self.blke = sb("blke", [128, NBk])
        self.widx = sb("widx", [128, NBk], mybir.dt.int32)
        self.crow = sb("crows", [128, 32 + NBk])
        self.pidx = sb("pidxs", [128, 1])
        self.xblk = [sb(f"xblk{i}", [128, 1024], BF16) for i in range(2)]
        self.xtok = self.xblk
        self.xbT = sb("xbT", [128, 8, 128], BF16)
        self.hidtok = sb("hidtok", [128, 512], BF16)
        self.hidT = sb("hidT", [128, 4, 128], BF16)
        self.yblk = [self.fa, self.fc]
        self.psf = [nc.alloc_psum_tensor(f"psf{i}", [128, 512], F32) for i in range(7)]
        self.psb = nc.alloc_psum_tensor("psb", [128, 1024], BF16)

    def program(self):
        D = self.depth
        cs = self.consts
        self.identF, self.maskP, self.maskS = cs[:, 0, :], cs[:, 1, :], cs[:, 2, :]
        self.negP, self.negS, self.bdS, self.onesF = cs[:, 3, :], cs[:, 4, :], cs[:, 5, :], cs[:, 6, :]
        self.triS = cs[:, 7, :]
        self.DMA(self.crow[:], self.d_crow.partition_broadcast(128), (), ["crow"])
        self.DMA(self.pidx[:], self.d_pidx, (), ["pidx"])
        self.MEMSET(self.xblk[0][:], 0.0, [("xblk", 0)])
        for b in range(self.NBLK):
            self.DMA(self.d_buf[b * 128:(b + 1) * 128, :], self.xblk[0][:], [("xblk", 0)], ["bufz"])
        self.xres = [("xT", t) for t in range(self.ntile)]
        self.DMA(self.xT[:], self.d_xT, (), self.xres)
        self.DMA(self.cTf[:], self.d_cT, (), ["cTf"])
        self.DMA(self.consts[:], self.d_consts, (), ["consts"])
        self.DMA(self.m3[:], self.d_m3, (), ["m3"])
        self.DMA(self.corr[:], self.d_corr, (), ["corr"])
        self.DMA(self.badaT[:], self.d_badaT, (), ["badaT"])
        self.DMA(self.fvec[:], self.d_fvec, (), ["fvec"])
        self.CP(self.identB[:], self.identF, ["consts"], ["identB"])
        self.ACT(self.scT[:], self.cTf[:], AF.Silu, ["cTf"], ["scT"])
        for l in range(D):
            self.layer(l)
        self.DMA(self.o_yT, self.xT[:], self.xres, (), final=True)

    def fv(self, l, off, n=1):
        return self.fvec[:, l, off:off + n]

    def layer(self, l):
        self.DMA(self.rvec[:], self.d_rvec[l:l + 1, :].partition_broadcast(128), (), ["rvec"])
        self.DMA(self.sgub[:], self.d_sgub[l, 0:1, :].partition_broadcast(128), (), ["sgub"])
        self.DMA(self.poolw[:], self.d_poolw[l].rearrange("g c d -> c g d"), (), ["poolw"], q="pool")
        self.DMA(self.routerw[:], self.d_router[l].rearrange("(k p) n -> p k n", p=128), (), ["routerw"])
        for v in range(2):
            stg = self.X2[:, 0:4, :]
            self.DMA(stg, self.d_sguw[l, v], (), ["X2"])
            m = self.maskP if v == 0 else self.maskS
            self.TT(self.sguwB[:, v], stg, m.unsqueeze(1).to_broadcast([128, 4, 128]), ALU.mult,
                    ["X2", "consts"], ["sguwB"])
        self.ACT(self.arow[:], self.rvec[:, RV_ALOG:RV_ALOG + 12], AF.Exp, ["rvec"], ["arow"])
        self.TS(self.arow[:], self.arow[:], -1.0, None, ALU.mult, None, ["arow"], ["arow"])
        for fb in range(12):
            wv, wr = self.wload(self.d_wada[l][:, fb * 512:(fb + 1) * 512], 8, 512)
            ps = self.psf[fb % 2]
            mms = []
            for j in range(4):
                for dc in range(8):
                    mms.append((ps[:, j * 32:j * 32 + 17], wv[:, dc, j * 128:(j + 1) * 128], self.scT[:, dc, :],
                                dc == 0, dc == 7))
            self.MM(mms, [wr, "scT"], [("ps", fb % 2)])
            for j in range(4):
                fcx = fb * 4 + j
                self.ACT(self.ada[:, fcx, :], ps[:, j * 32:j * 32 + 17], AF.Identity, [("ps", fb % 2), "badaT"],
                         ["ada"], bias=self.badaT[:, l, fcx:fcx + 1])
        self.TS(self.ada[:, 8:16, :], self.ada[:, 8:16, :], 1.0, None, ALU.add, None, ["ada"], ["ada"])
        self.TS(self.ada[:, 32:40, :], self.ada[:, 32:40, :], 1.0, None, ALU.add, None, ["ada"], ["ada"])
        self.MEMSET(self.convc[:], 0.0, ["convc"])
        self.MEMSET(self.poolc[:], 0.0, ["poolc"])
        self.MEMSET(self.sconvc[:], 0.0, ["sconvc"])
        self.MEMSET(self.state[:], 0.0, ["state"])
        self.MEMSET(self.stateB[:], 0.0, ["stateB"])
        self.MEMSET(self.carry[:], 0.0, ["carry"])
        for ti in range(self.ntile):
            if ti == self.npt:
                self.DMA(self.sgub[:], self.d_sgub[l, 1:2, :].partition_broadcast(128), (), ["sgub"])
            self.mix_tile(l, ti)
            if self.stop_after != "mix":
                self.route_tile(l, ti)
        if self.stop_after == "mix":
            return
        self.moe(l)

    def modulate(self, ti, sh0, sc0, out, wr):
        t0 = ti * 128
        kindS = ti == self.npt
        xr = ("xT", ti)
        for c in range(8):
            xin = self.xT[:, c, t0:t0 + 128]
            if not kindS:
                self.ACT(out[:, c, :], xin, AF.Identity, [xr, "ada"], wr, scale=self.ada[:, sc0 + c, 0:1],
                         bias=self.ada[:, sh0 + c, 0:1])
            else:
                v3 = lambda a: a.rearrange("p (s j) -> p s j", s=16)
                scb = self.ada[:, sc0 + c, 1:17].unsqueeze(2).to_broadcast([128, 16, 8])
                shb = self.ada[:, sh0 + c, 1:17].unsqueeze(2).to_broadcast([128, 16, 8])
                self.TT(v3(self.st1[:]), v3(xin), scb, ALU.mult, [xr, "ada"], ["st1"])
                self.TT(v3(out[:, c, :]), v3(self.st1[:]), shb, ALU.add, ["st1", "ada"], wr)

    def ln_feat(self, src, nch, rd, g_off, b_off, l, dst_fn, dst_res, func=AF.Identity):
        n = nch * 128
        sq = self.X2[:, 0:nch, :]
        self.ACT(sq, src, AF.Square, rd, ["X2"])
        p1, p2 = self.psf[5], self.psf[6]
        self.MM([(p1[:, 0:128], self.onesF, src[:, c, :], c == 0, c == nch - 1) for c in range(nch)], rd + ["consts"],
                [("ps", 5)])
        self.MM([(p2[:, 0:128], self.onesF, sq[:, c, :], c == 0, c == nch - 1) for c in range(nch)], ["X2", "consts"],
                [("ps", 6)])
        mean, rstd, tmp = self.st1[:], self.st2[:], self.st3[:]
        self.TS(mean, p1[:, 0:128], 1.0 / n, None, ALU.mult, None, [("ps", 5)], ["st1"])
        self.TT(tmp, mean, mean, ALU.mult, ["st1"], ["st3"])
        self.STT(rstd, p2[:, 0:128], 1.0 / n, tmp, ALU.mult, ALU.subtract, [("ps", 6), "st3"], ["st2"])
        self.TS(rstd, rstd, LN_EPS, None, ALU.add, None, ["st2"], ["st2"])
        self.SQRT(rstd, "st2")
        self.RECIP(rstd, rstd, ["st2"], ["st2"])
        self.TT(src, src, mean.unsqueeze(1).to_broadcast([128, nch, 128]), ALU.subtract, rd + ["st1"], rd)
        self.TT(src, src, rstd.unsqueeze(1).to_broadcast([128, nch, 128]), ALU.mult, rd + ["st2"], rd)
        for c in range(nch):
            self.ACT(dst_fn(c), src[:, c, :], func, rd + ["fvec"], dst_res, scale=self.fv(l, g_off + c),
                     bias=self.fv(l, b_off + c))

    def merge_branch(self, l, k, ykT, kc, rd):
        for h in range(2):
            wg, wgr = self.wload(self.d_win[l][:, C_GL + k * 1024 + h * 512: C_GL + k * 1024 + (h + 1) * 512], 8, 512)
            pg = self.psf[2]
            mms = []
            for mc in range(4):
                for dc in range(8):
                    mms.append((pg[:, mc * 128:(mc + 1) * 128], wg[:, dc, mc * 128:(mc + 1) * 128], self.xm[:, dc, :],
                                dc == 0, dc == 7))
            self.MM(mms, [wgr, "xm"], [("ps", 2)])
            sig = self.fa[:, 0:512]
            self.ACT(sig, pg[:], AF.Sigmoid, [("ps", 2)], ["fa"])
            wb, wbr = self.wload(self.d_wbr[k][l][:, h * 512:(h + 1) * 512], kc, 512)
            pb = self.psf[3]
            mms = []
            for mc in range(4):
                for c in range(kc):
                    mms.append((pb[:, mc * 128:(mc + 1) * 128], wb[:, c, mc * 128:(mc + 1) * 128], ykT[:, c, :],
                                c == 0, c == kc - 1))
            self.MM(mms, [wbr] + rd, [("ps", 3)])
            mg = self.merged[:, h * 4:(h + 1) * 4, :].rearrange("p c t -> p (c t)")
            if k == 0:
                self.TT(mg, sig, pb[:], ALU.mult, ["fa", ("ps", 3)], [("merged", h)])
            else:
                self.TT(sig, sig, pb[:], ALU.mult, ["fa", ("ps", 3)], ["fa"])
                self.TT(mg, mg, sig, ALU.add, ["fa", ("merged", h)], [("merged", h)])

    def mix_tile(self, l, ti):
        kindS = ti == self.npt
        lastP = ti == self.npt - 1
        t0 = ti * 128
        L = 8 if kindS else 128
        xr = ("xT", ti)
        ps = self.psf
        xm = self.xm
        sm = self.sm
        yk = self.yk
        h3 = lambda a: a.rearrange("p (h d) -> p h d", h=12)
        c4 = lambda a, c=4: a.rearrange("p (c t) -> p c t", c=c)
        self.modulate(ti, 0, 8, xm, ["xm"])

        def proj_feat(col0, ncols, pst, psr):
            wv, wr = self.wload(self.d_win[l][:, col0:col0 + ncols], 8, ncols)
            mms = []
            for fc in range(ncols // 128):
                for dc in range(8):
                    mms.append((pst[:, fc * 128:(fc + 1) * 128], wv[:, dc, fc * 128:(fc + 1) * 128], xm[:, dc, :], dc == 0,
                                dc == 7))
            self.MM(mms, [wr, "xm"], [psr])

        def proj_tok(col0, ncols, pst, psr):
            wv, wr = self.wload(self.d_win[l][:, col0:col0 + ncols], 8, ncols)
            self.MM([(pst[:, 0:ncols], xm[:, dc, :], wv[:, dc, :], dc == 0, dc == 7) for dc in range(8)], [wr, "xm"], [psr])

        def seq_groups(per):
            return [(0, 1)] if not kindS else [(s, per) for s in range(0, 16, per)]

        proj_feat(C_U, 512, ps[0], ("ps", 0))
        gu = self.g4a[:].rearrange("p c t -> p (c t)")
        self.ACT(gu, ps[0][:], AF.Gelu_apprx_tanh, [("ps", 0)], ["g4a"])
        proj_tok(C_V, 512, ps[1], ("ps", 1))
        vg = self.fa[:, 0:512]
        self.ACT(vg, ps[1][:], AF.Gelu_apprx_tanh, [("ps", 1)], ["fa"])
        self.RSUM(sm[:, 0:1], vg, ["fa"], ["sm"])
        self.TT(self.fb[:, 0:512], vg, vg, ALU.mult, ["fa"], ["fb"])
        self.RSUM(sm[:, 1:2], self.fb[:, 0:512], ["fb"], ["sm"])
        self.TS(sm[:, 0:1], sm[:, 0:1], 1.0 / 512, None, ALU.mult, None, ["sm"], ["sm"])
        self.TT(sm[:, 2:3], sm[:, 0:1], sm[:, 0:1], ALU.mult, ["sm"], ["sm"])
        self.STT(sm[:, 3:4], sm[:, 1:2], 1.0 / 512, sm[:, 2:3], ALU.mult, ALU.subtract, ["sm"], ["sm"])
        self.TS(sm[:, 3:4], sm[:, 3:4], LN_EPS, None, ALU.add, None, ["sm"], ["sm"])
        self.SQRT(sm[:, 3:4], "sm")
        self.RECIP(sm[:, 3:4], sm[:, 3:4], ["sm"], ["sm"])
        self.TS(vg, vg, sm[:, 0:1], sm[:, 3:4], ALU.subtract, ALU.mult, ["fa", "sm"], ["fa"])
        self.TT(vg, vg, self.rvec[:, RV_SLG:RV_SLG + 512], ALU.mult, ["fa", "rvec"], ["fa"])
        self.TT(vg, vg, self.rvec[:, RV_SLB:RV_SLB + 512], ALU.add, ["fa", "rvec"], ["fa"])
        if kindS:
            self.DMA(self.o_vS[l], vg, ["fa"], (), final=True)
        vnb = self.hb1[:, 0:512]
        self.CP(vnb, vg, ["fa"], ["hb1"], eng="act")
        sw = self.sguwB[:, 1 if kindS else 0]
        self.MM([(ps[0][:, g * 128:(g + 1) * 128], vnb[:, g * 128:(g + 1) * 128], sw[:, g, :], True, True) for g in range(4)],
                ["hb1", "sguwB"], [("ps", 0)])
        self.TT(self.fb[:, 0:512], ps[0][:], self.sgub[:], ALU.add, [("ps", 0), "sgub"], ["fb"])
        self.TT(yk[:, 0:4, :].rearrange("p c t -> p (c t)"), self.fb[:, 0:512], gu, ALU.mult, ["fb", "g4a"], ["yk"])
        self.merge_branch(l, 0, yk, 4, ["yk"])

        proj_feat(C_GA, 512, ps[0], ("ps", 0))
        proj_feat(C_GG, 512, ps[1], ("ps", 1))
        self.ACT(self.g4a[:].rearrange("p c t -> p (c t)"), ps[1][:], AF.Sigmoid, [("ps", 1)], ["g4a"])
        E = 30 + L
        acc = self.g4b
        accr = [("g4b", c) for c in range(4)]
        for (s0, ns) in seq_groups(4):
            a0, n = s0 * L, ns * L
            ext = self.ext[:, :, 0:ns * E].rearrange("p c (s e) -> p c s e", s=ns)
            sv = lambda a: a[:, :, a0:a0 + n].rearrange("p c (s j) -> p c s j", s=ns)
            if kindS:
                self.DMA4(ext[:, :, :, 0:30], self.d_convS[l][:, :, s0:s0 + ns, :], (), ["ext"])
            else:
                self.CP(ext[:, :, 0, 0:30], self.convc[:], ["convc"], ["ext"])
            self.TT(ext[:, :, :, 30:E], sv(c4(ps[0][:])), sv(self.g4a[:]), ALU.mult, [("ps", 0), "g4a"], ["ext"])
            if kindS:
                self.DMA4(self.o_convS[l][:, :, s0:s0 + ns, :], ext[:, :, :, L:E], ["ext"], (), final=True)
            else:
                self.CP(self.convc[:], ext[:, :, 0, L:E], ["ext"], ["convc"])
                if lastP:
                    self.DMA(self.o_convP[l], self.convc[:], ["convc"], (), final=True)
            for k in range(31):
                for c in range(4):
                    a3 = acc[:, c, a0:a0 + n].rearrange("p (s j) -> p s j", s=ns)
                    wk = self.fv(l, FV_CW + c * 31 + k)
                    if k == 0:
                        self.TS(a3, ext[:, c, :, 0:L], wk, self.fv(l, FV_CONVB + c), ALU.mult, ALU.add, ["ext", "fvec"],
                                [("g4b", c)])
                    else:
                        self.STT(a3, ext[:, c, :, k:k + L], wk, a3, ALU.mult, ALU.add, ["ext", "fvec", ("g4b", c)],
                                 [("g4b", c)])
        self.ln_feat(acc[:], 4, accr, FV_CLNG, FV_CLNB, l, lambda c: yk[:, c, :], ["yk"], func=AF.Silu)
        self.merge_branch(l, 1, yk, 4, ["yk"])

        proj_feat(C_PIN, 512, ps[0], ("ps", 0))
        E = 15 + L
        mixed = self.hb2[:, 0:512].rearrange("p (c t) -> p c t", c=4)
        for (s0, ns) in seq_groups(4):
            a0, n = s0 * L, ns * L
            pext = self.pext[:, :, 0:ns * E].rearrange("p c (s e) -> p c s e", s=ns)
            sv = lambda a: a[:, :, a0:a0 + n].rearrange("p c (s j) -> p c s j", s=ns)
            if kindS:
                self.DMA4(pext[:, :, :, 0:15], self.d_poolS[l][:, :, s0:s0 + ns, :], (), ["pext"])
            else:
                self.CP(pext[:, :, 0, 0:15], self.poolc[:], ["poolc"], ["pext"])
            self.CP(pext[:, :, :, 15:E], sv(c4(ps[0][:])), [("ps", 0)], ["pext"], eng="act")
            if kindS:
                self.DMA4(self.o_poolS[l][:, :, s0:s0 + ns, :], pext[:, :, :, L:E], ["pext"], (), final=True)
            else:
                self.CP(self.poolc[:], pext[:, :, 0, L:E], ["pext"], ["poolc"])
                if lastP:
                    self.DMA(self.o_poolP[l], self.poolc[:], ["poolc"], (), final=True)
            tA = self.ptA[:, 0:ns * E].rearrange("p (s e) -> p s e", s=ns)
            tB = self.ptB[:, 0:ns * E].rearrange("p (s e) -> p s e", s=ns)
            for gi in range(4):
                src, srcr = pext[:, gi], "pext"
                bufs = [(tA, "ptA"), (tB, "ptB")]
                sh = 1
                for step in range(gi + 1):
                    dst, dstr = bufs[step % 2]
                    lo = 2 * sh - 1
                    self.TT(dst[:, :, lo:E], src[:, :, lo:E], src[:, :, lo - sh:E - sh], ALU.add, [srcr], [dstr])
                    src, srcr = dst, dstr
                    sh *= 2
                win = 2 ** (gi + 1)
                if (not kindS) and ti == 0:
                    self.TT(src[:, 0, 15:30], src[:, 0, 15:30], self.corr[:, gi, :], ALU.mult, [srcr, "corr"], [srcr])
                mo = mixed[:, gi, a0:a0 + n].rearrange("p (s j) -> p s j", s=ns)
                self.STT(mo, src[:, :, 15:E], 1.0 / win, pext[:, gi, :, 15:E], ALU.mult, ALU.subtract, [srcr, "pext"], ["hb2"])
        self.MM([(ps[1][:, g * 128:(g + 1) * 128], self.poolw[:, g, :], mixed[:, g, :], True, True) for g in range(4)],
                ["hb2", "poolw"], [("ps", 1)])
        self.TT(yk[:, 0:4, :], c4(ps[1][:]), self.fv(l, FV_PSC, 4).unsqueeze(2).to_broadcast([128, 4, 128]), ALU.mult,
                [("ps", 1), "fvec"], ["yk"])
        self.merge_branch(l, 2, yk, 4, ["yk"])

        proj_tok(C_ZZ, 512, ps[4], ("ps", 4))
        proj_tok(C_ZZ + 512, 256, ps[5], ("ps", 5))
        zs = self.fa[:, 0:768]
        self.ACT(zs[:, 0:512], ps[4][:], AF.Silu, [("ps", 4)], ["fa"])
        self.ACT(zs[:, 512:768], ps[5][:, 0:256], AF.Silu, [("ps", 5)], ["fa"])
        E = 3 + L
        xc = self.xc
        for j, (c0, nco) in enumerate([(0, 512), (512, 512), (1024, 512), (1536, 256)]):
            pst, psr = ps[j % 2], ("ps", j % 2)
            proj_feat(C_XBC + c0, nco, pst, psr)
            nch = nco // 128
            ch0 = c0 // 128
            xe, xer = self.xext[j % 2], ("xext", j % 2)
            for (s0, ns) in seq_groups(8):
                a0, n = s0 * L, ns * L
                xext = xe[:, 0:nch, 0:ns * E].rearrange("p c (s e) -> p c s e", s=ns)
                if kindS:
                    self.DMA4(xext[:, :, :, 0:3], self.d_sconvS[l][:, ch0:ch0 + nch, s0:s0 + ns, :], (), [xer])
                else:
                    self.CP(xext[:, :, 0, 0:3], self.sconvc[:, ch0:ch0 + nch, :], ["sconvc"], [xer])
                self.CP(xext[:, :, :, 3:E], c4(pst[:, 0:nco], nch)[:, :, a0:a0 + n].rearrange("p c (s j) -> p c s j", s=ns),
                        [psr], [xer], eng="act")
                if kindS:
                    self.DMA4(self.o_sconvS[l][:, ch0:ch0 + nch, s0:s0 + ns, :], xext[:, :, :, L:E], [xer], (), final=True)
                else:
                    self.CP(self.sconvc[:, ch0:ch0 + nch, :], xext[:, :, 0, L:E], [xer], ["sconvc"])
                for k in range(4):
                    for cc in range(nch):
                        c = ch0 + cc
                        a3 = self.g4b[:, cc, a0:a0 + n].rearrange("p (s j) -> p s j", s=ns)
                        wk = self.fv(l, FV_SCW + c * 4 + k)
                        if k == 0:
                            self.TS(a3, xext[:, cc, :, 0:L], wk, self.fv(l, FV_SCB + c), ALU.mult, ALU.add, [xer, "fvec"],
                                    [("g4b", cc)])
                        else:
                            self.STT(a3, xext[:, cc, :, k:k + L], wk, a3, ALU.mult, ALU.add, [xer, "fvec", ("g4b", cc)],
                                     [("g4b", cc)])
            gr = [("g4b", cc) for cc in range(nch)]
            if j == 0:
                self.ACT(xc[:, 0:4, :], self.g4b[:, 0:4, :], AF.Silu, gr, ["xc"])
            elif j == 1:
                self.ACT(xc[:, 4:6, :], self.g4b[:, 0:2, :], AF.Silu, gr, ["xc"])
                self.ACT(self.bc[:, 0:2, :], self.g4b[:, 2:4, :], AF.Silu, gr, ["bc"])
            elif j == 2:
                self.ACT(self.bc[:, 2:6, :], self.g4b[:, 0:4, :], AF.Silu, gr, ["bc"])
            else:
                self.ACT(self.bc[:, 6:8, :], self.g4b[:, 0:2, :], AF.Silu, gr, ["bc"])
        if lastP:
            self.DMA(self.o_sconvP[l], self.sconvc[:], ["sconvc"], (), final=True)
        wv, wr = self.wload(self.d_win[l][:, C_DT:C_DT + 12], 8, 12)
        self.MM([(ps[2][:, 0:12], xm[:, dc, :], wv[:, dc, :], dc == 0, dc == 7) for dc in range(8)], [wr, "xm"], [("ps", 2)])
        dt, dta = sm[:, 8:20], sm[:, 20:32]
        self.TT(dt, ps[2][:, 0:12], self.rvec[:, RV_DTB:RV_DTB + 12], ALU.add, [("ps", 2), "rvec"], ["sm"])
        self.ACT(dt, dt, AF.Exp, ["sm"], ["sm"])
        self.TS(dt, dt, 1.0, None, ALU.add, None, ["sm"], ["sm"])
        self.ACT(dt, dt, AF.Ln, ["sm"], ["sm"])
        self.TT(dta, dt, self.arow[:], ALU.mult, ["sm", "arow"], ["sm"])
        self.TR([(ps[4][:, c * 128:(c + 1) * 128], xc[:, c, :], self.identF) for c in range(4)], ["xc", "consts"], [("ps", 4)])
        self.TR([(ps[5][:, (c - 4) * 128:(c - 3) * 128], xc[:, c, :], self.identF) for c in range(4, 6)], ["xc", "consts"],
                [("ps", 5)])
        xs = self.fb[:, 0:768]
        self.CP(xs[:, 0:512], ps[4][:], [("ps", 4)], ["fb"], eng="act")
        self.CP(xs[:, 512:768], ps[5][:, 0:256], [("ps", 5)], ["fb"], eng="act")
        self.TR([(self.psb[:, g * 128:(g + 1) * 128], self.bc[:, g, :], self.identB[:]) for g in range(4)], ["bc", "identB"],
                [("psb", 0)])
        self.CP(self.btok[:].rearrange("p g t -> p (g t)"), self.psb[:, 0:512], [("psb", 0)], ["btok"])
        xd = self.hb1
        self.TT(h3(xd[:]), h3(xs[:]), dt.unsqueeze(2).to_broadcast([128, 12, 64]), ALU.mult, ["fb", "sm"], ["hb1"])
        tri = self.maskS if kindS else self.maskP
        bd = self.bdS if kindS else self.onesF
        neg = self.negS if kindS else self.negP
        self.MM([(ps[2][:, 16:28], tri, dta, True, True), (ps[2][:, 28:40], bd, dta, True, True)], ["sm", "consts"], [("ps", 2)])
        acs, tot, nacs, eacs, dst, et = sm[:, 32:44], sm[:, 44:56], sm[:, 56:68], sm[:, 68:80], sm[:, 80:92], sm[:, 92:104]
        self.CP(sm[:, 32:56], ps[2][:, 16:40], [("ps", 2)], ["sm"])
        self.TS(nacs, acs, -1.0, None, ALU.mult, None, ["sm"], ["sm"])
        self.ACT(eacs, acs, AF.Exp, ["sm"], ["sm"])
        self.TT(dst, tot, acs, ALU.subtract, ["sm"], ["sm"])
        self.ACT(dst, dst, AF.Exp, ["sm"], ["sm"])
        pxi = [0, 1, 3]
        for b3 in range(3):
            pxt, pxr = ps[pxi[b3]], ("ps", pxi[b3])
            mms = []
            for hh in range(4):
                h = b3 * 4 + hh
                o = pxt[:, hh * 128:(hh + 1) * 128]
                mms.append((o, dta[:, h:h + 1].to_broadcast([128, 128]), tri, True, False))
                mms.append((o, self.identF, neg, False, True))
            self.MM(mms, ["sm", "consts"], [pxr])
            for hh in range(4):
                h = b3 * 4 + hh
                self.ACT(self.dec[:, h, :], pxt[:, hh * 128:(hh + 1) * 128], AF.Exp, [pxr, "sm"], [("dec", h // 3)],
                         bias=nacs[:, h:h + 1])
        self.MM([(ps[2][:, g * 128:(g + 1) * 128], self.bc[:, g, :], self.bc[:, 4 + g, :], True, True) for g in range(4)], ["bc"],
                [("ps", 2)])
        for g in range(4):
            self.TT(self.dec[:, 3 * g:3 * g + 3, :], ps[2][:, g * 128:(g + 1) * 128].unsqueeze(1).to_broadcast([128, 3, 128]),
                    self.dec[:, 3 * g:3 * g + 3, :], ALU.mult, [("ps", 2), ("dec", g)], [("dec", g)])
        decr = [("dec", g) for g in range(4)]
        pyo = lambda pa, pb, h: (pa if h < 8 else pb)[:, (h % 8) * 64:(h % 8) * 64 + 64]
        self.MM([(pyo(ps[4], ps[5], h), self.dec[:, h, :], xd[:, h * 64:(h + 1) * 64], True, True) for h in range(12)],
                decr + ["hb1"], [("ps", 4), ("ps", 5)])
        xdd = self.hb2
        self.TT(h3(xdd[:]), h3(xd[:]), dst.unsqueeze(2).to_broadcast([128, 12, 64]), ALU.mult, ["hb1", "sm"], ["hb2"])
        if not kindS:
            self.MM([(pyo(ps[6], ps[2], h), self.bc[:, 4 + h // 3, :], self.stateB[:, h * 64:(h + 1) * 64], True, True)
                     for h in range(12)], ["bc", "stateB"], [("ps", 6), ("ps", 2)])
        else:
            sgi, sgB, sout, xddm = self.state, self.stateB, self.fc[:, 0:768], self.hb1
            for i in range(16):
                self.DMA(sgi[:], self.d_ssmS[l][:, i, :], (), ["state"])
                self.CP(sgB[:], sgi[:], ["state"], ["stateB"], eng="act")
                self.MEMSET(self.cmT[:], 0.0, ["cmT"])
                self.CP(self.cmT[:, :, i * 8:(i + 1) * 8], self.bc[:, 4:8, i * 8:(i + 1) * 8], ["bc"], ["cmT"])
                self.MM([(pyo(ps[6], ps[2], h), self.cmT[:, h // 3, :], sgB[:, h * 64:(h + 1) * 64], i == 0, i == 15)
                         for h in range(12)], ["cmT", "stateB"], [("ps", 6), ("ps", 2)])
                self.TS(xddm[:], xdd[:], self.m3[:, i:i + 1], None, ALU.mult, None, ["hb2", "m3"], ["hb1"])
                self.MM([(pyo(ps[0], ps[1], h), self.btok[:, h // 3, :], xddm[:, h * 64:(h + 1) * 64], True, True)
                         for h in range(12)], ["btok", "hb1"], [("ps", 0), ("ps", 1)])
                self.MM([(ps[3][:, 0:12], self.m3[:, i:i + 1].to_broadcast([128, 128]), dta, True, True)], ["m3", "sm"],
                        [("ps", 3)])
                self.ACT(et, ps[3][:, 0:12], AF.Exp, [("ps", 3)], ["sm"])
                self.TT(h3(sout[:]), h3(sgi[:]), et.unsqueeze(2).to_broadcast([128, 12, 64]), ALU.mult, ["state", "sm"], ["fc"])
                self.TT(sout[:, 0:512], sout[:, 0:512], ps[0][:], ALU.add, ["fc", ("ps", 0)], ["fc"])
                self.TT(sout[:, 512:768], sout[:, 512:768], ps[1][:, 0:256], ALU.add, ["fc", ("ps", 1)], ["fc"])
                self.DMA(self.o_ssmS[l][:, i, :], sout[:], ["fc"], (), final=True)
        y = self.fc[:, 0:768]
        self.TT(h3(y[:])[:, 0:8, :], ps[6][:].rearrange("p (h d) -> p h d", h=8),
                eacs[:, 0:8].unsqueeze(2).to_broadcast([128, 8, 64]), ALU.mult, [("ps", 6), "sm"], ["fc"])
        self.TT(h3(y[:])[:, 8:12, :], ps[2][:, 0:256].rearrange("p (h d) -> p h d", h=4),
                eacs[:, 8:12].unsqueeze(2).to_broadcast([128, 4, 64]), ALU.mult, [("ps", 2), "sm"], ["fc"])
        self.TT(y[:, 0:512], y[:, 0:512], ps[4][:], ALU.add, ["fc", ("ps", 4)], ["fc"])
        self.TT(y[:, 512:768], y[:, 512:768], ps[5][:, 0:256], ALU.add, ["fc", ("ps", 5)], ["fc"])
        self.TT(h3(xs[:]), h3(xs[:]), self.rvec[:, RV_D:RV_D + 12].unsqueeze(2).to_broadcast([128, 12, 64]), ALU.mult,
                ["fb", "rvec"], ["fb"])
        self.TT(y[:], y[:], xs[:], ALU.add, ["fc", "fb"], ["fc"])
        self.TT(y[:], y[:], zs[:], ALU.mult, ["fc", "fa"], ["fc"])
        self.TT(xs[:], y[:], y[:], ALU.mult, ["fc"], ["fb"])
        self.RSUM(sm[:, 4:5], xs[:], ["fb"], ["sm"])
        self.TS(sm[:, 4:5], sm[:, 4:5], 1.0 / 768, LN_EPS, ALU.mult, ALU.add, ["sm"], ["sm"])
        self.SQRT(sm[:, 4:5], "sm")
        self.RECIP(sm[:, 4:5], sm[:, 4:5], ["sm"], ["sm"])
        yd = self.hb1
        self.TS(yd[:], y[:], sm[:, 4:5], None, ALU.mult, None, ["fc", "sm"], ["hb1"])
        self.TR([(self.psb[:, c * 128:(c + 1) * 128], yd[:, c * 128:(c + 1) * 128], self.identB[:]) for c in range(6)],
                ["hb1", "identB"], [("psb", 0), ("psb", 1)])
        for c in range(6):
            self.ACT(yk[:, c, :], self.psb[:, c * 128:(c + 1) * 128], AF.Identity, [("psb", 0), ("psb", 1), "fvec"], ["yk"],
                     scale=self.fv(l, FV_NG + c))
        if not kindS:
            self.MM([(pyo(ps[0], ps[1], h), self.btok[:, h // 3, :], xdd[:, h * 64:(h + 1) * 64], True, True) for h in range(12)],
                    ["btok", "hb2"], [("ps", 0), ("ps", 1)])
            self.ACT(et, tot, AF.Exp, ["sm"], ["sm"])
            self.TT(h3(self.state[:]), h3(self.state[:]), et.unsqueeze(2).to_broadcast([128, 12, 64]), ALU.mult,
                    ["state", "sm"], ["state"])
            self.TT(self.state[:, 0:512], self.state[:, 0:512], ps[0][:], ALU.add, ["state", ("ps", 0)], ["state"])
            self.TT(self.state[:, 512:768], self.state[:, 512:768], ps[1][:, 0:256], ALU.add, ["state", ("ps", 1)], ["state"])
            self.CP(self.stateB[:], self.state[:], ["state"], ["stateB"], eng="act")
            if lastP:
                self.DMA(self.o_ssmP[l], self.state[:], ["state"], (), final=True)
        self.merge_branch(l, 3, yk, 6, ["yk"])

        self.CP(self.mergedB[:], self.merged[:], [("merged", 0), ("merged", 1)], ["mergedB"], eng="act")
        xt = self.xT[:, :, t0:t0 + 128]
        self.TS(xt, xt, DN_ALPHA, None, ALU.mult, None, [xr, "xm"], [xr])
        for h in range(2):
            wo, wor = self.wload(self.d_wo[l][:, h * 512:(h + 1) * 512], 8, 512)
            pm, pmr = ps[h], ("ps", h)
            mms = []
            for mc in range(4):
                for c in range(8):
                    mms.append((pm[:, mc * 128:(mc + 1) * 128], wo[:, c, mc * 128:(mc + 1) * 128], self.mergedB[:, c, :], c == 0,
                                c == 7))
            self.MM(mms, [wor, "mergedB"], [pmr])
            if not kindS:
                for mc in range(4):
                    c = h * 4 + mc
                    self.STT(xt[:, c, :], pm[:, mc * 128:(mc + 1) * 128], self.ada[:, 16 + c, 0:1], xt[:, c, :], ALU.mult, ALU.add,
                             [pmr, "ada", xr], [xr])
            else:
                g1b = self.ada[:, 16 + h * 4:16 + h * 4 + 4, 1:17].unsqueeze(3).to_broadcast([128, 4, 16, 8])
                tmp = self.g4a[:].rearrange("p c (s j) -> p c s j", s=16)
                self.TT(tmp, pm[:].rearrange("p (c s j) -> p c s j", c=4, s=16), g1b, ALU.mult, [pmr, "ada"], ["g4a"])
                self.TT(xt[:, h * 4:h * 4 + 4, :], xt[:, h * 4:h * 4 + 4, :], self.g4a[:], ALU.add, ["g4a", xr], [xr])
        self.ln_feat(xt, 8, [xr], FV_L1G, FV_L1B, l, lambda c: xt[:, c, :], [xr])

    def route_tile(self, l, ti):
        ps = self.psf
        rt = self.rt
        X2 = self.X2
        self.modulate(ti, 24, 32, X2, ["X2"])
        self.MM([(ps[0][:, 0:36], X2[:, dc, :], self.routerw[:, dc, :], dc == 0, dc == 7) for dc in range(8)],
                ["X2", "routerw"], [("ps", 0)])
        lg = rt[:, 0:36]
        self.CP(lg, ps[0][:, 0:36], [("ps", 0)], ["rt"])
        R = lambda a, b: rt[:, a:b]
        mg, nmg, ohg, eg, sg_, les, v1 = R(36, 37), R(37, 38), R(40, 44), R(44, 48), R(48, 49), R(52, 60), R(60, 61)
        oh1, le2, v2, oh2 = R(64, 72), R(72, 80), R(80, 81), R(84, 92)
        e2, w1, w2, tmp32 = R(92, 93), R(93, 94), R(94, 95), R(112, 144)
        rr = ["rt"]
        self.RMAX(mg, lg[:, 0:4], rr, rr)
        self.TS(ohg, lg[:, 0:4], mg, None, ALU.is_equal, None, rr, rr)
        self.TS(nmg, mg, -1.0, None, ALU.mult, None, rr, rr)
        self.ACT(eg, lg[:, 0:4], AF.Exp, rr, rr, bias=nmg)
        self.RSUM(sg_, eg, rr, rr)
        self.RECIP(sg_, sg_, rr, rr)
        self.TT(tmp32.rearrange("p (g j) -> p g j", g=4), lg[:, 4:36].rearrange("p (g j) -> p g j", g=4),
                ohg.unsqueeze(2).to_broadcast([128, 4, 8]), ALU.mult, rr, rr)
        self.RSUM(les, tmp32.rearrange("p (g j) -> p j g", g=4), rr, rr)
        self.RMAX(v1, les, rr, rr)
        self.TS(oh1, les, v1, None, ALU.is_equal, None, rr, rr)
        self.STT(le2, oh1, -1e30, les, ALU.mult, ALU.add, rr, rr)
        self.RMAX(v2, le2, rr, rr)
        self.TS(oh2, le2, v2, None, ALU.is_equal, None, rr, rr)
        self.TT(e2, v2, v1, ALU.subtract, rr, rr)
        self.ACT(e2, e2, AF.Exp, rr, rr)
        self.TS(w1, e2, 1.0, None, ALU.add, None, rr, rr)
        self.RECIP(w1, w1, rr, rr)
        self.TT(w2, e2, w1, ALU.mult, rr, rr)
        self.TT(self.Wv[:, ti, 0:1], w1, sg_, ALU.mult, rr, ["Wv"])
        self.TT(self.Wv[:, ti, 1:2], w2, sg_, ALU.mult, rr, ["Wv"])
        oh = self.OHall[:, ti, :]
        ohr = ("OH", ti)
        g3 = lambda a: a.rearrange("p (g j) -> p g j", g=4)
        self.TT(g3(oh[:, 0:32]), ohg.unsqueeze(2).to_broadcast([128, 4, 8]), oh1.unsqueeze(1).to_broadcast([128, 4, 8]),
                ALU.mult, rr, [ohr])
        self.TT(g3(oh[:, 32:64]), ohg.unsqueeze(2).to_broadcast([128, 4, 8]), oh2.unsqueeze(1).to_broadcast([128, 4, 8]),
                ALU.mult, rr, [ohr])
        self.MM([(ps[1][:, 0:64], self.triS, oh, True, True), (ps[1][:, 64:128], self.onesF, oh, True, True)],
                [ohr, "consts"], [("ps", 1)])
        t64 = rt[:, 160:224]
        self.TT(t64, ps[1][:, 0:64], self.carry[:], ALU.add, [("ps", 1), "carry"], rr)
        self.TT(t64, t64, oh, ALU.mult, rr + [ohr], rr)
        self.RSUM(self.rank[:, ti, :], t64.rearrange("p (k e) -> p k e", k=2), rr, ["rank"])
        self.TT(self.carry[:], self.carry[:], ps[1][:, 64:128], ALU.add, [("ps", 1), "carry"], ["carry"])

    def moe(self, l):
        ps = self.psf
        NB = self.NBLK
        I32 = mybir.dt.int32
        gm = self.gm
        gr = ["gm"]
        G = lambda a, b: gm[:, a:b]
        cnt1, cnt2, tot, nblk, pad, e0, e1, startp = G(0, 32), G(32, 64), G(64, 96), G(96, 128), G(128, 160), G(160, 192), G(192, 224), G(224, 256)
        basev = G(256, 320)
        self.CP(gm[:, 0:64], self.carry[:], ["carry"], gr)
        self.TT(tot, cnt1, cnt2, ALU.add, gr, gr)
        NJ = self.NJ
        cmp = self.fa[:, 0:32 * NJ].rearrange("p (e j) -> p e j", e=32)
        self.TT(cmp, tot.unsqueeze(2).to_broadcast([128, 32, NJ]), self.crow[:, 0:NJ].unsqueeze(1).to_broadcast([128, 32, NJ]),
                ALU.is_gt, gr + ["crow"], ["fa"])
        self.RSUM(nblk, cmp, ["fa"], gr)
        self.TS(pad, nblk, 128.0, None, ALU.mult, None, gr, gr)
        src, dst = pad, e0
        sh = 1
        while sh < 32:
            self.CP(dst[:, 0:sh], src[:, 0:sh], gr, gr)
            self.TT(dst[:, sh:32], src[:, sh:32], src[:, 0:32 - sh], ALU.add, gr, gr)
            src, dst = dst, (e1 if dst is e0 else e0)
            sh *= 2
        endp = src
        self.TT(startp, endp, pad, ALU.subtract, gr, gr)
        self.CP(basev[:, 0:32], startp, gr, gr)
        self.TT(basev[:, 32:64], startp, cnt1, ALU.add, gr, gr)
        CH = 22
        for b0 in range(0, NB, CH):
            nb = min(CH, NB - b0)
            cm2 = self.fa[:, 0:nb * 32].rearrange("p (b e) -> p b e", b=nb)
            self.TT(cm2, endp.unsqueeze(1).to_broadcast([128, nb, 32]),
                    self.crow[:, 32 + b0:32 + b0 + nb].unsqueeze(2).to_broadcast([128, nb, 32]), ALU.is_le, gr + ["crow"], ["fa"])
            self.RSUM(self.blke[:, b0:b0 + nb], cm2, ["fa"], ["blke"])
        self.TS(self.blke[:], self.blke[:], 31.0, 128.0, ALU.min, ALU.mult, ["blke"], ["blke"])
        self.TS(self.blke[:], self.blke[:], self.pidx[:, 0:1], float(l * 4096), ALU.add, ALU.add, ["blke", "pidx"], ["blke"])
        self.CP(self.widx[:], self.blke[:], ["blke"], ["widx"])
        if l == 0:
            self.CP(self.blke[:], self.widx[:], ["widx"], ["blke"])
            self.DMA(self.o_dbg2, self.blke[:], ["blke"], (), final=True)
            self.DMA(self.o_dbg3, gm[:], gr, (), final=True)
        for ti in range(self.ntile):
            t0 = ti * 128
            xr = ("xT", ti)
            oh = self.OHall[:, ti, :]
            t64 = self.rt[:, 160:224]
            self.TT(t64, oh, basev, ALU.mult, [("OH", ti)] + gr, ["rt"])
            df = self.rt[:, 224:226]
            self.RSUM(df, t64.rearrange("p (k e) -> p k e", k=2), ["rt"], ["rt"])
            self.TT(df, df, self.rank[:, ti, :], ALU.add, ["rt", "rank"], ["rt"])
            self.CP(self.desti[:, ti, :], df, ["rt"], [("desti", ti)])
            X2 = self.X2
            self.modulate(ti, 24, 32, X2, ["X2"])
            self.TR([(ps[2][:, c * 128:(c + 1) * 128], X2[:, c, :], self.identF) for c in range(4)], ["X2", "consts"], [("ps", 2)])
            self.TR([(ps[3][:, (c - 4) * 128:(c - 3) * 128], X2[:, c, :], self.identF) for c in range(4, 8)], ["X2", "consts"],
                    [("ps", 3)])
            xtok = self.xtok[ti % 2]
            xtr = ("xtok", ti % 2)
            self.CP(xtok[:, 0:512], ps[2][:], [("ps", 2)], [xtr], eng="act")
            self.CP(xtok[:, 512:1024], ps[3][:], [("ps", 3)], [xtr])
            for k in range(2):
                if self.stop_after == "route":
                    break
                off = self.desti[:, ti, k:k + 1]
                self.S.add("pool", lambda e, off=off, xtok=xtok: e.indirect_dma_start(
                    out=self.d_buf, out_offset=bass.IndirectOffsetOnAxis(ap=off, axis=0), in_=xtok[:, :], in_offset=None), [xtr, ("desti", ti), "bufz"], [("bufw", ti, k)], dma=True)
            xt = self.xT[:, :, t0:t0 + 128]
            self.TS(xt, xt, DN_ALPHA, None, ALU.mult, None, [xr, "X2"], [xr])
        if l == 0:
            dd = self.fb[:, 0:self.ntile * 2]
            self.CP(dd, self.desti[:].rearrange("p t k -> p (t k)"), [("desti", ti) for ti in range(self.ntile)], ["fb"])
            self.DMA(self.o_dbg1, dd, ["fb"], (), final=True)
        if self.stop_after in ("route", "scatter"):
            return
        bufw = [("bufw", ti, k) for ti in range(self.ntile) for k in range(2)]
        def blk_loads(b):
            xb = self.xblk[b % 2]
            xbr = ("xblk", b % 2)
            self.DMA(xb[:], self.d_buf[b * 128:(b + 1) * 128, :], bufw, [xbr], q="pool")
            wts = [wgather(b, self.d_wegp), wgather(b, self.d_weup)]
            return xb, xbr, wts

        def wgather(b, dsrc):
            off = self.widx[:, b:b + 1]
            i = self._wsi % len(self.ws)
            self._wsi += 1
            t = self.ws[i]
            self.S.add("pool", lambda e, off=off, t=t, dsrc=dsrc: e.indirect_dma_start(
                out=t[:, :], out_offset=None, in_=dsrc, in_offset=bass.IndirectOffsetOnAxis(ap=off, axis=0)),
                ["widx"], [("ws", i), ("gch", self._gch % 2)], dma=True)
            self._gch += 1
            return (t, ("ws", i))

        pend = blk_loads(0)
        pend_d = wgather(0, self.d_wedp)
        for b in range(NB):
            xb, xbr, wts = pend
            wts = wts + [pend_d]
            if b + 1 < NB:
                pend = blk_loads(b + 1)
            for h in range(2):
                self.TR([(self.psb[:, (h * 4 + c) * 128:(h * 4 + c + 1) * 128], xb[:, (h * 4 + c) * 128:(h * 4 + c + 1) * 128],
                          self.identB[:]) for c in range(4)], [xbr, "identB"], [("psb", 0), ("psb", 1)])
                if self.stop_after == "blocks1a":
                    continue
                if h == 0:
                    self.ACT(self.xbT[:, 0:4, :].rearrange("p c t -> p (c t)"), self.psb[:, 0:512], AF.Identity, [("psb", 0), ("psb", 1)], [("xbT", 0)])
                else:
                    self.CP(self.xbT[:, 4:8, :].rearrange("p c t -> p (c t)"), self.psb[:, 512:1024], [("psb", 0), ("psb", 1)], [("xbT", 1)])
            if self.stop_after == "blocks2":
                continue
            wg = wts[0][0][:, :].rearrange("p (k n) -> p k n", k=8)
            wu = wts[1][0][:, :].rearrange("p (k n) -> p k n", k=8)
            wd = wts[2][0][:, :].rearrange("p (k n) -> p k n", k=4)
            xbTr = [("xbT", 0), ("xbT", 1)]
            self.MM([(ps[0][:], self.xbT[:, k, :], wg[:, k, :], k == 0, k == 7) for k in range(8)], xbTr + [wts[0][1]], [("ps", 0)])
            self.MM([(ps[1][:], self.xbT[:, k, :], wu[:, k, :], k == 0, k == 7) for k in range(8)], xbTr + [wts[1][1]], [("ps", 1)])
            if b + 1 < NB:
                pend_d = wgather(b + 1, self.d_wedp)
            sgt = self.fb[:, 0:512]
            self.ACT(sgt, ps[0][:], AF.Silu, [("ps", 0)], ["fb"])
            self.TT(self.hidtok[:], sgt, ps[1][:], ALU.mult, ["fb", ("ps", 1)], ["hidtok"])
            self.TR([(self.psb[:, c * 128:(c + 1) * 128], self.hidtok[:, c * 128:(c + 1) * 128], self.identB[:]) for c in range(4)],
                    ["hidtok", "identB"], [("psb", 0), ("psb", 1)])
            self.ACT(self.hidT[:].rearrange("p c t -> p (c t)"), self.psb[:, 0:512], AF.Identity, [("psb", 0), ("psb", 1)], ["hidT"])
            self.MM([(ps[4][:], self.hidT[:, fc, :], wd[:, fc, 0:512], fc == 0, fc == 3) for fc in range(4)], ["hidT", wts[2][1]],
                    [("ps", 4)])
            self.MM([(ps[5][:], self.hidT[:, fc, :], wd[:, fc, 512:1024], fc == 0, fc == 3) for fc in range(4)], ["hidT", wts[2][1]],
                    [("ps", 5)])
            yb = self.yblk[b % 2]
            ybr = ("yblk", b % 2)
            self.CP(yb[:, 0:512], ps[4][:], [("ps", 4)], [ybr], eng="act")
            self.CP(yb[:, 512:1024], ps[5][:], [("ps", 5)], [ybr])
            self.DMA(self.d_ybuf[b * 128:(b + 1) * 128, :], yb[:], [ybr], [("ybuf", b)], q="pool")
        ybr_all = [("ybuf", b) for b in range(NB)]
        if self.stop_after in ("blocks", "blocks0", "blocks1", "blocks1a", "blocks2"):
            return
        for ti in range(self.ntile):
            t0 = ti * 128
            kindS = ti == self.npt
            xr = ("xT", ti)
            G1, G2 = self.fa, self.fc
            for k, (Gt, Gr) in enumerate(((G1, "fa"), (G2, "fc"))):
                off = self.desti[:, ti, k:k + 1]
                self.S.add("pool", lambda e, off=off, Gt=Gt: e.indirect_dma_start(
                    out=Gt[:, :], out_offset=None, in_=self.d_ybuf, in_offset=bass.IndirectOffsetOnAxis(ap=off, axis=0)), ybr_all + [("desti", ti)], [Gr], dma=True)
            self.TS(G1[:], G1[:], self.Wv[:, ti, 0:1], None, ALU.mult, None, ["fa", "Wv"], ["fa"])
            self.STT(G1[:], G2[:], self.Wv[:, ti, 1:2], G1[:], ALU.mult, ALU.add, ["fa", "fc", "Wv"], ["fa"])
            self.TR([(ps[2][:, c * 128:(c + 1) * 128], G1[:, c * 128:(c + 1) * 128], self.identF) for c in range(4)],
                    ["fa", "consts"], [("ps", 2)])
            self.TR([(ps[3][:, (c - 4) * 128:(c - 3) * 128], G1[:, c * 128:(c + 1) * 128], self.identF) for c in range(4, 8)],
                    ["fa", "consts"], [("ps", 3)])
            xt = self.xT[:, :, t0:t0 + 128]
            for h in range(2):
                pm, pmr = ps[2 + h], ("ps", 2 + h)
                if not kindS:
                    for mc in range(4):
                        c = h * 4 + mc
                        self.STT(xt[:, c, :], pm[:, mc * 128:(mc + 1) * 128], self.ada[:, 40 + c, 0:1], xt[:, c, :], ALU.mult, ALU.add,
                                 [pmr, "ada", xr], [xr])
                else:
                    g2b = self.ada[:, 40 + h * 4:40 + h * 4 + 4, 1:17].unsqueeze(3).to_broadcast([128, 4, 16, 8])
                    tmp = self.g4a[:].rearrange("p c (s j) -> p c s j", s=16)
                    self.TT(tmp, pm[:].rearrange("p (c s j) -> p c s j", c=4, s=16), g2b, ALU.mult, [pmr, "ada"], ["g4a"])
                    self.TT(xt[:, h * 4:h * 4 + 4, :], xt[:, h * 4:h * 4 + 4, :], self.g4a[:], ALU.add, ["g4a", xr], [xr])
            self.ln_feat(xt, 8, [xr], FV_L2G, FV_L2B, l, lambda c: xt[:, c, :], [xr])


def _feat_major(a2d):
    r, f = a2d.shape
    return np.ascontiguousarray(a2d.T.reshape(f // 128, 128, r).transpose(1, 0, 2))


def _fm_vec(v):
    f = v.shape[-1]
    return np.moveaxis(v.reshape(v.shape[:-1] + (f // 128, 128)), -1, 0)


_CACHE = {}


def kernel(x_prompt, x_sample, state_conv, state_pool, state_ssm_conv, state_ssm, c_prompt, c_sample,
           w_ada, b_ada, w_in, sgu_ln_g, sgu_ln_b, sgu_w, sgu_b, conv_w, conv_bias, conv_ln_g, conv_ln_b,
           pool_w, pool_scale, ssm_conv_w, ssm_conv_b, ssm_dt_bias, ssm_a_log, ssm_d, ssm_norm_g,
           w_br_a, w_br_b, w_br_c, w_br_d, w_o, ln1_g, ln1_b, router_g, router_e, w_e_gate, w_e_up,
           w_e_down, ln2_g, ln2_b, _stop_after=None):
    f = lambda a: np.asarray(a, dtype=np.float32)
    x_prompt, x_sample = f(x_prompt), f(x_sample)
    ncore = x_prompt.shape[0]
    D = w_in.shape[0]
    seq = x_prompt.shape[1]
    npt = seq // 128
    T = seq + 128
    nsb = x_sample.shape[0] // ncore
    assert nsb == 16 and x_sample.shape[1] == 8
    key = (D, npt, _stop_after)
    if key not in _CACHE:
        _CACHE[key] = Builder(D, npt, stop_after=_stop_after)
    bld = _CACHE[key]

    shared = {}
    shared["w_ada"] = f(w_ada)
    shared["b_adaT"] = np.ascontiguousarray(_fm_vec(f(b_ada)))
    shared["w_in"] = f(w_in)
    fv = np.zeros((128, D, NFV), np.float32)
    fv[:, :, FV_CONVB:FV_CONVB + 4] = _fm_vec(f(conv_bias))
    fv[:, :, FV_CLNG:FV_CLNG + 4] = _fm_vec(f(conv_ln_g))
    fv[:, :, FV_CLNB:FV_CLNB + 4] = _fm_vec(f(conv_ln_b))
    fv[:, :, FV_PSC:FV_PSC + 4] = _fm_vec(f(pool_scale))
    fv[:, :, FV_SCB:FV_SCB + 14] = _fm_vec(f(ssm_conv_b))
    fv[:, :, FV_L1G:FV_L1G + 8] = _fm_vec(f(ln1_g))
    fv[:, :, FV_L1B:FV_L1B + 8] = _fm_vec(f(ln1_b))
    fv[:, :, FV_L2G:FV_L2G + 8] = _fm_vec(f(ln2_g))
    fv[:, :, FV_L2B:FV_L2B + 8] = _fm_vec(f(ln2_b))
    cw = _fm_vec(f(conv_w))
    fv[:, :, FV_CW:FV_CW + 124] = cw.transpose(0, 1, 3, 2).reshape(128, D, 124)
    scw = _fm_vec(f(ssm_conv_w))
    fv[:, :, FV_SCW:FV_SCW + 56] = scw.transpose(0, 1, 3, 2).reshape(128, D, 56)
    fv[:, :, FV_NG:FV_NG + 6] = _fm_vec(f(ssm_norm_g))
    shared["fvec"] = fv
    rv = np.zeros((D, NRV), np.float32)
    rv[:, RV_SLG:RV_SLG + 512] = f(sgu_ln_g)
    rv[:, RV_SLB:RV_SLB + 512] = f(sgu_ln_b)
    rv[:, RV_DTB:RV_DTB + 12] = f(ssm_dt_bias)
    rv[:, RV_ALOG:RV_ALOG + 12] = f(ssm_a_log)
    rv[:, RV_D:RV_D + 12] = f(ssm_d)
    shared["rvec"] = rv
    sgb = np.zeros((D, 2, 512), np.float32)
    sgb[:, 0] = f(sgu_b).reshape(D, 512)
    sgb[:, 1] = np.tile(f(sgu_b)[:, :, :8], (1, 1, 16)).reshape(D, 512)
    shared["sgub"] = sgb
    sw = f(sgu_w)
    sws = np.zeros((D, 2, 128, 4, 128), np.float32)
    sws[:, 0] = sw.transpose(0, 3, 1, 2)
    blk = sw[:, :, :8, :8].transpose(0, 3, 1, 2)
    for i in range(16):
        sws[:, 1, i * 8:(i + 1) * 8, :, i * 8:(i + 1) * 8] = blk
    shared["sgu_wT"] = sws
    shared["pool_w"] = f(pool_w)
    for nm, a in (("w_br_a", w_br_a), ("w_br_b", w_br_b), ("w_br_c", w_br_c), ("w_br_d", w_br_d), ("w_o", w_o),
                  ):
        shared[nm] = f(a)
    shared["w_e_gate_p"] = np.ascontiguousarray(f(w_e_gate).reshape(D, 32, 8, 128, 512).transpose(0, 1, 3, 2, 4)).reshape(D * 4096, 4096)
    shared["w_e_up_p"] = np.ascontiguousarray(f(w_e_up).reshape(D, 32, 8, 128, 512).transpose(0, 1, 3, 2, 4)).reshape(D * 4096, 4096)
    shared["w_e_down_p"] = np.ascontiguousarray(f(w_e_down).reshape(D, 32, 4, 128, 1024).transpose(0, 1, 3, 2, 4)).reshape(D * 4096, 4096)
    crow = np.zeros((1, 32 + bld.NBLK), np.float32)
    crow[0, 0:32] = 128.0 * np.arange(32)
    crow[0, 32:] = 128.0 * np.arange(bld.NBLK)
    shared["crow"] = crow
    shared["pidx"] = np.arange(128, dtype=np.float32).reshape(128, 1)
    shared["router"] = np.ascontiguousarray(np.concatenate([f(router_g), f(router_e)], axis=2))
    idx = np.arange(128)
    same = (idx[:, None] // 8) == (idx[None, :] // 8)
    caus = idx[:, None] <= idx[None, :]
    cst = np.zeros((128, 8, 128), np.float32)
    cst[:, 0] = np.eye(128)
    cst[:, 1] = caus
    cst[:, 2] = caus & same
    cst[:, 3] = np.where(caus, 0.0, NEG)
    cst[:, 4] = np.where(caus & same, 0.0, NEG)
    cst[:, 5] = same
    cst[:, 6] = 1.0
    cst[:, 7] = idx[:, None] < idx[None, :]
    shared["consts"] = cst
    shared["m3"] = ((idx[:, None] // 8) == np.arange(16)[None, :]).astype(np.float32)
    corr = np.zeros((128, 4, 15), np.float32)
    for gi in range(4):
        w = 2 ** (gi + 1)
        corr[:, gi, :] = w / np.minimum(w, np.arange(15) + 1.0)
    shared["corr"] = corr

    in_maps = []
    for c in range(ncore):
        m = dict(shared)
        xs = x_sample[c * 16:(c + 1) * 16].reshape(128, 1024)
        m["xT_in"] = _feat_major(np.concatenate([x_prompt[c], xs], axis=0))
        m["cT_in"] = _feat_major(np.concatenate([f(c_prompt)[c:c + 1], f(c_sample)[c * 16:(c + 1) * 16]], axis=0))
        sl = slice(c * 16, (c + 1) * 16)
        def st(a):
            a = f(a)[:, sl]
            d, s, r, cc = a.shape
            return np.ascontiguousarray(a.reshape(d, s, r, cc // 128, 128).transpose(0, 4, 3, 1, 2))
        m["conv_sT"] = st(state_conv)
        m["pool_sT"] = st(state_pool)
        m["sconv_sT"] = st(state_ssm_conv)
        ss = f(state_ssm)[:, sl]
        m["ssm_sT"] = np.ascontiguousarray(ss.transpose(0, 4, 1, 2, 3).reshape(D, 128, 16, 768))
        in_maps.append(m)

    res = run_bass_kernel_spmd(bld.nc, in_maps, core_ids=list(range(ncore)))
    R = res.results
    kernel._last = R

    def unfm(a):
        a = np.asarray(a)
        nch = a.shape[1]
        return np.moveaxis(a, (0, 1), (-1, -2)).reshape(a.shape[2:] + (nch * 128,))

    y_prompt = np.stack([unfm(R[c]["yT"])[:seq] for c in range(ncore)])
    y_sample = np.concatenate([unfm(R[c]["yT"])[seq:].reshape(16, 8, 1024) for c in range(ncore)])
    def pst(name):
        return np.stack([np.stack([unfm(R[c][name][l]) for l in range(D)]) for c in range(ncore)], axis=1)
    p_conv = pst("o_conv_p")
    p_pool = pst("o_pool_p")
    p_sconv = pst("o_sconv_p")
    p_ssm = np.stack([np.asarray(R[c]["o_ssm_p"]).reshape(D, 128, 12, 64).transpose(0, 2, 3, 1) for c in range(ncore)], axis=1)
    def sst(name):
        return np.concatenate([np.stack([unfm(R[c][name][l]) for l in range(D)]) for c in range(ncore)], axis=1)
    s_conv = sst("o_conv_s")
    s_pool = sst("o_pool_s")
    s_sconv = sst("o_sconv_s")
    s_ssm = np.concatenate([np.asarray(R[c]["o_ssm_s"]).reshape(D, 128, 16, 12, 64).transpose(0, 2, 3, 4, 1)
                            for c in range(ncore)], axis=1)
    s_v = np.concatenate([np.asarray(R[c]["o_v_s"]).reshape(D, 16, 8, 512) for c in range(ncore)], axis=1)
    outs = (y_prompt, y_sample, p_conv, p_pool, p_sconv, p_ssm, s_conv, s_pool, s_sconv, s_ssm, s_v)
    return tuple(np.ascontiguousarray(o, dtype=np.float32) for o in outs)
```

```python
import numpy as np
import concourse.bass as bass
import concourse.mybir as mybir
from concourse.bass_utils import run_bass_kernel_spmd

F32 = mybir.dt.float32
BF16 = mybir.dt.bfloat16
AF = mybir.ActivationFunctionType
ALU = mybir.AluOpType
AX = mybir.AxisListType

ENGINES = ("pe", "act", "dve", "pool", "sp")
NDMA = 8
NWS = 5
DN_ALPHA = 8.0 ** 0.25
LN_EPS = 1e-5
NEG = -30000.0

C_U, C_V, C_GA, C_GG, C_PIN, C_ZZ, C_XBC, C_DT, C_GL = 0, 512, 1024, 1536, 2048, 2560, 3328, 5120, 5132
FV_CONVB, FV_CLNG, FV_CLNB, FV_PSC, FV_SCB, FV_L1G, FV_L1B, FV_L2G, FV_L2B, FV_CW, FV_SCW = 0, 4, 8, 12, 16, 30, 38, 46, 54, 62, 186
FV_NG = 242
NFV = 248
RV_SLG, RV_SLB, RV_DTB, RV_ALOG, RV_D = 0, 512, 1024, 1036, 1048
NRV = 1060


class Sched:
    def __init__(self, nc):
        self.nc = nc
        self.ops = {e: [] for e in ENGINES}
        self.count = {e: 0 for e in ENGINES}
        self.dmaj = {e: 0 for e in ENGINES}
        self.last_writer = {}
        self.readers = {}
        self.known = {e: {} for e in ENGINES}
        self.out_tokens = []

    def add(self, engine, fn, reads=(), writes=(), dma=False, out=False):
        deps = []
        for r in reads:
            t = self.last_writer.get(r)
            if t is not None:
                deps.append(t)
        for w in writes:
            t = self.last_writer.get(w)
            if t is not None:
                deps.append(t)
            deps.extend(self.readers.get(w, ()))
        if dma:
            j = self.dmaj[engine]
            s = j % NDMA
            key = ("dma", engine, s)
            tok = (key, 16 * (j // NDMA + 1))
            if j >= NDMA:
                deps.append((key, 16 * (j // NDMA)))
            self.dmaj[engine] = j + 1
        else:
            self.count[engine] += 1
            tok = (("eng", engine), self.count[engine])
        best = {}
        for k, v in deps:
            if best.get(k, 0) < v:
                best[k] = v
        waits = []
        kn = self.known[engine]
        for k, v in best.items():
            if kn.get(k, 0) < v:
                waits.append((k, v))
                kn[k] = v
        self.ops[engine].append((waits, fn, tok, dma))
        for r in reads:
            self.readers.setdefault(r, []).append(tok)
        for w in writes:
            self.last_writer[w] = tok
            self.readers[w] = []
        if out:
            self.out_tokens.append(tok)
        return tok

    def emit(self):
        nc = self.nc
        best = {}
        for k, v in self.out_tokens:
            if best.get(k, 0) < v:
                best[k] = v
        for e in ENGINES:
            if self.count[e]:
                best[("eng", e)] = self.count[e]
            for j in range(max(0, self.dmaj[e] - NDMA), self.dmaj[e]):
                k = ("dma", e, j % NDMA)
                best[k] = max(best.get(k, 0), 16 * (j // NDMA + 1))
        final_waits = list(best.items())
        keys = set()
        for e in ENGINES:
            for waits, fn, tok, dma in self.ops[e]:
                keys.add(tok[0])
        sems = {}
        for k in sorted(keys, key=str):
            sems[k] = nc.alloc_semaphore(name="s_" + "_".join(str(x) for x in k))
        attr = {"pe": "tensor", "act": "scalar", "dve": "vector", "pool": "gpsimd", "sp": "sync"}
        ops = self.ops
        with nc.Block() as block:
            def mk(e):
                def body(eng):
                    for waits, fn, tok, dma in ops[e]:
                        for k, v in waits:
                            eng.wait_ge(sems[k], v)
                        ins = fn(eng)
                        ins.then_inc(sems[tok[0]], 16 if dma else 1)
                    if e == "sp":
                        for k, v in final_waits:
                            eng.wait_ge(sems[k], v)
                return body
            for e in ENGINES:
                if ops[e] or e == "sp":
                    getattr(block, attr[e])(mk(e))


class Builder:
    def __init__(self, depth, npt, stop_after=None):
        self.depth = depth
        self.npt = npt
        self.T = npt * 128 + 128
        self.ntile = npt + 1
        self.stop_after = stop_after
        nc = bass.Bass("TRN2", target_bir_lowering=False)
        self.nc = nc
        self.S = Sched(nc)
        self._wsi = 0
        self._gch = 0
        self.NBLK = -(-(2 * self.T + 32 * 127) // 128)
        self.NJ = self.ntile
        self.declare()
        self.program()
        self.S.emit()

    def TT(self, out, in0, in1, op, r, w, eng="dve"):
        self.S.add(eng, lambda e: e.tensor_tensor(out=out, in0=in0, in1=in1, op=op), r, w)

    def TS(self, out, in0, s1, s2, op0, op1, r, w, eng="dve"):
        if s2 is None:
            self.S.add(eng, lambda e: e.tensor_scalar(out=out, in0=in0, scalar1=s1, scalar2=None, op0=op0), r, w)
        else:
            self.S.add(eng, lambda e: e.tensor_scalar(out=out, in0=in0, scalar1=s1, scalar2=s2, op0=op0, op1=op1), r, w)

    def STT(self, out, in0, scalar, in1, op0, op1, r, w, eng="dve"):
        self.S.add(eng, lambda e: e.scalar_tensor_tensor(out=out, in0=in0, scalar=scalar, in1=in1, op0=op0, op1=op1), r, w)

    def CP(self, out, in_, r, w, eng="dve"):
        if eng == "act":
            self.S.add("act", lambda e: e.copy(out=out, in_=in_), r, w)
        else:
            self.S.add(eng, lambda e: e.tensor_copy(out=out, in_=in_), r, w)

    def ACT(self, out, in_, func, r, w, scale=None, bias=None):
        kw = {}
        if scale is not None:
            kw["scale"] = scale
        if bias is not None:
            kw["bias"] = bias
        self.S.add("act", lambda e: e.activation(out=out, in_=in_, func=func, **kw), r, w)

    def SQRT(self, ap, res):
        self.S.add("act", lambda e: e.sqrt(out=ap, in_=ap), [res], [res])

    def MM(self, mms, r, w):
        def fn(e):
            ins = None
            for (o, l, rh, st, sp) in mms:
                ins = e.matmul(o, lhsT=l, rhs=rh, start=st, stop=sp)
            return ins
        self.S.add("pe", fn, r, w)

    def TR(self, trs, r, w):
        def fn(e):
            ins = None
            for (o, i, idn) in trs:
                ins = e.transpose(o, i, idn)
            return ins
        self.S.add("pe", fn, r, w)

    def DMA(self, out, in_, r, w, q="sp", final=False):
        self.S.add(q, lambda e: e.dma_start(out=out, in_=in_), r, w, dma=True, out=final)

    def DMA4(self, out, in_, r, w, final=False):
        for c in range(out.shape[1]):
            self.DMA(out[:, c], in_[:, c], r, w, final=final)

    def MEMSET(self, ap, val, w, eng="dve"):
        self.S.add(eng, lambda e: e.memset(ap, val), (), w)

    def RSUM(self, out, in_, r, w):
        self.S.add("dve", lambda e: e.reduce_sum(out=out, in_=in_, axis=AX.X), r, w)

    def RMAX(self, out, in_, r, w):
        self.S.add("dve", lambda e: e.reduce_max(out=out, in_=in_, axis=AX.X), r, w)

    def RECIP(self, out, in_, r, w):
        self.S.add("dve", lambda e: e.reciprocal(out=out, in_=in_), r, w)

    def wload(self, src, kc, ncols):
        i = self._wsi % len(self.ws)
        self._wsi += 1
        t = self.ws[i]
        view = t[:, 0:kc * ncols].rearrange("p (k n) -> p k n", k=kc)
        self.DMA(view, src.rearrange("(k p) n -> p k n", p=128), (), [("ws", i)], q="pool")
        return view, ("ws", i)

    def declare(self):
        nc, D, T = self.nc, self.depth, self.T
        di = lambda n, s: nc.dram_tensor(n, s, F32, kind="ExternalInput").ap()
        do = lambda n, s: nc.dram_tensor(n, s, F32, kind="ExternalOutput").ap()
        self.d_xT = di("xT_in", [128, 8, T])
        self.d_cT = di("cT_in", [128, 8, 17])
        self.d_wada = di("w_ada", [D, 1024, 6144])
        self.d_badaT = di("b_adaT", [128, D, 48])
        self.d_win = di("w_in", [D, 1024, 9228])
        self.d_fvec = di("fvec", [128, D, NFV])
        self.d_rvec = di("rvec", [D, NRV])
        self.d_sgub = di("sgub", [D, 2, 512])
        self.d_sguw = di("sgu_wT", [D, 2, 128, 4, 128])
        self.d_poolw = di("pool_w", [D, 4, 128, 128])
        self.d_wbr = [di("w_br_a", [D, 512, 1024]), di("w_br_b", [D, 512, 1024]), di("w_br_c", [D, 512, 1024]),
                      di("w_br_d", [D, 768, 1024])]
        self.d_wo = di("w_o", [D, 1024, 1024])
        self.d_router = di("router", [D, 1024, 36])
        self.d_wegp = di("w_e_gate_p", [D * 4096, 4096])
        self.d_weup = di("w_e_up_p", [D * 4096, 4096])
        self.d_wedp = di("w_e_down_p", [D * 4096, 4096])
        self.d_crow = di("crow", [1, 32 + self.NBLK])
        self.d_pidx = di("pidx", [128, 1])
        self.d_buf = nc.dram_tensor("moe_buf", [self.NBLK * 128, 1024], BF16, kind="Internal").ap()
        self.d_ybuf = nc.dram_tensor("moe_ybuf", [self.NBLK * 128, 1024], F32, kind="Internal").ap()
        self.d_convS = di("conv_sT", [D, 128, 4, 16, 30])
        self.d_poolS = di("pool_sT", [D, 128, 4, 16, 15])
        self.d_sconvS = di("sconv_sT", [D, 128, 14, 16, 3])
        self.d_ssmS = di("ssm_sT", [D, 128, 16, 768])
        self.d_consts = di("consts", [128, 8, 128])
        self.d_m3 = di("m3", [128, 16])
        self.d_corr = di("corr", [128, 4, 15])
        self.o_yT = do("yT", [128, 8, T])
        self.o_convP = do("o_conv_p", [D, 128, 4, 30])
        self.o_poolP = do("o_pool_p", [D, 128, 4, 15])
        self.o_sconvP = do("o_sconv_p", [D, 128, 14, 3])
        self.o_ssmP = do("o_ssm_p", [D, 128, 768])
        self.o_convS = do("o_conv_s", [D, 128, 4, 16, 30])
        self.o_poolS = do("o_pool_s", [D, 128, 4, 16, 15])
        self.o_sconvS = do("o_sconv_s", [D, 128, 14, 16, 3])
        self.o_ssmS = do("o_ssm_s", [D, 128, 16, 768])
        self.o_vS = do("o_v_s", [D, 128, 512])
        self.o_dbg1 = do("dbg_desti", [128, self.ntile * 2])
        self.o_dbg2 = do("dbg_widx", [128, self.NBLK])
        self.o_dbg3 = do("dbg_gm", [128, 320])

        sb = lambda n, s, dt=F32: nc.alloc_sbuf_tensor(n, s, dt)
        self.xT = sb("xT", [128, 8, T])
        self.ws = [sb(f"ws{i}", [128, 4096], BF16) for i in range(NWS)]
        self.ada = sb("ada", [128, 48, 17])
        self.cTf = sb("cTf", [128, 8, 17])
        self.scT = sb("scT", [128, 8, 17], BF16)
        self.badaT = sb("badaT", [128, D, 48])
        self.fvec = sb("fvecs", [128, D, NFV])
        self.rvec = sb("rvecs", [128, NRV])
        self.sgub = sb("sgubs", [128, 512])
        self.consts = sb("constss", [128, 8, 128])
        self.identB = sb("identB", [128, 128], BF16)
        self.m3 = sb("m3s", [128, 16])
        self.corr = sb("corrs", [128, 4, 15])
        self.sguwB = sb("sguwB", [128, 2, 4, 128], BF16)
        self.poolw = sb("poolws", [128, 4, 128], BF16)
        self.routerw = sb("routerws", [128, 8, 36])
        self.arow = sb("arow", [128, 12])
        self.convc = sb("convc", [128, 4, 30])
        self.poolc = sb("poolc", [128, 4, 15])
        self.sconvc = sb("sconvc", [128, 14, 3])
        self.state = sb("state", [128, 768])
        self.stateB = sb("stateB", [128, 768], BF16)
        self.X2 = sb("X2", [128, 8, 128])
        self.sm = sb("sm", [128, 128])
        self.st1 = sb("st1", [128, 128])
        self.st2 = sb("st2", [128, 128])
        self.st3 = sb("st3", [128, 128])
        self.xm = sb("xm", [128, 8, 128], BF16)
        self.merged = sb("merged", [128, 8, 128])
        self.mergedB = sb("mergedB", [128, 8, 128], BF16)
        self.g4a = sb("g4a", [128, 4, 128])
        self.g4b = sb("g4b", [128, 4, 128])
        self.fa = sb("fa", [128, 1024])
        self.fb = sb("fb", [128, 1024])
        self.fc = sb("fc", [128, 1024])
        self.yk = sb("yk", [128, 6, 128], BF16)
        self.hb1 = sb("hb1", [128, 768], BF16)
        self.hb2 = sb("hb2", [128, 768], BF16)
        self.ext = sb("ext", [128, 4, 158])
        self.pext = sb("pext", [128, 4, 143])
        self.ptA = sb("ptA", [128, 143])
        self.ptB = sb("ptB", [128, 143])
        self.xext = [sb(f"xext{i}", [128, 4, 131]) for i in range(2)]
        self.xc = sb("xc", [128, 6, 128])
        self.bc = sb("bc", [128, 8, 128], BF16)
        self.btok = sb("btok", [128, 4, 128], BF16)
        self.dec = sb("dec", [128, 12, 128], BF16)
        self.cmT = sb("cmT", [128, 4, 128], BF16)
        NT, NBk = self.ntile, self.NBLK
        self.rt = sb("rt", [128, 256])
        self.OHall = sb("OHall", [128, NT, 64])
        self.Wv = sb("Wv", [128, NT, 2])
        self.rank = sb("rank", [128, NT, 2])
        self.desti = sb("desti", [128, NT, 2], mybir.dt.int32)
        self.carry = sb("carry", [128, 64])
        self.gm = sb("gm", [128, 320])
        self.blke = sb("blke", [128, NBk])
        self.widx = sb("widx", [128, NBk], mybir.dt.int32)
        self.crow = sb("crows", [128, 32 + NBk])
        self.pidx = sb("pidxs", [128, 1])
        self.xblk = [sb(f"xblk{i}", [128, 1024], BF16) for i in range(2)]
        self.xtok = self.xblk
        self.xbT = sb("xbT", [128, 8, 128], BF16)
        self.hidtok = sb("hidtok", [128, 512], BF16)
        self.hidT = sb("hidT", [128, 4, 128], BF16)
        self.yblk = [self.fa, self.fc]
        self.psf = [nc.alloc_psum_tensor(f"psf{i}", [128, 512], F32) for i in range(7)]
        self.psb = nc.alloc_psum_tensor("psb", [128, 1024], BF16)

    def program(self):
        D = self.depth
        cs = self.consts
        self.identF, self.maskP, self.maskS = cs[:, 0, :], cs[:, 1, :], cs[:, 2, :]
        self.negP, self.negS, self.bdS, self.onesF = cs[:, 3, :], cs[:, 4, :], cs[:, 5, :], cs[:, 6, :]
        self.triS = cs[:, 7, :]
        self.DMA(self.crow[:], self.d_crow.partition_broadcast(128), (), ["crow"])
        self.DMA(self.pidx[:], self.d_pidx, (), ["pidx"])
        self.MEMSET(self.xblk[0][:], 0.0, [("xblk", 0)])
        for b in range(self.NBLK):
            self.DMA(self.d_buf[b * 128:(b + 1) * 128, :], self.xblk[0][:], [("xblk", 0)], ["bufz"])
        self.xres = [("xT", t) for t in range(self.ntile)]
        self.DMA(self.xT[:], self.d_xT, (), self.xres)
        self.DMA(self.cTf[:], self.d_cT, (), ["cTf"])
        self.DMA(self.consts[:], self.d_consts, (), ["consts"])
        self.DMA(self.m3[:], self.d_m3, (), ["m3"])
        self.DMA(self.corr[:], self.d_corr, (), ["corr"])
        self.DMA(self.badaT[:], self.d_badaT, (), ["badaT"])
        self.DMA(self.fvec[:], self.d_fvec, (), ["fvec"])
        self.CP(self.identB[:], self.identF, ["consts"], ["identB"])
        self.ACT(self.scT[:], self.cTf[:], AF.Silu, ["cTf"], ["scT"])
        for l in range(D):
            self.layer(l)
        self.DMA(self.o_yT, self.xT[:], self.xres, (), final=True)

    def fv(self, l, off, n=1):
        return self.fvec[:, l, off:off + n]

    def layer(self, l):
        self.DMA(self.rvec[:], self.d_rvec[l:l + 1, :].partition_broadcast(128), (), ["rvec"])
        self.DMA(self.sgub[:], self.d_sgub[l, 0:1, :].partition_broadcast(128), (), ["sgub"])
        self.DMA(self.poolw[:], self.d_poolw[l].rearrange("g c d -> c g d"), (), ["poolw"], q="pool")
        self.DMA(self.routerw[:], self.d_router[l].rearrange("(k p) n -> p k n", p=128), (), ["routerw"])
        for v in range(2):
            stg = self.X2[:, 0:4, :]
            self.DMA(stg, self.d_sguw[l, v], (), ["X2"])
            m = self.maskP if v == 0 else self.maskS
            self.TT(self.sguwB[:, v], stg, m.unsqueeze(1).to_broadcast([128, 4, 128]), ALU.mult,
                    ["X2", "consts"], ["sguwB"])
        self.ACT(self.arow[:], self.rvec[:, RV_ALOG:RV_ALOG + 12], AF.Exp, ["rvec"], ["arow"])
        self.TS(self.arow[:], self.arow[:], -1.0, None, ALU.mult, None, ["arow"], ["arow"])
        for fb in range(12):
            wv, wr = self.wload(self.d_wada[l][:, fb * 512:(fb + 1) * 512], 8, 512)
            ps = self.psf[fb % 2]
            mms = []
            for j in range(4):
                for dc in range(8):
                    mms.append((ps[:, j * 32:j * 32 + 17], wv[:, dc, j * 128:(j + 1) * 128], self.scT[:, dc, :],
                                dc == 0, dc == 7))
            self.MM(mms, [wr, "scT"], [("ps", fb % 2)])
            for j in range(4):
                fcx = fb * 4 + j
                self.ACT(self.ada[:, fcx, :], ps[:, j * 32:j * 32 + 17], AF.Identity, [("ps", fb % 2), "badaT"],
                         ["ada"], bias=self.badaT[:, l, fcx:fcx + 1])
        self.TS(self.ada[:, 8:16, :], self.ada[:, 8:16, :], 1.0, None, ALU.add, None, ["ada"], ["ada"])
        self.TS(self.ada[:, 32:40, :], self.ada[:, 32:40, :], 1.0, None, ALU.add, None, ["ada"], ["ada"])
        self.MEMSET(self.convc[:], 0.0, ["convc"])
        self.MEMSET(self.poolc[:], 0.0, ["poolc"])
        self.MEMSET(self.sconvc[:], 0.0, ["sconvc"])
        self.MEMSET(self.state[:], 0.0, ["state"])
        self.MEMSET(self.stateB[:], 0.0, ["stateB"])
        self.MEMSET(self.carry[:], 0.0, ["carry"])
        for ti in range(self.ntile):
            if ti == self.npt:
                self.DMA(self.sgub[:], self.d_sgub[l, 1:2, :].partition_broadcast(128), (), ["sgub"])
            self.mix_tile(l, ti)
            if self.stop_after != "mix":
                self.route_tile(l, ti)
        if self.stop_after == "mix":
            return
        self.moe(l)

    def modulate(self, ti, sh0, sc0, out, wr):
        t0 = ti * 128
        kindS = ti == self.npt
        xr = ("xT", ti)
        for c in range(8):
            xin = self.xT[:, c, t0:t0 + 128]
            if not kindS:
                self.ACT(out[:, c, :], xin, AF.Identity, [xr, "ada"], wr, scale=self.ada[:, sc0 + c, 0:1],
                         bias=self.ada[:, sh0 + c, 0:1])
            else:
                v3 = lambda a: a.rearrange("p (s j) -> p s j", s=16)
                scb = self.ada[:, sc0 + c, 1:17].unsqueeze(2).to_broadcast([128, 16, 8])
                shb = self.ada[:, sh0 + c, 1:17].unsqueeze(2).to_broadcast([128, 16, 8])
                self.TT(v3(self.st1[:]), v3(xin), scb, ALU.mult, [xr, "ada"], ["st1"])
                self.TT(v3(out[:, c, :]), v3(self.st1[:]), shb, ALU.add, ["st1", "ada"], wr)

    def ln_feat(self, src, nch, rd, g_off, b_off, l, dst_fn, dst_res, func=AF.Identity):
        n = nch * 128
        sq = self.X2[:, 0:nch, :]
        self.ACT(sq, src, AF.Square, rd, ["X2"])
        p1, p2 = self.psf[5], self.psf[6]
        self.MM([(p1[:, 0:128], self.onesF, src[:, c, :], c == 0, c == nch - 1) for c in range(nch)], rd + ["consts"],
                [("ps", 5)])
        self.MM([(p2[:, 0:128], self.onesF, sq[:, c, :], c == 0, c == nch - 1) for c in range(nch)], ["X2", "consts"],
                [("ps", 6)])
        mean, rstd, tmp = self.st1[:], self.st2[:], self.st3[:]
        self.TS(mean, p1[:, 0:128], 1.0 / n, None, ALU.mult, None, [("ps", 5)], ["st1"])
        self.TT(tmp, mean, mean, ALU.mult, ["st1"], ["st3"])
        self.STT(rstd, p2[:, 0:128], 1.0 / n, tmp, ALU.mult, ALU.subtract, [("ps", 6), "st3"], ["st2"])
        self.TS(rstd, rstd, LN_EPS, None, ALU.add, None, ["st2"], ["st2"])
        self.SQRT(rstd, "st2")
        self.RECIP(rstd, rstd, ["st2"], ["st2"])
        self.TT(src, src, mean.unsqueeze(1).to_broadcast([128, nch, 128]), ALU.subtract, rd + ["st1"], rd)
        self.TT(src, src, rstd.unsqueeze(1).to_broadcast([128, nch, 128]), ALU.mult, rd + ["st2"], rd)
        for c in range(nch):
            self.ACT(dst_fn(c), src[:, c, :], func, rd + ["fvec"], dst_res, scale=self.fv(l, g_off + c),
                     bias=self.fv(l, b_off + c))

    def merge_branch(self, l, k, ykT, kc, rd):
        for h in range(2):
            wg, wgr = self.wload(self.d_win[l][:, C_GL + k * 1024 + h * 512: C_GL + k * 1024 + (h + 1) * 512], 8, 512)
            pg = self.psf[2]
            mms = []
            for mc in range(4):
                for dc in range(8):
                    mms.append((pg[:, mc * 128:(mc + 1) * 128], wg[:, dc, mc * 128:(mc + 1) * 128], self.xm[:, dc, :],
                                dc == 0, dc == 7))
            self.MM(mms, [wgr, "xm"], [("ps", 2)])
            sig = self.fa[:, 0:512]
            self.ACT(sig, pg[:], AF.Sigmoid, [("ps", 2)], ["fa"])
            wb, wbr = self.wload(self.d_wbr[k][l][:, h * 512:(h + 1) * 512], kc, 512)
            pb = self.psf[3]
            mms = []
            for mc in range(4):
                for c in range(kc):
                    mms.append((pb[:, mc * 128:(mc + 1) * 128], wb[:, c, mc * 128:(mc + 1) * 128], ykT[:, c, :],
                                c == 0, c == kc - 1))
            self.MM(mms, [wbr] + rd, [("ps", 3)])
            mg = self.merged[:, h * 4:(h + 1) * 4, :].rearrange("p c t -> p (c t)")
            if k == 0:
                self.TT(mg, sig, pb[:], ALU.mult, ["fa", ("ps", 3)], [("merged", h)])
            else:
                self.TT(sig, sig, pb[:], ALU.mult, ["fa", ("ps", 3)], ["fa"])
                self.TT(mg, mg, sig, ALU.add, ["fa", ("merged", h)], [("merged", h)])

    def mix_tile(self, l, ti):
        kindS = ti == self.npt
        lastP = ti == self.npt - 1
        t0 = ti * 128
        L = 8 if kindS else 128
        xr = ("xT", ti)
        ps = self.psf
        xm = self.xm
        sm = self.sm
        yk = self.yk
        h3 = lambda a: a.rearrange("p (h d) -> p h d", h=12)
        c4 = lambda a, c=4: a.rearrange("p (c t) -> p c t", c=c)
        self.modulate(ti, 0, 8, xm, ["xm"])

        def proj_feat(col0, ncols, pst, psr):
            wv, wr = self.wload(self.d_win[l][:, col0:col0 + ncols], 8, ncols)
            mms = []
            for fc in range(ncols // 128):
                for dc in range(8):
                    mms.append((pst[:, fc * 128:(fc + 1) * 128], wv[:, dc, fc * 128:(fc + 1) * 128], xm[:, dc, :], dc == 0,
                                dc == 7))
            self.MM(mms, [wr, "xm"], [psr])

        def proj_tok(col0, ncols, pst, psr):
            wv, wr = self.wload(self.d_win[l][:, col0:col0 + ncols], 8, ncols)
            self.MM([(pst[:, 0:ncols], xm[:, dc, :], wv[:, dc, :], dc == 0, dc == 7) for dc in range(8)], [wr, "xm"], [psr])

        def seq_groups(per):
            return [(0, 1)] if not kindS else [(s, per) for s in range(0, 16, per)]

        proj_feat(C_U, 512, ps[0], ("ps", 0))
        gu = self.g4a[:].rearrange("p c t -> p (c t)")
        self.ACT(gu, ps[0][:], AF.Gelu_apprx_tanh, [("ps", 0)], ["g4a"])
        proj_tok(C_V, 512, ps[1], ("ps", 1))
        vg = self.fa[:, 0:512]
        self.ACT(vg, ps[1][:], AF.Gelu_apprx_tanh, [("ps", 1)], ["fa"])
        self.RSUM(sm[:, 0:1], vg, ["fa"], ["sm"])
        self.TT(self.fb[:, 0:512], vg, vg, ALU.mult, ["fa"], ["fb"])
        self.RSUM(sm[:, 1:2], self.fb[:, 0:512], ["fb"], ["sm"])
        self.TS(sm[:, 0:1], sm[:, 0:1], 1.0 / 512, None, ALU.mult, None, ["sm"], ["sm"])
        self.TT(sm[:, 2:3], sm[:, 0:1], sm[:, 0:1], ALU.mult, ["sm"], ["sm"])
        self.STT(sm[:, 3:4], sm[:, 1:2], 1.0 / 512, sm[:, 2:3], ALU.mult, ALU.subtract, ["sm"], ["sm"])
        self.TS(sm[:, 3:4], sm[:, 3:4], LN_EPS, None, ALU.add, None, ["sm"], ["sm"])
        self.SQRT(sm[:, 3:4], "sm")
        self.RECIP(sm[:, 3:4], sm[:, 3:4], ["sm"], ["sm"])
        self.TS(vg, vg, sm[:, 0:1], sm[:, 3:4], ALU.subtract, ALU.mult, ["fa", "sm"], ["fa"])
        self.TT(vg, vg, self.rvec[:, RV_SLG:RV_SLG + 512], ALU.mult, ["fa", "rvec"], ["fa"])
        self.TT(vg, vg, self.rvec[:, RV_SLB:RV_SLB + 512], ALU.add, ["fa", "rvec"], ["fa"])
        if kindS:
            self.DMA(self.o_vS[l], vg, ["fa"], (), final=True)
        vnb = self.hb1[:, 0:512]
        self.CP(vnb, vg, ["fa"], ["hb1"], eng="act")
        sw = self.sguwB[:, 1 if kindS else 0]
        self.MM([(ps[0][:, g * 128:(g + 1) * 128], vnb[:, g * 128:(g + 1) * 128], sw[:, g, :], True, True) for g in range(4)],
                ["hb1", "sguwB"], [("ps", 0)])
        self.TT(self.fb[:, 0:512], ps[0][:], self.sgub[:], ALU.add, [("ps", 0), "sgub"], ["fb"])
        self.TT(yk[:, 0:4, :].rearrange("p c t -> p (c t)"), self.fb[:, 0:512], gu, ALU.mult, ["fb", "g4a"], ["yk"])
        self.merge_branch(l, 0, yk, 4, ["yk"])

        proj_feat(C_GA, 512, ps[0], ("ps", 0))
        proj_feat(C_GG, 512, ps[1], ("ps", 1))
        self.ACT(self.g4a[:].rearrange("p c t -> p (c t)"), ps[1][:], AF.Sigmoid, [("ps", 1)], ["g4a"])
        E = 30 + L
        acc = self.g4b
        accr = [("g4b", c) for c in range(4)]
        for (s0, ns) in seq_groups(4):
            a0, n = s0 * L, ns * L
            ext = self.ext[:, :, 0:ns * E].rearrange("p c (s e) -> p c s e", s=ns)
            sv = lambda a: a[:, :, a0:a0 + n].rearrange("p c (s j) -> p c s j", s=ns)
            if kindS:
                self.DMA4(ext[:, :, :, 0:30], self.d_convS[l][:, :, s0:s0 + ns, :], (), ["ext"])
            else:
                self.CP(ext[:, :, 0, 0:30], self.convc[:], ["convc"], ["ext"])
            self.TT(ext[:, :, :, 30:E], sv(c4(ps[0][:])), sv(self.g4a[:]), ALU.mult, [("ps", 0), "g4a"], ["ext"])
            if kindS:
                self.DMA4(self.o_convS[l][:, :, s0:s0 + ns, :], ext[:, :, :, L:E], ["ext"], (), final=True)
            else:
                self.CP(self.convc[:], ext[:, :, 0, L:E], ["ext"], ["convc"])
                if lastP:
                    self.DMA(self.o_convP[l], self.convc[:], ["convc"], (), final=True)
            for k in range(31):
                for c in range(4):
                    a3 = acc[:, c, a0:a0 + n].rearrange("p (s j) -> p s j", s=ns)
                    wk = self.fv(l, FV_CW + c * 31 + k)
                    if k == 0:
                        self.TS(a3, ext[:, c, :, 0:L], wk, self.fv(l, FV_CONVB + c), ALU.mult, ALU.add, ["ext", "fvec"],
                                [("g4b", c)])
                    else:
                        self.STT(a3, ext[:, c, :, k:k + L], wk, a3, ALU.mult, ALU.add, ["ext", "fvec", ("g4b", c)],
                                 [("g4b", c)])
        self.ln_feat(acc[:], 4, accr, FV_CLNG, FV_CLNB, l, lambda c: yk[:, c, :], ["yk"], func=AF.Silu)
        self.merge_branch(l, 1, yk, 4, ["yk"])

        proj_feat(C_PIN, 512, ps[0], ("ps", 0))
        E = 15 + L
        mixed = self.hb2[:, 0:512].rearrange("p (c t) -> p c t", c=4)
        for (s0, ns) in seq_groups(4):
            a0, n = s0 * L, ns * L
            pext = self.pext[:, :, 0:ns * E].rearrange("p c (s e) -> p c s e", s=ns)
            sv = lambda a: a[:, :, a0:a0 + n].rearrange("p c (s j) -> p c s j", s=ns)
            if kindS:
                self.DMA4(pext[:, :, :, 0:15], self.d_poolS[l][:, :, s0:s0 + ns, :], (), ["pext"])
            else:
                self.CP(pext[:, :, 0, 0:15], self.poolc[:], ["poolc"], ["pext"])
            self.CP(pext[:, :, :, 15:E], sv(c4(ps[0][:])), [("ps", 0)], ["pext"], eng="act")
            if kindS:
                self.DMA4(self.o_poolS[l][:, :, s0:s0 + ns, :], pext[:, :, :, L:E], ["pext"], (), final=True)
            else:
                self.CP(self.poolc[:], pext[:, :, 0, L:E], ["pext"], ["poolc"])
                if lastP:
                    self.DMA(self.o_poolP[l], self.poolc[:], ["poolc"], (), final=True)
            tA = self.ptA[:, 0:ns * E].rearrange("p (s e) -> p s e", s=ns)
            tB = self.ptB[:, 0:ns * E].rearrange("p (s e) -> p s e", s=ns)
            for gi in range(4):
                src, srcr = pext[:, gi], "pext"
                bufs = [(tA, "ptA"), (tB, "ptB")]
                sh = 1
                for step in range(gi + 1):
                    dst, dstr = bufs[step % 2]
                    lo = 2 * sh - 1
                    self.TT(dst[:, :, lo:E], src[:, :, lo:E], src[:, :, lo - sh:E - sh], ALU.add, [srcr], [dstr])
                    src, srcr = dst, dstr
                    sh *= 2
                win = 2 ** (gi + 1)
                if (not kindS) and ti == 0:
                    self.TT(src[:, 0, 15:30], src[:, 0, 15:30], self.corr[:, gi, :], ALU.mult, [srcr, "corr"], [srcr])
                mo = mixed[:, gi, a0:a0 + n].rearrange("p (s j) -> p s j", s=ns)
                self.STT(mo, src[:, :, 15:E], 1.0 / win, pext[:, gi, :, 15:E], ALU.mult, ALU.subtract, [srcr, "pext"], ["hb2"])
        self.MM([(ps[1][:, g * 128:(g + 1) * 128], self.poolw[:, g, :], mixed[:, g, :], True, True) for g in range(4)],
                ["hb2", "poolw"], [("ps", 1)])
        self.TT(yk[:, 0:4, :], c4(ps[1][:]), self.fv(l, FV_PSC, 4).unsqueeze(2).to_broadcast([128, 4, 128]), ALU.mult,
                [("ps", 1), "fvec"], ["yk"])
        self.merge_branch(l, 2, yk, 4, ["yk"])

        proj_tok(C_ZZ, 512, ps[4], ("ps", 4))
        proj_tok(C_ZZ + 512, 256, ps[5], ("ps", 5))
        zs = self.fa[:, 0:768]
        self.ACT(zs[:, 0:512], ps[4][:], AF.Silu, [("ps", 4)], ["fa"])
        self.ACT(zs[:, 512:768], ps[5][:, 0:256], AF.Silu, [("ps", 5)], ["fa"])
        E = 3 + L
        xc = self.xc
        for j, (c0, nco) in enumerate([(0, 512), (512, 512), (1024, 512), (1536, 256)]):
            pst, psr = ps[j % 2], ("ps", j % 2)
            proj_feat(C_XBC + c0, nco, pst, psr)
            nch = nco // 128
            ch0 = c0 // 128
            xe, xer = self.xext[j % 2], ("xext", j % 2)
            for (s0, ns) in seq_groups(8):
                a0, n = s0 * L, ns * L
                xext = xe[:, 0:nch, 0:ns * E].rearrange("p c (s e) -> p c s e", s=ns)
                if kindS:
                    self.DMA4(xext[:, :, :, 0:3], self.d_sconvS[l][:, ch0:ch0 + nch, s0:s0 + ns, :], (), [xer])
                else:
                    self.CP(xext[:, :, 0, 0:3], self.sconvc[:, ch0:ch0 + nch, :], ["sconvc"], [xer])
                self.CP(xext[:, :, :, 3:E], c4(pst[:, 0:nco], nch)[:, :, a0:a0 + n].rearrange("p c (s j) -> p c s j", s=ns),
                        [psr], [xer], eng="act")
                if kindS:
                    self.DMA4(self.o_sconvS[l][:, ch0:ch0 + nch, s0:s0 + ns, :], xext[:, :, :, L:E], [xer], (), final=True)
                else:
                    self.CP(self.sconvc[:, ch0:ch0 + nch, :], xext[:, :, 0, L:E], [xer], ["sconvc"])
                for k in range(4):
                    for cc in range(nch):
                        c = ch0 + cc
                        a3 = self.g4b[:, cc, a0:a0 + n].rearrange("p (s j) -> p s j", s=ns)
                        wk = self.fv(l, FV_SCW + c * 4 + k)
                        if k == 0:
                            self.TS(a3, xext[:, cc, :, 0:L], wk, self.fv(l, FV_SCB + c), ALU.mult, ALU.add, [xer, "fvec"],
                                    [("g4b", cc)])
                        else:
                            self.STT(a3, xext[:, cc, :, k:k + L], wk, a3, ALU.mult, ALU.add, [xer, "fvec", ("g4b", cc)],
                                     [("g4b", cc)])
            gr = [("g4b", cc) for cc in range(nch)]
            if j == 0:
                self.ACT(xc[:, 0:4, :], self.g4b[:, 0:4, :], AF.Silu, gr, ["xc"])
            elif j == 1:
                self.ACT(xc[:, 4:6, :], self.g4b[:, 0:2, :], AF.Silu, gr, ["xc"])
                self.ACT(self.bc[:, 0:2, :], self.g4b[:, 2:4, :], AF.Silu, gr, ["bc"])
            elif j == 2:
                self.ACT(self.bc[:, 2:6, :], self.g4b[:, 0:4, :], AF.Silu, gr, ["bc"])
            else:
                self.ACT(self.bc[:, 6:8, :], self.g4b[:, 0:2, :], AF.Silu, gr, ["bc"])
        if lastP:
            self.DMA(self.o_sconvP[l], self.sconvc[:], ["sconvc"], (), final=True)
        wv, wr = self.wload(self.d_win[l][:, C_DT:C_DT + 12], 8, 12)
        self.MM([(ps[2][:, 0:12], xm[:, dc, :], wv[:, dc, :], dc == 0, dc == 7) for dc in range(8)], [wr, "xm"], [("ps", 2)])
        dt, dta = sm[:, 8:20], sm[:, 20:32]
        self.TT(dt, ps[2][:, 0:12], self.rvec[:, RV_DTB:RV_DTB + 12], ALU.add, [("ps", 2), "rvec"], ["sm"])
        self.ACT(dt, dt, AF.Exp, ["sm"], ["sm"])
        self.TS(dt, dt, 1.0, None, ALU.add, None, ["sm"], ["sm"])
        self.ACT(dt, dt, AF.Ln, ["sm"], ["sm"])
        self.TT(dta, dt, self.arow[:], ALU.mult, ["sm", "arow"], ["sm"])
        self.TR([(ps[4][:, c * 128:(c + 1) * 128], xc[:, c, :], self.identF) for c in range(4)], ["xc", "consts"], [("ps", 4)])
        self.TR([(ps[5][:, (c - 4) * 128:(c - 3) * 128], xc[:, c, :], self.identF) for c in range(4, 6)], ["xc", "consts"],
                [("ps", 5)])
        xs = self.fb[:, 0:768]
        self.CP(xs[:, 0:512], ps[4][:], [("ps", 4)], ["fb"], eng="act")
        self.CP(xs[:, 512:768], ps[5][:, 0:256], [("ps", 5)], ["fb"], eng="act")
        self.TR([(self.psb[:, g * 128:(g + 1) * 128], self.bc[:, g, :], self.identB[:]) for g in range(4)], ["bc", "identB"],
                [("psb", 0)])
        self.CP(self.btok[:].rearrange("p g t -> p (g t)"), self.psb[:, 0:512], [("psb", 0)], ["btok"])
        xd = self.hb1
        self.TT(h3(xd[:]), h3(xs[:]), dt.unsqueeze(2).to_broadcast([128, 12, 64]), ALU.mult, ["fb", "sm"], ["hb1"])
        tri = self.maskS if kindS else self.maskP
        bd = self.bdS if kindS else self.onesF
        neg = self.negS if kindS else self.negP
        self.MM([(ps[2][:, 16:28], tri, dta, True, True), (ps[2][:, 28:40], bd, dta, True, True)], ["sm", "consts"], [("ps", 2)])
        acs, tot, nacs, eacs, dst, et = sm[:, 32:44], sm[:, 44:56], sm[:, 56:68], sm[:, 68:80], sm[:, 80:92], sm[:, 92:104]
        self.CP(sm[:, 32:56], ps[2][:, 16:40], [("ps", 2)], ["sm"])
        self.TS(nacs, acs, -1.0, None, ALU.mult, None, ["sm"], ["sm"])
        self.ACT(eacs, acs, AF.Exp, ["sm"], ["sm"])
        self.TT(dst, tot, acs, ALU.subtract, ["sm"], ["sm"])
        self.ACT(dst, dst, AF.Exp, ["sm"], ["sm"])
        pxi = [0, 1, 3]
        for b3 in range(3):
            pxt, pxr = ps[pxi[b3]], ("ps", pxi[b3])
            mms = []
            for hh in range(4):
                h = b3 * 4 + hh
                o = pxt[:, hh * 128:(hh + 1) * 128]
                mms.append((o, dta[:, h:h + 1].to_broadcast([128, 128]), tri, True, False))
                mms.append((o, self.identF, neg, False, True))
            self.MM(mms, ["sm", "consts"], [pxr])
            for hh in range(4):
                h = b3 * 4 + hh
                self.ACT(self.dec[:, h, :], pxt[:, hh * 128:(hh + 1) * 128], AF.Exp, [pxr, "sm"], [("dec", h // 3)],
                         bias=nacs[:, h:h + 1])
        self.MM([(ps[2][:, g * 128:(g + 1) * 128], self.bc[:, g, :], self.bc[:, 4 + g, :], True, True) for g in range(4)], ["bc"],
                [("ps", 2)])
        for g in range(4):
            self.TT(self.dec[:, 3 * g:3 * g + 3, :], ps[2][:, g * 128:(g + 1) * 128].unsqueeze(1).to_broadcast([128, 3, 128]),
                    self.dec[:, 3 * g:3 * g + 3, :], ALU.mult, [("ps", 2), ("dec", g)], [("dec", g)])
        decr = [("dec", g) for g in range(4)]
        pyo = lambda pa, pb, h: (pa if h < 8 else pb)[:, (h % 8) * 64:(h % 8) * 64 + 64]
        self.MM([(pyo(ps[4], ps[5], h), self.dec[:, h, :], xd[:, h * 64:(h + 1) * 64], True, True) for h in range(12)],
                decr + ["hb1"], [("ps", 4), ("ps", 5)])
        xdd = self.hb2
        self.TT(h3(xdd[:]), h3(xd[:]), dst.unsqueeze(2).to_broadcast([128, 12, 64]), ALU.mult, ["hb1", "sm"], ["hb2"])
        if not kindS:
            self.MM([(pyo(ps[6], ps[2], h), self.bc[:, 4 + h // 3, :], self.stateB[:, h * 64:(h + 1) * 64], True, True)
                     for h in range(12)], ["bc", "stateB"], [("ps", 6), ("ps", 2)])
        else:
            sgi, sgB, sout, xddm = self.state, self.stateB, self.fc[:, 0:768], self.hb1
            for i in range(16):
                self.DMA(sgi[:], self.d_ssmS[l][:, i, :], (), ["state"])
                self.CP(sgB[:], sgi[:], ["state"], ["stateB"], eng="act")
                self.MEMSET(self.cmT[:], 0.0, ["cmT"])
                self.CP(self.cmT[:, :, i * 8:(i + 1) * 8], self.bc[:, 4:8, i * 8:(i + 1) * 8], ["bc"], ["cmT"])
                self.MM([(pyo(ps[6], ps[2], h), self.cmT[:, h // 3, :], sgB[:, h * 64:(h + 1) * 64], i == 0, i == 15)
                         for h in range(12)], ["cmT", "stateB"], [("ps", 6), ("ps", 2)])
                self.TS(xddm[:], xdd[:], self.m3[:, i:i + 1], None, ALU.mult, None, ["hb2", "m3"], ["hb1"])
                self.MM([(pyo(ps[0], ps[1], h), self.btok[:, h // 3, :], xddm[:, h * 64:(h + 1) * 64], True, True)
                         for h in range(12)], ["btok", "hb1"], [("ps", 0), ("ps", 1)])
                self.MM([(ps[3][:, 0:12], self.m3[:, i:i + 1].to_broadcast([128, 128]), dta, True, True)], ["m3", "sm"],
                        [("ps", 3)])
                self.ACT(et, ps[3][:, 0:12], AF.Exp, [("ps", 3)], ["sm"])
                self.TT(h3(sout[:]), h3(sgi[:]), et.unsqueeze(2).to_broadcast([128, 12, 64]), ALU.mult, ["state", "sm"], ["fc"])
                self.TT(sout[:, 0:512], sout[:, 0:512], ps[0][:], ALU.add, ["fc", ("ps", 0)], ["fc"])
                self.TT(sout[:, 512:768], sout[:, 512:768], ps[1][:, 0:256], ALU.add, ["fc", ("ps", 1)], ["fc"])
                self.DMA(self.o_ssmS[l][:, i, :], sout[:], ["fc"], (), final=True)
        y = self.fc[:, 0:768]
        self.TT(h3(y[:])[:, 0:8, :], ps[6][:].rearrange("p (h d) -> p h d", h=8),
                eacs[:, 0:8].unsqueeze(2).to_broadcast([128, 8, 64]), ALU.mult, [("ps", 6), "sm"], ["fc"])
        self.TT(h3(y[:])[:, 8:12, :], ps[2][:, 0:256].rearrange("p (h d) -> p h d", h=4),
                eacs[:, 8:12].unsqueeze(2).to_broadcast([128, 4, 64]), ALU.mult, [("ps", 2), "sm"], ["fc"])
        self.TT(y[:, 0:512], y[:, 0:512], ps[4][:], ALU.add, ["fc", ("ps", 4)], ["fc"])
        self.TT(y[:, 512:768], y[:, 512:768], ps[5][:, 0:256], ALU.add, ["fc", ("ps", 5)], ["fc"])
        self.TT(h3(xs[:]), h3(xs[:]), self.rvec[:, RV_D:RV_D + 12].unsqueeze(2).to_broadcast([128, 12, 64]), ALU.mult,
                ["fb", "rvec"], ["fb"])
        self.TT(y[:], y[:], xs[:], ALU.add, ["fc", "fb"], ["fc"])
        self.TT(y[:], y[:], zs[:], ALU.mult, ["fc", "fa"], ["fc"])
        self.TT(xs[:], y[:], y[:], ALU.mult, ["fc"], ["fb"])
        self.RSUM(sm[:, 4:5], xs[:], ["fb"], ["sm"])
        self.TS(sm[:, 4:5], sm[:, 4:5], 1.0 / 768, LN_EPS, ALU.mult, ALU.add, ["sm"], ["sm"])
        self.SQRT(sm[:, 4:5], "sm")
        self.RECIP(sm[:, 4:5], sm[:, 4:5], ["sm"], ["sm"])
        yd = self.hb1
        self.TS(yd[:], y[:], sm[:, 4:5], None, ALU.mult, None, ["fc", "sm"], ["hb1"])
        self.TR([(self.psb[:, c * 128:(c + 1) * 128], yd[:, c * 128:(c + 1) * 128], self.identB[:]) for c in range(6)],
                ["hb1", "identB"], [("psb", 0), ("psb", 1)])
        for c in range(6):
            self.ACT(yk[:, c, :], self.psb[:, c * 128:(c + 1) * 128], AF.Identity, [("psb", 0), ("psb", 1), "fvec"], ["yk"],
                     scale=self.fv(l, FV_NG + c))
        if not kindS:
            self.MM([(pyo(ps[0], ps[1], h), self.btok[:, h // 3, :], xdd[:, h * 64:(h + 1) * 64], True, True) for h in range(12)],
                    ["btok", "hb2"], [("ps", 0), ("ps", 1)])
            self.ACT(et, tot, AF.Exp, ["sm"], ["sm"])
            self.TT(h3(self.state[:]), h3(self.state[:]), et.unsqueeze(2).to_broadcast([128, 12, 64]), ALU.mult,
                    ["state", "sm"], ["state"])
            self.TT(self.state[:, 0:512], self.state[:, 0:512], ps[0][:], ALU.add, ["state", ("ps", 0)], ["state"])
            self.TT(self.state[:, 512:768], self.state[:, 512:768], ps[1][:, 0:256], ALU.add, ["state", ("ps", 1)], ["state"])
            self.CP(self.stateB[:], self.state[:], ["state"], ["stateB"], eng="act")
            if lastP:
                self.DMA(self.o_ssmP[l], self.state[:], ["state"], (), final=True)
        self.merge_branch(l, 3, yk, 6, ["yk"])

        self.CP(self.mergedB[:], self.merged[:], [("merged", 0), ("merged", 1)], ["mergedB"], eng="act")
        xt = self.xT[:, :, t0:t0 + 128]
        self.TS(xt, xt, DN_ALPHA, None, ALU.mult, None, [xr, "xm"], [xr])
        for h in range(2):
            wo, wor = self.wload(self.d_wo[l][:, h * 512:(h + 1) * 512], 8, 512)
            pm, pmr = ps[h], ("ps", h)
            mms = []
            for mc in range(4):
                for c in range(8):
                    mms.append((pm[:, mc * 128:(mc + 1) * 128], wo[:, c, mc * 128:(mc + 1) * 128], self.mergedB[:, c, :], c == 0,
                                c == 7))
            self.MM(mms, [wor, "mergedB"], [pmr])
            if not kindS:
                for mc in range(4):
                    c = h * 4 + mc
                    self.STT(xt[:, c, :], pm[:, mc * 128:(mc + 1) * 128], self.ada[:, 16 + c, 0:1], xt[:, c, :], ALU.mult, ALU.add,
                             [pmr, "ada", xr], [xr])
            else:
                g1b = self.ada[:, 16 + h * 4:16 + h * 4 + 4, 1:17].unsqueeze(3).to_broadcast([128, 4, 16, 8])
                tmp = self.g4a[:].rearrange("p c (s j) -> p c s j", s=16)
                self.TT(tmp, pm[:].rearrange("p (c s j) -> p c s j", c=4, s=16), g1b, ALU.mult, [pmr, "ada"], ["g4a"])
                self.TT(xt[:, h * 4:h * 4 + 4, :], xt[:, h * 4:h * 4 + 4, :], self.g4a[:], ALU.add, ["g4a", xr], [xr])
        self.ln_feat(xt, 8, [xr], FV_L1G, FV_L1B, l, lambda c: xt[:, c, :], [xr])

    def route_tile(self, l, ti):
        ps = self.psf
        rt = self.rt
        X2 = self.X2
        self.modulate(ti, 24, 32, X2, ["X2"])
        self.MM([(ps[0][:, 0:36], X2[:, dc, :], self.routerw[:, dc, :], dc == 0, dc == 7) for dc in range(8)],
                ["X2", "routerw"], [("ps", 0)])
        lg = rt[:, 0:36]
        self.CP(lg, ps[0][:, 0:36], [("ps", 0)], ["rt"])
        R = lambda a, b: rt[:, a:b]
        mg, nmg, ohg, eg, sg_, les, v1 = R(36, 37), R(37, 38), R(40, 44), R(44, 48), R(48, 49), R(52, 60), R(60, 61)
        oh1, le2, v2, oh2 = R(64, 72), R(72, 80), R(80, 81), R(84, 92)
        e2, w1, w2, tmp32 = R(92, 93), R(93, 94), R(94, 95), R(112, 144)
        rr = ["rt"]
        self.RMAX(mg, lg[:, 0:4], rr, rr)
        self.TS(ohg, lg[:, 0:4], mg, None, ALU.is_equal, None, rr, rr)
        self.TS(nmg, mg, -1.0, None, ALU.mult, None, rr, rr)
        self.ACT(eg, lg[:, 0:4], AF.Exp, rr, rr, bias=nmg)
        self.RSUM(sg_, eg, rr, rr)
        self.RECIP(sg_, sg_, rr, rr)
        self.TT(tmp32.rearrange("p (g j) -> p g j", g=4), lg[:, 4:36].rearrange("p (g j) -> p g j", g=4),
                ohg.unsqueeze(2).to_broadcast([128, 4, 8]), ALU.mult, rr, rr)
        self.RSUM(les, tmp32.rearrange("p (g j) -> p j g", g=4), rr, rr)
        self.RMAX(v1, les, rr, rr)
        self.TS(oh1, les, v1, None, ALU.is_equal, None, rr, rr)
        self.STT(le2, oh1, -1e30, les, ALU.mult, ALU.add, rr, rr)
        self.RMAX(v2, le2, rr, rr)
        self.TS(oh2, le2, v2, None, ALU.is_equal, None, rr, rr)
        self.TT(e2, v2, v1, ALU.subtract, rr, rr)
        self.ACT(e2, e2, AF.Exp, rr, rr)
        self.TS(w1, e2, 1.0, None, ALU.add, None, rr, rr)
        self.RECIP(w1, w1, rr, rr)
        self.TT(w2, e2, w1, ALU.mult, rr, rr)
        self.TT(self.Wv[:, ti, 0:1], w1, sg_, ALU.mult, rr, ["Wv"])
        self.TT(self.Wv[:, ti, 1:2], w2, sg_, ALU.mult, rr, ["Wv"])
        oh = self.OHall[:, ti, :]
        ohr = ("OH", ti)
        g3 = lambda a: a.rearrange("p (g j) -> p g j", g=4)
        self.TT(g3(oh[:, 0:32]), ohg.unsqueeze(2).to_broadcast([128, 4, 8]), oh1.unsqueeze(1).to_broadcast([128, 4, 8]),
                ALU.mult, rr, [ohr])
        self.TT(g3(oh[:, 32:64]), ohg.unsqueeze(2).to_broadcast([128, 4, 8]), oh2.unsqueeze(1).to_broadcast([128, 4, 8]),
                ALU.mult, rr, [ohr])
        self.MM([(ps[1][:, 0:64], self.triS, oh, True, True), (ps[1][:, 64:128], self.onesF, oh, True, True)],
                [ohr, "consts"], [("ps", 1)])
        t64 = rt[:, 160:224]
        self.TT(t64, ps[1][:, 0:64], self.carry[:], ALU.add, [("ps", 1), "carry"], rr)
        self.TT(t64, t64, oh, ALU.mult, rr + [ohr], rr)
        self.RSUM(self.rank[:, ti, :], t64.rearrange("p (k e) -> p k e", k=2), rr, ["rank"])
        self.TT(self.carry[:], self.carry[:], ps[1][:, 64:128], ALU.add, [("ps", 1), "carry"], ["carry"])

    def moe(self, l):
        ps = self.psf
        NB = self.NBLK
        I32 = mybir.dt.int32
        gm = self.gm
        gr = ["gm"]
        G = lambda a, b: gm[:, a:b]
        cnt1, cnt2, tot, nblk, pad, e0, e1, startp = G(0, 32), G(32, 64), G(64, 96), G(96, 128), G(128, 160), G(160, 192), G(192, 224), G(224, 256)
        basev = G(256, 320)
        self.CP(gm[:, 0:64], self.carry[:], ["carry"], gr)
        self.TT(tot, cnt1, cnt2, ALU.add, gr, gr)
        NJ = self.NJ
        cmp = self.fa[:, 0:32 * NJ].rearrange("p (e j) -> p e j", e=32)
        self.TT(cmp, tot.unsqueeze(2).to_broadcast([128, 32, NJ]), self.crow[:, 0:NJ].unsqueeze(1).to_broadcast([128, 32, NJ]),
                ALU.is_gt, gr + ["crow"], ["fa"])
        self.RSUM(nblk, cmp, ["fa"], gr)
        self.TS(pad, nblk, 128.0, None, ALU.mult, None, gr, gr)
        src, dst = pad, e0
        sh = 1
        while sh < 32:
            self.CP(dst[:, 0:sh], src[:, 0:sh], gr, gr)
            self.TT(dst[:, sh:32], src[:, sh:32], src[:, 0:32 - sh], ALU.add, gr, gr)
            src, dst = dst, (e1 if dst is e0 else e0)
            sh *= 2
        endp = src
        self.TT(startp, endp, pad, ALU.subtract, gr, gr)
        self.CP(basev[:, 0:32], startp, gr, gr)
        self.TT(basev[:, 32:64], startp, cnt1, ALU.add, gr, gr)
        CH = 22
        for b0 in range(0, NB, CH):
            nb = min(CH, NB - b0)
            cm2 = self.fa[:, 0:nb * 32].rearrange("p (b e) -> p b e", b=nb)
            self.TT(cm2, endp.unsqueeze(1).to_broadcast([128, nb, 32]),
                    self.crow[:, 32 + b0:32 + b0 + nb].unsqueeze(2).to_broadcast([128, nb, 32]), ALU.is_le, gr + ["crow"], ["fa"])
            self.RSUM(self.blke[:, b0:b0 + nb], cm2, ["fa"], ["blke"])
        self.TS(self.blke[:], self.blke[:], 31.0, 128.0, ALU.min, ALU.mult, ["blke"], ["blke"])
        self.TS(self.blke[:], self.blke[:], self.pidx[:, 0:1], float(l * 4096), ALU.add, ALU.add, ["blke", "pidx"], ["blke"])
        self.CP(self.widx[:], self.blke[:], ["blke"], ["widx"])
        if l == 0:
            self.CP(self.blke[:], self.widx[:], ["widx"], ["blke"])
            self.DMA(self.o_dbg2, self.blke[:], ["blke"], (), final=True)
            self.DMA(self.o_dbg3, gm[:], gr, (), final=True)
        for ti in range(self.ntile):
            t0 = ti * 128
            xr = ("xT", ti)
            oh = self.OHall[:, ti, :]
            t64 = self.rt[:, 160:224]
            self.TT(t64, oh, basev, ALU.mult, [("OH", ti)] + gr, ["rt"])
            df = self.rt[:, 224:226]
            self.RSUM(df, t64.rearrange("p (k e) -> p k e", k=2), ["rt"], ["rt"])
            self.TT(df, df, self.rank[:, ti, :], ALU.add, ["rt", "rank"], ["rt"])
            self.CP(self.desti[:, ti, :], df, ["rt"], [("desti", ti)])
            if ti % 2 == 0:
                X2, x2r, pa, pb_ = self.X2, ["X2"], 2, 3
            else:
                X2, x2r, pa, pb_ = self.merged, [("merged", 0), ("merged", 1)], 4, 5
            self.modulate(ti, 24, 32, X2, x2r)
            self.TR([(ps[pa][:, c * 128:(c + 1) * 128], X2[:, c, :], self.identF) for c in range(4)], x2r + ["consts"], [("ps", pa)])
            self.TR([(ps[pb_][:, (c - 4) * 128:(c - 3) * 128], X2[:, c, :], self.identF) for c in range(4, 8)], x2r + ["consts"],
                    [("ps", pb_)])
            xtok = self.xtok[ti % 2]
            xtr = ("xtok", ti % 2)
            self.CP(xtok[:, 0:512], ps[pa][:], [("ps", pa)], [xtr], eng="act")
            self.CP(xtok[:, 512:1024], ps[pb_][:], [("ps", pb_)], [xtr])
            for k in range(2):
                if self.stop_after == "route":
                    break
                off = self.desti[:, ti, k:k + 1]
                self.S.add("pool", lambda e, off=off, xtok=xtok: e.indirect_dma_start(
                    out=self.d_buf, out_offset=bass.IndirectOffsetOnAxis(ap=off, axis=0), in_=xtok[:, :], in_offset=None), [xtr, ("desti", ti), "bufz"], [("bufw", ti, k)], dma=True)
            xt = self.xT[:, :, t0:t0 + 128]
            self.TS(xt, xt, DN_ALPHA, None, ALU.mult, None, [xr] + x2r, [xr])
        if l == 0:
            dd = self.fb[:, 0:self.ntile * 2]
            self.CP(dd, self.desti[:].rearrange("p t k -> p (t k)"), [("desti", ti) for ti in range(self.ntile)], ["fb"])
            self.DMA(self.o_dbg1, dd, ["fb"], (), final=True)
        if self.stop_after in ("route", "scatter"):
            return
        bufw = [("bufw", ti, k) for ti in range(self.ntile) for k in range(2)]
        def blk_loads(b):
            xb = self.xblk[b % 2]
            xbr = ("xblk", b % 2)
            self.DMA(xb[:], self.d_buf[b * 128:(b + 1) * 128, :], bufw, [xbr], q="pool")
            wts = [wgather(b, self.d_wegp), wgather(b, self.d_weup)]
            return xb, xbr, wts

        def wgather(b, dsrc):
            off = self.widx[:, b:b + 1]
            i = self._wsi % len(self.ws)
            self._wsi += 1
            t = self.ws[i]
            self.S.add("pool", lambda e, off=off, t=t, dsrc=dsrc: e.indirect_dma_start(
                out=t[:, :], out_offset=None, in_=dsrc, in_offset=bass.IndirectOffsetOnAxis(ap=off, axis=0)),
                ["widx"], [("ws", i), ("gch", self._gch % 2)], dma=True)
            self._gch += 1
            return (t, ("ws", i))

        pend = blk_loads(0)
        pend_d = wgather(0, self.d_wedp)
        for b in range(NB):
            xb, xbr, wts = pend
            wts = wts + [pend_d]
            if b + 1 < NB:
                pend = blk_loads(b + 1)
            for h in range(2):
                self.TR([(self.psb[:, (h * 4 + c) * 128:(h * 4 + c + 1) * 128], xb[:, (h * 4 + c) * 128:(h * 4 + c + 1) * 128],
                          self.identB[:]) for c in range(4)], [xbr, "identB"], [("psb", 0), ("psb", 1)])
                if self.stop_after == "blocks1a":
                    continue
                if h == 0:
                    self.ACT(self.xbT[:, 0:4, :].rearrange("p c t -> p (c t)"), self.psb[:, 0:512], AF.Identity, [("psb", 0), ("psb", 1)], [("xbT", 0)])
                else:
                    self.CP(self.xbT[:, 4:8, :].rearrange("p c t -> p (c t)"), self.psb[:, 512:1024], [("psb", 0), ("psb", 1)], [("xbT", 1)])
            if self.stop_after == "blocks2":
                continue
            wg = wts[0][0][:, :].rearrange("p (k n) -> p k n", k=8)
            wu = wts[1][0][:, :].rearrange("p (k n) -> p k n", k=8)
            wd = wts[2][0][:, :].rearrange("p (k n) -> p k n", k=4)
            xbTr = [("xbT", 0), ("xbT", 1)]
            self.MM([(ps[0][:], self.xbT[:, k, :], wg[:, k, :], k == 0, k == 7) for k in range(8)], xbTr + [wts[0][1]], [("ps", 0)])
            self.MM([(ps[1][:], self.xbT[:, k, :], wu[:, k, :], k == 0, k == 7) for k in range(8)], xbTr + [wts[1][1]], [("ps", 1)])
            if b + 1 < NB:
                pend_d = wgather(b + 1, self.d_wedp)
            sgt = self.fb[:, 0:512]
            self.ACT(sgt, ps[0][:], AF.Silu, [("ps", 0)], ["fb"])
            self.TT(self.hidtok[:], sgt, ps[1][:], ALU.mult, ["fb", ("ps", 1)], ["hidtok"])
            self.TR([(self.psb[:, c * 128:(c + 1) * 128], self.hidtok[:, c * 128:(c + 1) * 128], self.identB[:]) for c in range(4)],
                    ["hidtok", "identB"], [("psb", 0), ("psb", 1)])
            self.ACT(self.hidT[:].rearrange("p c t -> p (c t)"), self.psb[:, 0:512], AF.Identity, [("psb", 0), ("psb", 1)], ["hidT"])
            self.MM([(ps[4][:], self.hidT[:, fc, :], wd[:, fc, 0:512], fc == 0, fc == 3) for fc in range(4)], ["hidT", wts[2][1]],
                    [("ps", 4)])
            self.MM([(ps[5][:], self.hidT[:, fc, :], wd[:, fc, 512:1024], fc == 0, fc == 3) for fc in range(4)], ["hidT", wts[2][1]],
                    [("ps", 5)])
            yb = self.yblk[b % 2]
            ybr = ("yblk", b % 2)
            self.CP(yb[:, 0:512], ps[4][:], [("ps", 4)], [ybr], eng="act")
            self.CP(yb[:, 512:1024], ps[5][:], [("ps", 5)], [ybr])
            self.DMA(self.d_ybuf[b * 128:(b + 1) * 128, :], yb[:], [ybr], [("ybuf", b)], q="pool")
        ybr_all = [("ybuf", b) for b in range(NB)]
        if self.stop_after in ("blocks", "blocks0", "blocks1", "blocks1a", "blocks2"):
            return
        for ti in range(self.ntile):
            t0 = ti * 128
            kindS = ti == self.npt
            xr = ("xT", ti)
            G1, G2 = self.fa, self.fc
            for k, (Gt, Gr) in enumerate(((G1, "fa"), (G2, "fc"))):
                off = self.desti[:, ti, k:k + 1]
                self.S.add("pool", lambda e, off=off, Gt=Gt: e.indirect_dma_start(
                    out=Gt[:, :], out_offset=None, in_=self.d_ybuf, in_offset=bass.IndirectOffsetOnAxis(ap=off, axis=0)), ybr_all + [("desti", ti)], [Gr], dma=True)
            self.TS(G1[:], G1[:], self.Wv[:, ti, 0:1], None, ALU.mult, None, ["fa", "Wv"], ["fa"])
            self.STT(G1[:], G2[:], self.Wv[:, ti, 1:2], G1[:], ALU.mult, ALU.add, ["fa", "fc", "Wv"], ["fa"])
            self.TR([(ps[2][:, c * 128:(c + 1) * 128], G1[:, c * 128:(c + 1) * 128], self.identF) for c in range(4)],
                    ["fa", "consts"], [("ps", 2)])
            self.TR([(ps[3][:, (c - 4) * 128:(c - 3) * 128], G1[:, c * 128:(c + 1) * 128], self.identF) for c in range(4, 8)],
                    ["fa", "consts"], [("ps", 3)])
            xt = self.xT[:, :, t0:t0 + 128]
            for h in range(2):
                pm, pmr = ps[2 + h], ("ps", 2 + h)
                if not kindS:
                    for mc in range(4):
                        c = h * 4 + mc
                        self.STT(xt[:, c, :], pm[:, mc * 128:(mc + 1) * 128], self.ada[:, 40 + c, 0:1], xt[:, c, :], ALU.mult, ALU.add,
                                 [pmr, "ada", xr], [xr])
                else:
                    g2b = self.ada[:, 40 + h * 4:40 + h * 4 + 4, 1:17].unsqueeze(3).to_broadcast([128, 4, 16, 8])
                    tmp = self.g4a[:].rearrange("p c (s j) -> p c s j", s=16)
                    self.TT(tmp, pm[:].rearrange("p (c s j) -> p c s j", c=4, s=16), g2b, ALU.mult, [pmr, "ada"], ["g4a"])
                    self.TT(xt[:, h * 4:h * 4 + 4, :], xt[:, h * 4:h * 4 + 4, :], self.g4a[:], ALU.add, ["g4a", xr], [xr])
            self.ln_feat(xt, 8, [xr], FV_L2G, FV_L2B, l, lambda c: xt[:, c, :], [xr])


def _feat_major(a2d):
    r, f = a2d.shape
    return np.ascontiguousarray(a2d.T.reshape(f // 128, 128, r).transpose(1, 0, 2))


def _fm_vec(v):
    f = v.shape[-1]
    return np.moveaxis(v.reshape(v.shape[:-1] + (f // 128, 128)), -1, 0)


_CACHE = {}


def kernel(x_prompt, x_sample, state_conv, state_pool, state_ssm_conv, state_ssm, c_prompt, c_sample,
           w_ada, b_ada, w_in, sgu_ln_g, sgu_ln_b, sgu_w, sgu_b, conv_w, conv_bias, conv_ln_g, conv_ln_b,
           pool_w, pool_scale, ssm_conv_w, ssm_conv_b, ssm_dt_bias, ssm_a_log, ssm_d, ssm_norm_g,
           w_br_a, w_br_b, w_br_c, w_br_d, w_o, ln1_g, ln1_b, router_g, router_e, w_e_gate, w_e_up,
           w_e_down, ln2_g, ln2_b, _stop_after=None):
    f = lambda a: np.asarray(a, dtype=np.float32)
    x_prompt, x_sample = f(x_prompt), f(x_sample)
    ncore = x_prompt.shape[0]
    D = w_in.shape[0]
    seq = x_prompt.shape[1]
    npt = seq // 128
    T = seq + 128
    nsb = x_sample.shape[0] // ncore
    assert nsb == 16 and x_sample.shape[1] == 8
    key = (D, npt, _stop_after)
    if key not in _CACHE:
        _CACHE[key] = Builder(D, npt, stop_after=_stop_after)
    bld = _CACHE[key]

    shared = {}
    shared["w_ada"] = f(w_ada)
    shared["b_adaT"] = np.ascontiguousarray(_fm_vec(f(b_ada)))
    shared["w_in"] = f(w_in)
    fv = np.zeros((128, D, NFV), np.float32)
    fv[:, :, FV_CONVB:FV_CONVB + 4] = _fm_vec(f(conv_bias))
    fv[:, :, FV_CLNG:FV_CLNG + 4] = _fm_vec(f(conv_ln_g))
    fv[:, :, FV_CLNB:FV_CLNB + 4] = _fm_vec(f(conv_ln_b))
    fv[:, :, FV_PSC:FV_PSC + 4] = _fm_vec(f(pool_scale))
    fv[:, :, FV_SCB:FV_SCB + 14] = _fm_vec(f(ssm_conv_b))
    fv[:, :, FV_L1G:FV_L1G + 8] = _fm_vec(f(ln1_g))
    fv[:, :, FV_L1B:FV_L1B + 8] = _fm_vec(f(ln1_b))
    fv[:, :, FV_L2G:FV_L2G + 8] = _fm_vec(f(ln2_g))
    fv[:, :, FV_L2B:FV_L2B + 8] = _fm_vec(f(ln2_b))
    cw = _fm_vec(f(conv_w))
    fv[:, :, FV_CW:FV_CW + 124] = cw.transpose(0, 1, 3, 2).reshape(128, D, 124)
    scw = _fm_vec(f(ssm_conv_w))
    fv[:, :, FV_SCW:FV_SCW + 56] = scw.transpose(0, 1, 3, 2).reshape(128, D, 56)
    fv[:, :, FV_NG:FV_NG + 6] = _fm_vec(f(ssm_norm_g))
    shared["fvec"] = fv
    rv = np.zeros((D, NRV), np.float32)
    rv[:, RV_SLG:RV_SLG + 512] = f(sgu_ln_g)
    rv[:, RV_SLB:RV_SLB + 512] = f(sgu_ln_b)
    rv[:, RV_DTB:RV_DTB + 12] = f(ssm_dt_bias)
    rv[:, RV_ALOG:RV_ALOG + 12] = f(ssm_a_log)
    rv[:, RV_D:RV_D + 12] = f(ssm_d)
    shared["rvec"] = rv
    sgb = np.zeros((D, 2, 512), np.float32)
    sgb[:, 0] = f(sgu_b).reshape(D, 512)
    sgb[:, 1] = np.tile(f(sgu_b)[:, :, :8], (1, 1, 16)).reshape(D, 512)
    shared["sgub"] = sgb
    sw = f(sgu_w)
    sws = np.zeros((D, 2, 128, 4, 128), np.float32)
    sws[:, 0] = sw.transpose(0, 3, 1, 2)
    blk = sw[:, :, :8, :8].transpose(0, 3, 1, 2)
    for i in range(16):
        sws[:, 1, i * 8:(i + 1) * 8, :, i * 8:(i + 1) * 8] = blk
    shared["sgu_wT"] = sws
    shared["pool_w"] = f(pool_w)
    for nm, a in (("w_br_a", w_br_a), ("w_br_b", w_br_b), ("w_br_c", w_br_c), ("w_br_d", w_br_d), ("w_o", w_o),
                  ):
        shared[nm] = f(a)
    shared["w_e_gate_p"] = np.ascontiguousarray(f(w_e_gate).reshape(D, 32, 8, 128, 512).transpose(0, 1, 3, 2, 4)).reshape(D * 4096, 4096)
    shared["w_e_up_p"] = np.ascontiguousarray(f(w_e_up).reshape(D, 32, 8, 128, 512).transpose(0, 1, 3, 2, 4)).reshape(D * 4096, 4096)
    shared["w_e_down_p"] = np.ascontiguousarray(f(w_e_down).reshape(D, 32, 4, 128, 1024).transpose(0, 1, 3, 2, 4)).reshape(D * 4096, 4096)
    crow = np.zeros((1, 32 + bld.NBLK), np.float32)
    crow[0, 0:32] = 128.0 * np.arange(32)
    crow[0, 32:] = 128.0 * np.arange(bld.NBLK)
    shared["crow"] = crow
    shared["pidx"] = np.arange(128, dtype=np.float32).reshape(128, 1)
    shared["router"] = np.ascontiguousarray(np.concatenate([f(router_g), f(router_e)], axis=2))
    idx = np.arange(128)
    same = (idx[:, None] // 8) == (idx[None, :] // 8)
    caus = idx[:, None] <= idx[None, :]
    cst = np.zeros((128, 8, 128), np.float32)
    cst[:, 0] = np.eye(128)
    cst[:, 1] = caus
    cst[:, 2] = caus & same
    cst[:, 3] = np.where(caus, 0.0, NEG)
    cst[:, 4] = np.where(caus & same, 0.0, NEG)
    cst[:, 5] = same
    cst[:, 6] = 1.0
    cst[:, 7] = idx[:, None] < idx[None, :]
    shared["consts"] = cst
    shared["m3"] = ((idx[:, None] // 8) == np.arange(16)[None, :]).astype(np.float32)
    corr = np.zeros((128, 4, 15), np.float32)
    for gi in range(4):
        w = 2 ** (gi + 1)
        corr[:, gi, :] = w / np.minimum(w, np.arange(15) + 1.0)
    shared["corr"] = corr

    in_maps = []
    for c in range(ncore):
        m = dict(shared)
        xs = x_sample[c * 16:(c + 1) * 16].reshape(128, 1024)
        m["xT_in"] = _feat_major(np.concatenate([x_prompt[c], xs], axis=0))
        m["cT_in"] = _feat_major(np.concatenate([f(c_prompt)[c:c + 1], f(c_sample)[c * 16:(c + 1) * 16]], axis=0))
        sl = slice(c * 16, (c + 1) * 16)
        def st(a):
            a = f(a)[:, sl]
            d, s, r, cc = a.shape
            return np.ascontiguousarray(a.reshape(d, s, r, cc // 128, 128).transpose(0, 4, 3, 1, 2))
        m["conv_sT"] = st(state_conv)
        m["pool_sT"] = st(state_pool)
        m["sconv_sT"] = st(state_ssm_conv)
        ss = f(state_ssm)[:, sl]
        m["ssm_sT"] = np.ascontiguousarray(ss.transpose(0, 4, 1, 2, 3).reshape(D, 128, 16, 768))
        in_maps.append(m)

    res = run_bass_kernel_spmd(bld.nc, in_maps, core_ids=list(range(ncore)))
    R = res.results
    kernel._last = R

    def unfm(a):
        a = np.asarray(a)
        nch = a.shape[1]
        return np.moveaxis(a, (0, 1), (-1, -2)).reshape(a.shape[2:] + (nch * 128,))

    y_prompt = np.stack([unfm(R[c]["yT"])[:seq] for c in range(ncore)])
    y_sample = np.concatenate([unfm(R[c]["yT"])[seq:].reshape(16, 8, 1024) for c in range(ncore)])
    def pst(name):
        return np.stack([np.stack([unfm(R[c][name][l]) for l in range(D)]) for c in range(ncore)], axis=1)
    p_conv = pst("o_conv_p")
    p_pool = pst("o_pool_p")
    p_sconv = pst("o_sconv_p")
    p_ssm = np.stack([np.asarray(R[c]["o_ssm_p"]).reshape(D, 128, 12, 64).transpose(0, 2, 3, 1) for c in range(ncore)], axis=1)
    def sst(name):
        return np.concatenate([np.stack([unfm(R[c][name][l]) for l in range(D)]) for c in range(ncore)], axis=1)
    s_conv = sst("o_conv_s")
    s_pool = sst("o_pool_s")
    s_sconv = sst("o_sconv_s")
    s_ssm = np.concatenate([np.asarray(R[c]["o_ssm_s"]).reshape(D, 128, 16, 12, 64).transpose(0, 2, 3, 4, 1)
                            for c in range(ncore)], axis=1)
    s_v = np.concatenate([np.asarray(R[c]["o_v_s"]).reshape(D, 16, 8, 512) for c in range(ncore)], axis=1)
    outs = (y_prompt, y_sample, p_conv, p_pool, p_sconv, p_ssm, s_conv, s_pool, s_sconv, s_ssm, s_v)
    return tuple(np.ascontiguousarray(o, dtype=np.float32) for o in outs)
```
